# Optimizing a Trainium2 kernel written in Bass

```python
import jax, jax.numpy as jnp
from jax import lax
import numpy as np

D_MODEL = 1024
BATCH = 4
SEQ = 8192
DEPTH = 1

GRID_W = 64
CTX_LEN = 256
CONF_WIDTH = D_MODEL
CONF_KERNEL = 31
LRU_WIDTH = 1280
LRU_BLOCKS = 10
LRU_BLOCK_W = LRU_WIDTH // LRU_BLOCKS
LRU_CONV = 4
LRU_PAD = (2, 1)
LRU_C = 8.0
FFN_HIDDEN = 2816
N_MOD = 9
EPS = 1e-6
COL_LRU_X = 2 * CONF_WIDTH
COL_LRU_G = COL_LRU_X + LRU_WIDTH
COL_GATE = COL_LRU_G + LRU_WIDTH
D_IN = COL_GATE + 2 * D_MODEL

kernel_name = "hybrid_conformer_rglru_prefix_block"


def rmsnorm(x, g):
    x32 = x.astype(jnp.float32)
    y = x32 * lax.rsqrt(jnp.mean(x32 * x32, axis=-1, keepdims=True) + EPS)
    return y.astype(x.dtype) * g


def layernorm(x, g, b):
    x32 = x.astype(jnp.float32)
    mu = jnp.mean(x32, axis=-1, keepdims=True)
    xc = x32 - mu
    y = xc * lax.rsqrt(jnp.mean(xc * xc, axis=-1, keepdims=True) + EPS)
    return y.astype(x.dtype) * g + b


def modulate(x, shift, scale):
    return x * (1 + scale) + shift


def depthwise_conv(x, w, b, pad):
    y = lax.conv_general_dilated(x, w[:, None, :], window_strides=(1,), padding=[pad],
                                 dimension_numbers=("NWC", "WIO", "NWC"),
                                 feature_group_count=x.shape[-1])
    return y + b


def swiglu(x, w_up, w_down):
    g, u = jnp.split(x @ w_up, 2, axis=-1)
    return (jax.nn.silu(g) * u) @ w_down


def grid_pos_embedding(seq_len, dim):
    rows = seq_len // GRID_W
    t = jnp.arange(rows * GRID_W)
    row = (t // GRID_W).astype(jnp.float32)
    col = (t % GRID_W).astype(jnp.float32)
    q = dim // 4
    omega = 1.0 / (10000.0 ** (jnp.arange(q, dtype=jnp.float32) / q))
    er = row[:, None] * omega
    ec = col[:, None] * omega
    return jnp.concatenate([jnp.sin(er), jnp.cos(er), jnp.sin(ec), jnp.cos(ec)], axis=-1)


def _combine(left, right):
    a1, b1 = left
    a2, b2 = right
    return a1 * a2, a2 * b1 + b2


def linear_scan(a, b, h0, reverse):
    first = -1 if reverse else 0
    b = b.at[:, first].add(a[:, first] * h0)
    _, h = lax.associative_scan(_combine, (a, b), axis=1, reverse=reverse)
    return h


def rglru_direction(xr, w_a, b_a, w_x, b_x, lam, h0, reverse):
    B, T, _ = xr.shape
    xb = xr.reshape(B, T, LRU_BLOCKS, LRU_BLOCK_W)
    r = jax.nn.sigmoid(jnp.einsum("bthi,hij->bthj", xb, w_a).reshape(B, T, LRU_WIDTH) + b_a)
    ig = jax.nn.sigmoid(jnp.einsum("bthi,hij->bthj", xb, w_x).reshape(B, T, LRU_WIDTH) + b_x)
    log_a = -LRU_C * r.astype(jnp.float32) * jax.nn.softplus(-lam.astype(jnp.float32))
    a = jnp.exp(log_a)
    bb = jnp.sqrt(-jnp.expm1(2.0 * log_a)) * (ig * xr).astype(jnp.float32)
    return linear_scan(a, bb, h0, reverse)


def rglru_scans(u_x, lp, h0f, h0b):
    xr = depthwise_conv(u_x, lp["w_lru_conv"], lp["b_lru_conv"], LRU_PAD)
    hf = rglru_direction(xr, lp["w_rec_gate"][0], lp["b_rec_gate"][0], lp["w_in_gate"][0],
                         lp["b_in_gate"][0], lp["lru_lambda"][0], h0f, False)
    hb = rglru_direction(xr, lp["w_rec_gate"][1], lp["b_rec_gate"][1], lp["w_in_gate"][1],
                         lp["b_in_gate"][1], lp["lru_lambda"][1], h0b, True)
    return xr, hf, hb


def conformer_branch(u_glu, lp):
    v, gt = jnp.split(u_glu, 2, axis=-1)
    u = v * jax.nn.sigmoid(gt)
    u = depthwise_conv(u, lp["w_dw"], lp["b_dw"], (CONF_KERNEL // 2, CONF_KERNEL // 2))
    u = jax.nn.silu(layernorm(u, lp["g_ln"], lp["b_ln"]))
    return u @ lp["w_conf_out"]


def mixer(h, lp, h0f, h0b):
    proj = h @ lp["w_in"] + lp["b_in"]
    y_conf = conformer_branch(proj[..., :COL_LRU_X], lp)
    xr, hf, hb = rglru_scans(proj[..., COL_LRU_X:COL_LRU_G], lp, h0f, h0b)
    y_lru = ((hf + hb).astype(xr.dtype) * jax.nn.gelu(proj[..., COL_LRU_G:COL_GATE])) @ lp["w_lru_out"]
    g_conf, g_lru = jnp.split(jax.nn.sigmoid(proj[..., COL_GATE:]), 2, axis=-1)
    y = (g_conf * y_conf + g_lru * y_lru) @ lp["w_out"]
    return y, hf[:, -1], hb[:, 0]


def context_lru_states(hc, lp):
    u_x = hc @ lp["w_in"][:, COL_LRU_X:COL_LRU_G] + lp["b_in"][COL_LRU_X:COL_LRU_G]
    h0 = jnp.zeros((hc.shape[0], LRU_WIDTH), jnp.float32)
    _, hf, hb = rglru_scans(u_x, lp, h0, h0)
    return hf[:, -1], hb[:, 0]


def layer(x, xc, c, c_ctx, lp, update_context):
    m = jnp.split((jax.nn.silu(c) @ lp["w_mod"] + lp["b_mod"])[:, None, :], N_MOD, axis=-1)
    mc = jnp.split((jax.nn.silu(c_ctx) @ lp["w_mod"] + lp["b_mod"])[None, None, :], N_MOD, axis=-1)
    x = x + 0.5 * m[2] * swiglu(modulate(rmsnorm(x, lp["g_n1"]), m[0], m[1]), lp["w_ffn1_up"], lp["w_ffn1_down"])
    xc = xc + 0.5 * mc[2] * swiglu(modulate(rmsnorm(xc, lp["g_n1"]), mc[0], mc[1]), lp["w_ffn1_up"], lp["w_ffn1_down"])
    hc = modulate(rmsnorm(xc, lp["g_n2"]), mc[3], mc[4])
    if update_context:
        h0 = jnp.zeros((hc.shape[0], LRU_WIDTH), jnp.float32)
        yc, hf_c, hb_c = mixer(hc, lp, h0, h0)
        xc = xc + mc[5] * yc
    else:
        hf_c, hb_c = context_lru_states(hc, lp)
    h = modulate(rmsnorm(x, lp["g_n2"]), m[3], m[4])
    y, _, _ = mixer(h, lp, hf_c, hb_c)
    x = x + m[5] * y
    x = x + 0.5 * m[8] * swiglu(modulate(rmsnorm(x, lp["g_n3"]), m[6], m[7]), lp["w_ffn2_up"], lp["w_ffn2_down"])
    if update_context:
        xc = xc + 0.5 * mc[8] * swiglu(modulate(rmsnorm(xc, lp["g_n3"]), mc[6], mc[7]), lp["w_ffn2_up"], lp["w_ffn2_down"])
    return x, xc


def setup_inputs(seed: int = 0) -> dict:
    key = jax.random.key(seed)
    ks = jax.random.split(key, 32)
    L, D, F = DEPTH, D_MODEL, FFN_HIDDEN

    def nrm(k, shape, scale=1.0):
        return scale * jax.random.normal(k, shape, jnp.float32)

    u = jax.random.uniform(ks[23], (L, 2, LRU_WIDTH), jnp.float32, minval=0.9, maxval=0.999)
    a = u ** (1.0 / LRU_C)
    lam = jnp.log(a) - jnp.log1p(-a)
    return {
        "x": nrm(ks[0], (BATCH, SEQ, D)),
        "c": nrm(ks[1], (BATCH, D)),
        "ctx": nrm(ks[2], (BATCH, CTX_LEN, D)),
        "c_ctx": nrm(ks[3], (D,)),
        "w_mod": nrm(ks[4], (L, D, N_MOD * D), 0.5 * D ** -0.5),
        "b_mod": nrm(ks[5], (L, N_MOD * D), 0.02),
        "g_n1": 1.0 + nrm(ks[6], (L, D), 0.1),
        "w_ffn1_up": nrm(ks[7], (L, D, 2 * F), D ** -0.5),
        "w_ffn1_down": nrm(ks[8], (L, F, D), F ** -0.5),
        "g_n2": 1.0 + nrm(ks[9], (L, D), 0.1),
        "w_in": nrm(ks[10], (L, D, D_IN), D ** -0.5),
        "b_in": nrm(ks[11], (L, D_IN), 0.02),
        "w_dw": nrm(ks[12], (L, CONF_KERNEL, CONF_WIDTH), CONF_KERNEL ** -0.5),
        "b_dw": nrm(ks[13], (L, CONF_WIDTH), 0.02),
        "g_ln": 1.0 + nrm(ks[14], (L, CONF_WIDTH), 0.1),
        "b_ln": nrm(ks[15], (L, CONF_WIDTH), 0.02),
        "w_conf_out": nrm(ks[16], (L, CONF_WIDTH, D), CONF_WIDTH ** -0.5),
        "w_lru_conv": nrm(ks[17], (L, LRU_CONV, LRU_WIDTH), LRU_CONV ** -0.5),
        "b_lru_conv": nrm(ks[18], (L, LRU_WIDTH), 0.02),
        "w_rec_gate": nrm(ks[19], (L, 2, LRU_BLOCKS, LRU_BLOCK_W, LRU_BLOCK_W), LRU_BLOCK_W ** -0.5),
        "b_rec_gate": nrm(ks[20], (L, 2, LRU_WIDTH), 0.02),
        "w_in_gate": nrm(ks[21], (L, 2, LRU_BLOCKS, LRU_BLOCK_W, LRU_BLOCK_W), LRU_BLOCK_W ** -0.5),
        "b_in_gate": nrm(ks[22], (L, 2, LRU_WIDTH), 0.02),
        "lru_lambda": lam,
        "w_lru_out": nrm(ks[24], (L, LRU_WIDTH, D), LRU_WIDTH ** -0.5),
        "w_out": nrm(ks[25], (L, D, D), D ** -0.5),
        "g_n3": 1.0 + nrm(ks[26], (L, D), 0.1),
        "w_ffn2_up": nrm(ks[27], (L, D, 2 * F), D ** -0.5),
        "w_ffn2_down": nrm(ks[28], (L, F, D), F ** -0.5),
        "g_final": 1.0 + nrm(ks[29], (D,), 0.1),
    }


def reference(x, c, ctx, c_ctx, w_mod, b_mod, g_n1, w_ffn1_up, w_ffn1_down, g_n2, w_in, b_in,
              w_dw, b_dw, g_ln, b_ln, w_conf_out, w_lru_conv, b_lru_conv, w_rec_gate, b_rec_gate,
              w_in_gate, b_in_gate, lru_lambda, w_lru_out, w_out, g_n3, w_ffn2_up, w_ffn2_down,
              g_final):
    x = x + grid_pos_embedding(x.shape[1], x.shape[2]).astype(x.dtype)[None]
    xc = ctx
    for i in range(DEPTH):
        lp = dict(w_mod=w_mod[i], b_mod=b_mod[i], g_n1=g_n1[i], w_ffn1_up=w_ffn1_up[i],
                  w_ffn1_down=w_ffn1_down[i], g_n2=g_n2[i], w_in=w_in[i], b_in=b_in[i],
                  w_dw=w_dw[i], b_dw=b_dw[i], g_ln=g_ln[i], b_ln=b_ln[i], w_conf_out=w_conf_out[i],
                  w_lru_conv=w_lru_conv[i], b_lru_conv=b_lru_conv[i], w_rec_gate=w_rec_gate[i],
                  b_rec_gate=b_rec_gate[i], w_in_gate=w_in_gate[i], b_in_gate=b_in_gate[i],
                  lru_lambda=lru_lambda[i], w_lru_out=w_lru_out[i], w_out=w_out[i], g_n3=g_n3[i],
                  w_ffn2_up=w_ffn2_up[i], w_ffn2_down=w_ffn2_down[i])
        x, xc = layer(x, xc, c, c_ctx, lp, i < DEPTH - 1)
    y = rmsnorm(x, g_final)
    return y
```

```python
import numpy as np
import concourse.bass as bass
import concourse.mybir as mybir
from concourse.bass_utils import run_bass_kernel_spmd

F32 = mybir.dt.float32
BF16 = mybir.dt.bfloat16
AF = mybir.ActivationFunctionType
ALU = mybir.AluOpType

D = 1024
KC = 8
SEQ = 8192
LT = 4096
HALO = 16
CTX = 256
FF = 2816
LRU = 1280
LC = 10
NT = 512
EPS = 1e-6
NCORES = 8

_VEC_SPECS = [("g_n1", 8), ("g_n2", 8), ("g_n3", 8), ("g_final", 8), ("b_mod", 72), ("b_in", 52),
              ("w_dw", 31 * 8), ("b_dw", 8), ("g_ln", 8), ("b_ln", 8), ("w_lc", 5 * 10), ("b_lc", 10),
              ("b_a", 20), ("b_x", 20), ("lam", 20), ("c2", 16), ("flags", 2), ("pmask", 8)]
VOFF = {}
_o = 0
for _n, _k in _VEC_SPECS:
    VOFF[_n] = (_o, _k)
    _o += _k
NV = _o

BLK = {}
_b = 0
for _n, _k in [("UP1", 11), ("DN1", 8), ("IN", 14), ("CO", 2), ("LO", 4), ("WO", 2), ("UP2", 11), ("DN2", 8)]:
    BLK[_n] = _b
    _b += _k
NBLK = _b
WBE = 4096


class Buf:
    __slots__ = ("name", "w", "r", "prev", "dsem", "dcount", "excl")

    def __init__(self, name, excl=False):
        self.name = name
        self.excl = excl
        self.w = {}
        self.r = {}
        self.prev = {}
        self.dsem = None
        self.dcount = 0


class Tk:
    def __init__(self, nc):
        self.nc = nc
        self.eng = {"pe": nc.tensor, "act": nc.scalar, "dve": nc.vector, "pool": nc.gpsimd, "sp": nc.sync}
        self.sem = {k: nc.alloc_semaphore("sem_" + k) for k in self.eng}
        self.cnt = {k: 0 for k in self.eng}
        self.seen = {k: {} for k in self.eng}
        self.semobj = {}
        for k, s in self.sem.items():
            self.semobj[id(s)] = s
        self.dsems = []
        self.nwaits = 0

    def buf(self, name, excl=False):
        return Buf(name, excl)

    def _wait(self, e, deps):
        eng = self.eng[e]
        seen = self.seen[e]
        for sid, val in deps.items():
            if e == "pe" and sid == id(self.sem["pe"]):
                continue
            if seen.get(sid, 0) >= val:
                continue
            eng.wait_ge(self.semobj[sid], val)
            seen[sid] = val
            self.nwaits += 1

    @staticmethod
    def _merge(dst, src):
        for k, v in src.items():
            if dst.get(k, 0) < v:
                dst[k] = v

    def _collect(self, reads, writes, parts):
        deps = {}
        for b in reads:
            self._merge(deps, b.w)
            if b.excl:
                self._merge(deps, b.r)
        for b in writes:
            self._merge(deps, b.w)
            self._merge(deps, b.r)
        for b in parts:
            if b.r:
                self._merge(deps, b.w)
                self._merge(deps, b.r)
            else:
                self._merge(deps, b.prev)
        return deps

    def _record(self, tok, reads, writes, parts):
        sid, val = tok
        for b in reads:
            if b.r.get(sid, 0) < val:
                b.r[sid] = val
        for b in writes:
            prev = dict(b.w)
            self._merge(prev, b.r)
            b.prev = prev
            b.w = {sid: val}
            b.r = {}
        for b in parts:
            if b.r:
                prev = dict(b.w)
                self._merge(prev, b.r)
                b.prev = prev
                b.w = {sid: val}
                b.r = {}
            else:
                if b.w.get(sid, 0) < val:
                    b.w[sid] = val

    def op(self, e, fn, reads=(), writes=(), parts=()):
        deps = self._collect(reads, writes, parts)
        self._wait(e, deps)
        ins = fn(self.eng[e])
        ins.then_inc(self.sem[e], 1)
        self.cnt[e] += 1
        assert self.cnt[e] < 60000
        self._record((id(self.sem[e]), self.cnt[e]), reads, writes, parts)
        return ins

    def dma(self, e, out, in_, sb, reads=(), writes=(), parts=(), **kw):
        deps = self._collect(reads, writes, parts)
        self._wait(e, deps)
        if sb.dsem is None:
            sb.dsem = self.nc.alloc_semaphore("d_" + sb.name)
            self.semobj[id(sb.dsem)] = sb.dsem
            self.dsems.append(sb)
        ins = self.eng[e].dma_start(out=out, in_=in_, **kw)
        ins.then_inc(sb.dsem, 16)
        sb.dcount += 16
        assert sb.dcount < 60000
        self._record((id(sb.dsem), sb.dcount), reads, writes, parts)
        return ins

    def coll(self, fn, reads=(), writes=()):
        deps = self._collect(reads, writes, ())
        self._wait("pool", deps)
        if not hasattr(self, "ccsem"):
            self.ccsem = self.nc.alloc_semaphore("ccsem")
            self.semobj[id(self.ccsem)] = self.ccsem
            self.cccount = 0
        ins = fn(self.eng["pool"])
        ins.then_inc(self.ccsem)
        self.cccount += 1
        self._record((id(self.ccsem), self.cccount), reads, writes, ())
        return ins

    def wait_tokens(self, e, deps):
        self._wait(e, deps)

    def all_tokens(self):
        deps = {}
        for k in self.eng:
            if self.cnt[k] > 0:
                deps[id(self.sem[k])] = self.cnt[k]
        for sb in self.dsems:
            deps[id(sb.dsem)] = sb.dcount
        if hasattr(self, "ccsem") and self.cccount > 0:
            deps[id(self.ccsem)] = self.cccount
        return deps

    def barrier(self):
        deps = self.all_tokens()
        for k in self.eng:
            d = dict(deps)
            self._wait(k, d)


class Ring:
    def __init__(self, items):
        self.items = items
        self.i = 0

    def next(self):
        it = self.items[self.i % len(self.items)]
        self.i += 1
        return it


def build(debug=False, stop_after=None, tilesA_override=None):
    nc = bass.Bass("TRN2", target_bir_lowering=False)
    tk = Tk(nc)

    def din(name, shape, dt=F32):
        return nc.dram_tensor(name, list(shape), dt, kind="ExternalInput").ap()

    def dscr(name, shape, dt):
        return nc.dram_tensor(name, list(shape), dt).ap()

    x_d = din("x_loc", [LT + HALO, D])
    ctx_d = din("ctx_loc", [CTX, D])
    vecs_d = din("vecs", [128, NV])
    wmod_d = din("w_mod", [D, 9 * D])
    wup_d = [din("w_ffn1_up", [D, 2 * FF]), din("w_ffn2_up", [D, 2 * FF])]
    wdn_d = [din("w_ffn1_down", [FF, D]), din("w_ffn2_down", [FF, D])]
    win_d = din("w_in", [D, 6656])
    wco_d = din("w_conf_out", [D, D])
    wlo_d = din("w_lru_out", [LRU, D])
    wo_d = din("w_out", [D, D])
    wg_d = din("w_gates", [LC, 128, 4, 128])
    out_d = nc.dram_tensor("y_loc", [LT, D], F32, kind="ExternalOutput").ap()

    WB_d = dscr("WB", [NBLK, 128, WBE], BF16)
    X1_d = dscr("X1", [KC, 128, LT], F32)
    WU = 15 + LT + HALO
    U_d = dscr("U", [KC, 128, WU], BF16)
    WX = 2 + LT + HALO
    UX_d = dscr("UX", [LC, 128, WX], BF16)
    CUX_d = dscr("CUX", [LC, 128, CTX + 4], BF16)
    G_d = dscr("G", [LC, 128, LT], BF16)
    GCL_d = dscr("GCL", [16, 128, LT], BF16)
    M_d = dscr("M", [LC, 128, LT], BF16)
    C_d = dscr("C", [KC, 128, LT], F32)
    HF_d = dscr("HF", [LC, 128, LT], F32)
    cci_d = dscr("cc_in", [128, LC], F32)
    cco_d = dscr("cc_out", [NCORES * 128, LC], F32)

    wb_bufs = [tk.buf("wb%d" % i) for i in range(NBLK)]
    X1_b = tk.buf("X1"); U_b = tk.buf("U"); UX_b = tk.buf("UX"); CUX_b = tk.buf("CUX")
    G_b = tk.buf("G"); GCL_b = tk.buf("GCL"); M_b = tk.buf("M"); C_b = tk.buf("C"); HF_b = tk.buf("HF")
    cci_b = tk.buf("cci"); cco_b = tk.buf("cco"); out_b = tk.buf("out")

    import contextlib
    phase = {"st": None}

    def sb(name, shape, dt):
        if phase["st"] is None:
            return nc.alloc_sbuf_tensor("s_" + name, list(shape), dt).ap()
        return phase["st"].enter_context(nc.sbuf_tensor("s_" + name, list(shape), dt)).ap()

    def begin_phase():
        phase["st"] = contextlib.ExitStack()

    def end_phase():
        tk.barrier()
        phase["st"].close()
        phase["st"] = None

    vecs = sb("vecs", [128, NV], F32); vecs_b = tk.buf("vecs")
    ident = sb("ident", [128, 128], F32); ident_b = tk.buf("ident")
    ones_bf = sb("ones_bf", [128, 128], BF16); ones_b = tk.buf("ones")
    consts = sb("consts", [128, 8], F32); consts_b = tk.buf("consts")
    modp = sb("modp", [128, 9, KC, 2], F32); modp_b = tk.buf("modp")
    TR = sb("TR", [128, 4, 65], F32); TC = sb("TC", [128, 4, 64], F32); pos_b = tk.buf("pos")
    clam = sb("clam", [128, 2, 20], F32); clam_b = tk.buf("clam")
    s0 = sb("s0", [128, LC], F32); s0_b = tk.buf("s0")
    sfin = sb("sfin", [128, LC], F32); sfin_b = tk.buf("sfin")
    carry = sb("carry", [128, LC], F32); carry_b = tk.buf("carry")

    psum = [nc.alloc_psum_tensor("ps%d" % i, [128, 512], F32).ap() for i in range(8)]
    ps_b = [tk.buf("ps%d" % i, excl=True) for i in range(8)]
    psring = Ring(list(range(8)))

    def vcol(name, idx=0, n=1):
        o, k = VOFF[name]
        return vecs[:, o + idx:o + idx + n]

    def finish_early(items):
        tk.barrier()
        for nm, ap_ in items:
            o = nc.dram_tensor("dbg_" + nm, list(ap_.shape), ap_.dtype, kind="ExternalOutput").ap()
            b_ = tk.buf("dbg_" + nm)
            tk.dma("sp", o, ap_, b_, writes=[b_])
        z = nc.alloc_sbuf_tensor("s_zout", [128, D], F32).ap(); z_b = tk.buf("zout")
        tk.op("dve", lambda e: e.memset(z, 0.0), writes=[z_b])
        for i in range(LT // 128):
            tk.dma("sp", out_d[i * 128:(i + 1) * 128, :], z, z_b, reads=[z_b], parts=[out_b])
        tk.barrier()
        return nc

    tk.dma("sp", vecs, vecs_d, vecs_b, writes=[vecs_b])
    wring_t = [None] * 5
    wring_b = [tk.buf("wr%d" % i) for i in range(5)]

    def alloc_ring(tag):
        for i in range(5):
            wring_t[i] = sb("wr%s%d" % (tag, i), [128, WBE], BF16)

    small = sb("small", [128, 4, 512], F32)
    rs_b = tk.buf("rs_sb")
    tmpg = [sb("tmpg%d" % i, [128, 2, 512], F32) for i in range(2)]
    tmpg_b = [tk.buf("tmpg%d" % i) for i in range(2)]
    tmpring = Ring([0, 1])
    zt = sb("zt", [128, LC, 16], BF16); zt_b = tk.buf("zt")

    begin_phase()
    it = sb("iota_t", [128, 128], F32)
    it_b = tk.buf("iota_t")
    tk.op("pool", lambda e: e.iota(it, pattern=[[1, 128]], base=0, channel_multiplier=-1,
                                   allow_small_or_imprecise_dtypes=True), writes=[it_b])
    tk.op("dve", lambda e: e.tensor_single_scalar(out=ident, in_=it, scalar=0.0, op=ALU.is_equal),
          reads=[it_b], writes=[ident_b])
    tk.op("dve", lambda e: e.memset(ones_bf, 1.0), writes=[ones_b])
    tk.op("dve", lambda e: e.memset(consts[:, 0:1], EPS), parts=[consts_b])
    tk.op("dve", lambda e: e.memset(consts[:, 1:2], 1.0), parts=[consts_b])
    tk.op("dve", lambda e: e.memset(consts[:, 2:3], float(np.pi / 2)), parts=[consts_b])
    tk.op("dve", lambda e: e.memset(consts[:, 3:4], 0.0), parts=[consts_b])
    tk.op("dve", lambda e: e.memset(zt, 0.0), writes=[zt_b])

    def build_pos():
        om = sb("om", [128, 2], F32); om_b = tk.buf("om")
        pi_ = sb("pi_", [128, 2], F32); pi_b = tk.buf("pi_")
        Gs = sb("Gs", [128, 2, 128], F32); Gc = sb("Gc", [128, 2, 128], F32); G_bb = tk.buf("Gtab")
        sc = sb("sc_pos", [128, 2, 2], F32); sc_b = tk.buf("sc_pos")
        t1 = sb("t1_pos", [128, 128], F32); t1_b = tk.buf("t1_pos")
        sm = sb("sm_pos", [128, 16], F32); sm_b = tk.buf("sm_pos")
        tk.op("pool", lambda e: e.iota(pi_, pattern=[[128, 2]], base=0, channel_multiplier=1,
                                       allow_small_or_imprecise_dtypes=True), writes=[pi_b])
        tk.op("act", lambda e: e.activation(out=om, in_=pi_, func=AF.Exp, scale=float(-np.log(10000.0) / 256.0)),
              reads=[pi_b], writes=[om_b])
        tk.op("dve", lambda e: e.memset(Gs[:, :, 0:1], 0.0), parts=[G_bb])
        tk.op("dve", lambda e: e.memset(Gc[:, :, 0:1], 1.0), parts=[G_bb])
        tk.op("act", lambda e: e.activation(out=sc[:, :, 0], in_=om, func=AF.Sin, scale=1.0),
              reads=[om_b], parts=[sc_b])
        tk.op("act", lambda e: e.activation(out=sc[:, :, 1], in_=om, func=AF.Sin, scale=1.0, bias=consts[:, 2:3]),
              reads=[om_b, consts_b], parts=[sc_b])
        for b in range(7):
            w = 1 << b
            for j in range(2):
                sbv = sc[:, j, 0:1]; cbv = sc[:, j, 1:2]
                tk.op("dve", lambda e: e.tensor_scalar(out=t1[:, 0:w], in0=Gc[:, j, 0:w], scalar1=sbv, scalar2=None, op0=ALU.mult),
                      reads=[G_bb, sc_b], writes=[t1_b])
                tk.op("dve", lambda e: e.scalar_tensor_tensor(out=Gs[:, j, w:2 * w], in0=Gs[:, j, 0:w], scalar=cbv, in1=t1[:, 0:w],
                                                               op0=ALU.mult, op1=ALU.add),
                      reads=[G_bb, sc_b, t1_b], writes=[G_bb])
                tk.op("dve", lambda e: e.tensor_scalar(out=t1[:, 0:w], in0=Gs[:, j, 0:w], scalar1=sbv, scalar2=None, op0=ALU.mult),
                      reads=[G_bb, sc_b], writes=[t1_b])
                tk.op("dve", lambda e: e.scalar_tensor_tensor(out=Gc[:, j, w:2 * w], in0=Gc[:, j, 0:w], scalar=cbv, in1=t1[:, 0:w],
                                                               op0=ALU.mult, op1=ALU.subtract),
                      reads=[G_bb, sc_b, t1_b], writes=[G_bb])
            if b < 6:
                tk.op("dve", lambda e: e.tensor_tensor(out=sm[:, 0:2], in0=sc[:, :, 0], in1=sc[:, :, 0], op=ALU.mult),
                      reads=[sc_b], writes=[sm_b])
                tk.op("dve", lambda e: e.tensor_tensor(out=sm[:, 2:4], in0=sc[:, :, 0], in1=sc[:, :, 1], op=ALU.mult),
                      reads=[sc_b, sm_b], writes=[sm_b])
                tk.op("dve", lambda e: e.tensor_scalar(out=sc[:, :, 1], in0=sm[:, 0:2], scalar1=-2.0, scalar2=1.0,
                                                        op0=ALU.mult, op1=ALU.add), reads=[sm_b], writes=[sc_b])
                tk.op("dve", lambda e: e.tensor_scalar(out=sc[:, :, 0], in0=sm[:, 2:4], scalar1=2.0, scalar2=None,
                                                        op0=ALU.mult), reads=[sm_b, sc_b], writes=[sc_b])
        fo = VOFF["flags"][0]
        mfl = vecs[:, fo:fo + 1]; sfl = vecs[:, fo + 1:fo + 2]
        for (tab, r0max, n) in ((TR, 127, 65), (TC, 63, 64)):
            for j in range(2):
                S0 = sm[:, 4:5]; C0 = sm[:, 5:6]; sS0 = sm[:, 6:7]; sC0 = sm[:, 7:8]; om1 = sm[:, 8:9]
                tk.op("dve", lambda e: e.tensor_tensor(out=S0, in0=Gs[:, j, r0max:r0max + 1], in1=mfl, op=ALU.mult),
                      reads=[G_bb, vecs_b, sm_b], writes=[sm_b])
                tk.op("dve", lambda e: e.tensor_scalar(out=om1, in0=mfl, scalar1=-1.0, scalar2=1.0, op0=ALU.mult, op1=ALU.add),
                      reads=[vecs_b, sm_b], writes=[sm_b])
                tk.op("dve", lambda e: e.scalar_tensor_tensor(out=C0, in0=Gc[:, j, r0max:r0max + 1], scalar=mfl, in1=om1,
                                                               op0=ALU.mult, op1=ALU.add), reads=[G_bb, vecs_b, sm_b], writes=[sm_b])
                tk.op("dve", lambda e: e.tensor_tensor(out=sS0, in0=S0, in1=sfl, op=ALU.mult), reads=[sm_b, vecs_b], writes=[sm_b])
                tk.op("dve", lambda e: e.tensor_tensor(out=sC0, in0=C0, in1=sfl, op=ALU.mult), reads=[sm_b, vecs_b], writes=[sm_b])
                tk.op("dve", lambda e: e.tensor_scalar(out=t1[:, 0:n], in0=Gs[:, j, 0:n], scalar1=sC0, scalar2=None, op0=ALU.mult),
                      reads=[G_bb, sm_b], writes=[t1_b])
                tk.op("dve", lambda e: e.scalar_tensor_tensor(out=tab[:, j, 0:n], in0=Gc[:, j, 0:n], scalar=S0, in1=t1[:, 0:n],
                                                               op0=ALU.mult, op1=ALU.add), reads=[G_bb, sm_b, t1_b], writes=[pos_b])
                tk.op("dve", lambda e: e.tensor_scalar(out=t1[:, 0:n], in0=Gs[:, j, 0:n], scalar1=sS0, scalar2=None, op0=ALU.mult),
                      reads=[G_bb, sm_b], writes=[t1_b])
                tk.op("dve", lambda e: e.scalar_tensor_tensor(out=tab[:, 2 + j, 0:n], in0=Gc[:, j, 0:n], scalar=C0, in1=t1[:, 0:n],
                                                               op0=ALU.mult, op1=ALU.subtract), reads=[G_bb, sm_b, t1_b], writes=[pos_b])

    build_pos()

    def build_clam():
        e_ = sb("lam_e", [128, 20], F32); t_ = sb("lam_t", [128, 20], F32); lb = tk.buf("lamtmp")
        lo = VOFF["lam"][0]
        tk.op("act", lambda e: e.activation(out=e_, in_=vecs[:, lo:lo + 20], func=AF.Exp, scale=-1.0), reads=[vecs_b], writes=[lb])
        tk.op("dve", lambda e: e.tensor_scalar(out=t_, in0=e_, scalar1=-0.25, scalar2=1.0 / 3.0, op0=ALU.mult, op1=ALU.add), reads=[lb], writes=[lb])
        tk.op("dve", lambda e: e.tensor_tensor(out=t_, in0=t_, in1=e_, op=ALU.mult), reads=[lb], writes=[lb])
        tk.op("dve", lambda e: e.tensor_scalar(out=t_, in0=t_, scalar1=-0.5, scalar2=None, op0=ALU.add), reads=[lb], writes=[lb])
        tk.op("dve", lambda e: e.tensor_tensor(out=t_, in0=t_, in1=e_, op=ALU.mult), reads=[lb], writes=[lb])
        tk.op("dve", lambda e: e.tensor_scalar(out=t_, in0=t_, scalar1=1.0, scalar2=None, op0=ALU.add), reads=[lb], writes=[lb])
        tk.op("dve", lambda e: e.tensor_tensor(out=t_, in0=t_, in1=e_, op=ALU.mult), reads=[lb], writes=[lb])
        tk.op("dve", lambda e: e.tensor_scalar(out=clam[:, 0, :], in0=t_, scalar1=-8.0, scalar2=None, op0=ALU.mult), reads=[lb], parts=[clam_b])
        tk.op("dve", lambda e: e.tensor_scalar(out=clam[:, 1, :], in0=t_, scalar1=-16.0, scalar2=None, op0=ALU.mult), reads=[lb], parts=[clam_b])

    build_clam()

    def build_mod():
        scv = sb("silu_c", [128, 16], F32); scv_b = tk.buf("silu_c")
        wst = [sb("wmst%d" % i, [128, KC, 512], F32) for i in range(2)]
        wst_b = [tk.buf("wmst%d" % i) for i in range(2)]
        modfm = sb("modfm", [128, 72, 2], F32); modfm_b = tk.buf("modfm")
        co = VOFF["c2"][0]
        tk.op("act", lambda e: e.activation(out=scv, in_=vecs[:, co:co + 16], func=AF.Silu), reads=[vecs_b], writes=[scv_b])
        wm_v = wmod_d.rearrange("(kc p) o -> p kc o", p=128)
        pb = psring.next()
        first = True
        for nb in range(18):
            i = nb % 2
            tk.dma("sp", wst[i], wm_v[:, :, nb * 512:(nb + 1) * 512], wst_b[i], writes=[wst_b[i]])
            for mi in range(4):
                m = nb * 4 + mi
                for kc in range(KC):
                    tk.op("pe", lambda e: e.matmul(psum[pb][:, 2 * m:2 * m + 2], lhsT=wst[i][:, kc, mi * 128:(mi + 1) * 128],
                                                   rhs=scv[:, 2 * kc:2 * kc + 2], start=(kc == 0), stop=(kc == KC - 1)),
                          reads=[scv_b, wst_b[i]], **({"writes": [ps_b[pb]]} if first else {"parts": [ps_b[pb]]}))
                    first = False
        bo = VOFF["b_mod"][0]
        tk.op("dve", lambda e: e.tensor_tensor(out=modfm, in0=psum[pb][:, 0:144].rearrange("p (j t) -> p j t", t=2),
                                               in1=vecs[:, bo:bo + 72].unsqueeze(2).to_broadcast([128, 72, 2]), op=ALU.add),
              reads=[ps_b[pb], vecs_b], writes=[modfm_b])
        for n_i, (gname, i_sh, i_sc, i_gate, gscale) in enumerate((("g_n1", 0, 1, 2, 0.5), ("g_n2", 3, 4, 5, 1.0), ("g_n3", 6, 7, 8, 0.5))):
            go = VOFF[gname][0]
            base = 3 * n_i
            tk.op("dve", lambda e: e.tensor_scalar(out=modp[:, base, :, :], in0=modfm[:, i_sc * 8:(i_sc + 1) * 8, :], scalar1=1.0, scalar2=None,
                                                    op0=ALU.add), reads=[modfm_b], parts=[modp_b])
            tk.op("dve", lambda e: e.tensor_tensor(out=modp[:, base, :, :], in0=modp[:, base, :, :],
                                                    in1=vecs[:, go:go + 8].unsqueeze(2).to_broadcast([128, 8, 2]), op=ALU.mult),
                  reads=[modp_b, vecs_b], writes=[modp_b])
            tk.op("dve", lambda e: e.tensor_copy(out=modp[:, base + 1, :, :], in_=modfm[:, i_sh * 8:(i_sh + 1) * 8, :]),
                  reads=[modfm_b], parts=[modp_b])
            tk.op("dve", lambda e: e.tensor_scalar(out=modp[:, base + 2, :, :], in0=modfm[:, i_gate * 8:(i_gate + 1) * 8, :], scalar1=gscale,
                                                    scalar2=None, op0=ALU.mult), reads=[modfm_b], parts=[modp_b])

    if stop_after == "pos":
        return finish_early([("TR", TR), ("TC", TC), ("clam", clam)])
    build_mod()
    if stop_after == "mod":
        return finish_early([("TR", TR), ("TC", TC), ("clam", clam), ("modp", modp)])
    end_phase()
    begin_phase()

    def convert_weights():
        st32 = [sb("cv32_%d" % i, [128, WBE], F32) for i in range(3)]
        st16 = [sb("cv16_%d" % i, [128, WBE], BF16) for i in range(3)]
        st32_b = [tk.buf("cv32_%d" % i) for i in range(3)]
        st16_b = [tk.buf("cv16_%d" % i) for i in range(3)]
        state = {"i": 0}

        def job(blk, srcs, nel):
            i = state["i"] % 3
            eng = ("dve", "pool", "act")[state["i"] % 3]
            state["i"] += 1
            for k, (dv, src) in enumerate(srcs):
                tk.dma("sp", dv(st32[i]), src, st32_b[i], **({"writes": [st32_b[i]]} if k == 0 else {"parts": [st32_b[i]]}))
            if eng == "act":
                tk.op("act", lambda e: e.activation(out=st16[i][:, 0:nel], in_=st32[i][:, 0:nel], func=AF.Copy),
                      reads=[st32_b[i]], writes=[st16_b[i]])
            else:
                tk.op(eng, lambda e: e.tensor_copy(out=st16[i][:, 0:nel], in_=st32[i][:, 0:nel]),
                      reads=[st32_b[i]], writes=[st16_b[i]])
            tk.dma("sp", WB_d[blk, :, 0:nel], st16[i][:, 0:nel], st16_b[i], reads=[st16_b[i]], writes=[wb_bufs[blk]])

        def v3(kc, oc, c0=0, cw=None):
            cw = oc if cw is None else cw
            return lambda st: st[:, 0:kc * oc].rearrange("p (k o) -> p k o", o=oc)[:, :, c0:c0 + cw]

        for f in range(2):
            wv = wup_d[f].rearrange("(kc p) o -> p kc o", p=128)
            for j in range(11):
                job(BLK["UP%d" % (f + 1)] + j,
                    [(v3(KC, 512, 0, 256), wv[:, :, j * 256:(j + 1) * 256]),
                     (v3(KC, 512, 256, 256), wv[:, :, FF + j * 256:FF + (j + 1) * 256])], KC * 512)
            dv = wdn_d[f].rearrange("(kc p) o -> p kc o", p=128)
            for m in range(8):
                job(BLK["DN%d" % (f + 1)] + m, [(v3(22, 128), dv[:, :, m * 128:(m + 1) * 128])], 22 * 128)
            if f == 0:
                iv = win_d.rearrange("(kc p) o -> p kc o", p=128)
                for j in range(4):
                    job(BLK["IN"] + j, [(v3(KC, 512, 0, 256), iv[:, :, j * 256:(j + 1) * 256]),
                                        (v3(KC, 512, 256, 256), iv[:, :, D + j * 256:D + (j + 1) * 256])], KC * 512)
                for j in range(3):
                    cw = 512 if j < 2 else 256
                    job(BLK["IN"] + 4 + j, [(v3(KC, cw), iv[:, :, 2048 + j * 512:2048 + j * 512 + cw])], KC * cw)
                for j in range(3):
                    cw = 512 if j < 2 else 256
                    job(BLK["IN"] + 7 + j, [(v3(KC, cw), iv[:, :, 3328 + j * 512:3328 + j * 512 + cw])], KC * cw)
                for j in range(4):
                    job(BLK["IN"] + 10 + j, [(v3(KC, 512), iv[:, :, 4608 + j * 512:4608 + (j + 1) * 512])], KC * 512)
                cv = wco_d.rearrange("(kc p) o -> p kc o", p=128)
                for j in range(2):
                    job(BLK["CO"] + j, [(v3(KC, 512), cv[:, :, j * 512:(j + 1) * 512])], KC * 512)
                lv = wlo_d.rearrange("(kc p) o -> p kc o", p=128)
                for j in range(4):
                    job(BLK["LO"] + j, [(v3(LC, 256), lv[:, :, j * 256:(j + 1) * 256])], LC * 256)
                ov = wo_d.rearrange("(kc p) o -> p kc o", p=128)
                for j in range(2):
                    job(BLK["WO"] + j, [(v3(KC, 512), ov[:, :, j * 512:(j + 1) * 512])], KC * 512)

    convert_weights()
    end_phase()
    if stop_after == "conv":
        return finish_early([("TR", TR), ("TC", TC), ("clam", clam), ("modp", modp), ("WB", WB_d)])

    tk.dma("sp", U_d[:, :, 0:15].rearrange("k p n -> p k n"), zt[:, 0:KC, 0:15], zt_b, reads=[zt_b], parts=[U_b])
    tk.dma("sp", UX_d[:, :, 0:2].rearrange("k p n -> p k n"), zt[:, :, 0:2], zt_b, reads=[zt_b], parts=[UX_b])
    tk.dma("sp", CUX_d[:, :, 0:2].rearrange("k p n -> p k n"), zt[:, :, 0:2], zt_b, reads=[zt_b], parts=[CUX_b])
    tk.dma("sp", CUX_d[:, :, CTX + 2:CTX + 4].rearrange("k p n -> p k n"), zt[:, :, 0:2], zt_b, reads=[zt_b], parts=[CUX_b])

    class WStream:
        def __init__(self, seq, depth=4):
            self.seq = seq
            self.depth = depth
            self.issued = 0
            self.used = 0
            self.slot_of = {}

        def _issue(self):
            blk, nel = self.seq[self.issued]
            s = self.issued % 5
            tk.dma("sp", wring_t[s][:, 0:nel], WB_d[blk, :, 0:nel], wring_b[s], reads=[wb_bufs[blk]], writes=[wring_b[s]])
            self.issued += 1

        def next(self, blk):
            assert self.seq[self.used][0] == blk, (self.seq[self.used], blk)
            while self.issued < len(self.seq) and self.issued <= self.used + self.depth - 1:
                self._issue()
            s = self.used % 5
            self.used += 1
            return wring_t[s], wring_b[s]

    def rms_to_h(xf, xf_b, h, h_b, sq, sq_b, N, gm, sh):
        tk.op("act", lambda e: e.activation(out=sq[:, :, 0:N], in_=xf[:, :, 0:N], func=AF.Square), reads=[xf_b], writes=[sq_b])
        pb = psring.next()
        for kc in range(KC):
            tk.op("pe", lambda e: e.matmul(psum[pb][:, 0:N], lhsT=ones_bf, rhs=sq[:, kc, 0:N], start=(kc == 0), stop=(kc == KC - 1)),
                  reads=[ones_b, sq_b], **({"writes": [ps_b[pb]]} if kc == 0 else {"parts": [ps_b[pb]]}))
        tk.op("act", lambda e: e.activation(out=small[:, 0, 0:N], in_=psum[pb][:, 0:N], func=AF.Sqrt, scale=1.0 / D, bias=consts[:, 0:1]),
              reads=[ps_b[pb], consts_b], writes=[rs_b])
        pr = psring.next()
        tk.op("dve", lambda e: e.reciprocal(out=psum[pr][:, 0:N], in_=small[:, 0, 0:N]), reads=[rs_b], writes=[ps_b[pr]])
        for kc in range(KC):
            ti = tmpring.next()
            tk.op("dve", lambda e: e.tensor_tensor(out=tmpg[ti][:, 0, 0:N], in0=xf[:, kc, 0:N], in1=psum[pr][:, 0:N], op=ALU.mult),
                  reads=[xf_b, ps_b[pr]], writes=[tmpg_b[ti]])
            tk.op("act", lambda e: e.activation(out=h[:, kc, 0:N], in_=tmpg[ti][:, 0, 0:N], func=AF.Identity,
                                                scale=gm[:, kc:kc + 1], bias=sh[:, kc:kc + 1]),
                  reads=[tmpg_b[ti], modp_b], parts=[h_b])
        return pr

    def ffn(ws, f, xf, xf_b, h, h_b, hid, hid_b, N, gate):
        for j in range(11):
            wt, wb = ws.next(BLK["UP%d" % (f + 1)] + j)
            w3 = wt[:, 0:KC * 512].rearrange("p (k o) -> p k o", o=512)
            pbs = [psring.next() for _ in range(4)]
            for mi in range(4):
                pb = pbs[mi]
                for kc in range(KC):
                    tk.op("pe", lambda e: e.matmul(psum[pb][:, 0:N], lhsT=w3[:, kc, mi * 128:(mi + 1) * 128], rhs=h[:, kc, 0:N],
                                                   start=(kc == 0), stop=(kc == KC - 1)),
                          reads=[wb, h_b], **({"writes": [ps_b[pb]]} if kc == 0 else {"parts": [ps_b[pb]]}))
            ti = tmpring.next()
            for i in range(2):
                tk.op("act", lambda e: e.activation(out=tmpg[ti][:, i, 0:N], in_=psum[pbs[i]][:, 0:N], func=AF.Silu),
                      reads=[ps_b[pbs[i]]], **({"writes": [tmpg_b[ti]]} if i == 0 else {"parts": [tmpg_b[ti]]}))
            for i in range(2):
                tk.op("dve", lambda e: e.tensor_tensor(out=hid[:, 2 * j + i, 0:N], in0=psum[pbs[2 + i]][:, 0:N], in1=tmpg[ti][:, i, 0:N], op=ALU.mult),
                      reads=[ps_b[pbs[2 + i]], tmpg_b[ti]], parts=[hid_b])
        for m in range(8):
            wt, wb = ws.next(BLK["DN%d" % (f + 1)] + m)
            w3 = wt[:, 0:22 * 128].rearrange("p (k o) -> p k o", o=128)
            pb = psring.next()
            for kc in range(22):
                tk.op("pe", lambda e: e.matmul(psum[pb][:, 0:N], lhsT=w3[:, kc, :], rhs=hid[:, kc, 0:N], start=(kc == 0), stop=(kc == 21)),
                      reads=[wb, hid_b], **({"writes": [ps_b[pb]]} if kc == 0 else {"parts": [ps_b[pb]]}))
            tk.op("dve", lambda e: e.scalar_tensor_tensor(out=xf[:, m, 0:N], in0=psum[pb][:, 0:N], scalar=gate[:, m:m + 1], in1=xf[:, m, 0:N],
                                                           op0=ALU.mult, op1=ALU.add),
                  reads=[ps_b[pb], modp_b, xf_b], writes=[xf_b])

    def up_seq(f):
        return [(BLK["UP%d" % (f + 1)] + j, KC * 512) for j in range(11)] + [(BLK["DN%d" % (f + 1)] + m, 22 * 128) for m in range(8)]

    IN_NEL = [KC * 512] * 4 + [KC * 512, KC * 512, KC * 256] * 2 + [KC * 512] * 4

    begin_phase()
    alloc_ring("A")
    xin = [sb("xin%d" % i, [128, D], F32) for i in range(2)]
    xin_b = [tk.buf("xin%d" % i) for i in range(2)]
    xinring = Ring([0, 1])
    xfm = [sb("xfm%d" % i, [128, KC, NT], F32) for i in range(2)]
    xfm_b = [tk.buf("xfm%d" % i) for i in range(2)]
    hA = sb("hA", [128, KC, NT], BF16); hA_b = tk.buf("hA")
    sqA = sb("sqA", [128, KC, NT], BF16); sqA_b = tk.buf("sqA")
    hidA = sb("hidA", [128, 22, NT], BF16); hidA_b = tk.buf("hidA")
    ust = sb("ust", [128, KC, NT], BF16); ust_b = tk.buf("ust")
    uxst = sb("uxst", [128, LC, NT], BF16); uxst_b = tk.buf("uxst")
    gst = sb("gst", [128, LC, NT], BF16); gst_b = tk.buf("gst")
    gclst = sb("gclst", [128, 16, NT], BF16); gclst_b = tk.buf("gclst")

    tilesA = [("ctx", 0, CTX)] + [("lat", i * NT, NT) for i in range(LT // NT)] + [("halo", LT, HALO)]
    if tilesA_override is not None:
        tilesA = tilesA_override
    seqA = []
    for kind, t0, N in tilesA:
        seqA += up_seq(0)
        if kind == "lat":
            seqA += [(BLK["IN"] + j, IN_NEL[j]) for j in range(14)]
        elif kind == "halo":
            seqA += [(BLK["IN"] + j, IN_NEL[j]) for j in range(7)]
        else:
            seqA += [(BLK["IN"] + j, IN_NEL[j]) for j in range(4, 7)]
    wsA = WStream(seqA)
    bin_o = VOFF["b_in"][0]

    for ti_, (kind, t0, N) in enumerate(tilesA):
        mj = 1 if kind == "ctx" else 0
        xf = xfm[ti_ % 2]; xf_b = xfm_b[ti_ % 2]
        src = ctx_d if kind == "ctx" else x_d
        nsub = max(1, N // 128)
        sw = min(N, 128)
        for s in range(nsub):
            xi = xinring.next()
            tk.dma("sp", xin[xi][0:sw, :], src[t0 + s * 128:t0 + s * 128 + sw, :], xin_b[xi], writes=[xin_b[xi]])
            for half in range(2):
                pb = psring.next()
                for q in range(4):
                    kc = half * 4 + q
                    tk.op("pe", lambda e: e.transpose(out=psum[pb][:, q * 128:q * 128 + sw], in_=xin[xi][0:sw, kc * 128:(kc + 1) * 128],
                                                      identity=ident[0:sw, 0:sw]),
                          reads=[xin_b[xi], ident_b], **({"writes": [ps_b[pb]]} if q == 0 else {"parts": [ps_b[pb]]}))
                pv = psum[pb][:, :].rearrange("p (q n) -> p q n", n=128)[:, :, 0:sw]
                ov = xf[:, half * 4:half * 4 + 4, s * 128:s * 128 + sw]
                if kind == "ctx":
                    tk.op("dve", lambda e: e.tensor_copy(out=ov, in_=pv), reads=[ps_b[pb]], parts=[xf_b])
                else:
                    tl = t0 + s * 128
                    a0 = tl // 64
                    nr = max(1, sw // 64)
                    ncol = min(sw, 64)
                    if half == 0:
                        posv = TR[:, :, a0:a0 + nr].unsqueeze(3).to_broadcast([128, 4, nr, ncol])
                    else:
                        posv = TC[:, :, 0:ncol].unsqueeze(2).to_broadcast([128, 4, nr, ncol])
                    tk.op("dve", lambda e: e.tensor_tensor(out=ov.rearrange("p q (r c) -> p q r c", c=ncol),
                                                           in0=pv.rearrange("p q (r c) -> p q r c", c=ncol), in1=posv, op=ALU.add),
                          reads=[ps_b[pb], pos_b], parts=[xf_b])
        rms_to_h(xf, xf_b, hA, hA_b, sqA, sqA_b, N, modp[:, 0, :, mj], modp[:, 1, :, mj])
        ffn(wsA, 0, xf, xf_b, hA, hA_b, hidA, hidA_b, N, modp[:, 2, :, mj])
        if kind == "lat":
            tk.dma("sp", X1_d[:, :, t0:t0 + N].rearrange("k p n -> p k n"), xf[:, :, 0:N], xf_b, reads=[xf_b], parts=[X1_b])
        rms_to_h(xf, xf_b, hA, hA_b, sqA, sqA_b, N, modp[:, 3, :, mj], modp[:, 4, :, mj])
        blocks = list(range(14)) if kind == "lat" else (list(range(7)) if kind == "halo" else [4, 5, 6])
        for j in blocks:
            wt, wb = wsA.next(BLK["IN"] + j)
            ncols = IN_NEL[j] // KC
            nch = ncols // 128
            w3 = wt[:, 0:IN_NEL[j]].rearrange("p (k o) -> p k o", o=ncols)
            pbs = [psring.next() for _ in range(nch)]
            for mi in range(nch):
                pb = pbs[mi]
                for kc in range(KC):
                    tk.op("pe", lambda e: e.matmul(psum[pb][:, 0:N], lhsT=w3[:, kc, mi * 128:(mi + 1) * 128], rhs=hA[:, kc, 0:N],
                                                   start=(kc == 0), stop=(kc == KC - 1)),
                          reads=[wb, hA_b], **({"writes": [ps_b[pb]]} if kc == 0 else {"parts": [ps_b[pb]]}))
            if j < 4:
                ti = tmpring.next()
                for i in range(2):
                    bc = bin_o + 8 + 2 * j + i
                    tk.op("act", lambda e: e.activation(out=tmpg[ti][:, i, 0:N], in_=psum[pbs[2 + i]][:, 0:N], func=AF.Sigmoid,
                                                        bias=vecs[:, bc:bc + 1], scale=1.0),
                          reads=[ps_b[pbs[2 + i]], vecs_b], **({"writes": [tmpg_b[ti]]} if i == 0 else {"parts": [tmpg_b[ti]]}))
                for i in range(2):
                    bc = bin_o + 2 * j + i
                    tk.op("dve", lambda e: e.scalar_tensor_tensor(out=ust[:, 2 * j + i, 0:N], in0=psum[pbs[i]][:, 0:N], scalar=vecs[:, bc:bc + 1],
                                                                   in1=tmpg[ti][:, i, 0:N], op0=ALU.add, op1=ALU.mult),
                          reads=[ps_b[pbs[i]], vecs_b, tmpg_b[ti]], parts=[ust_b])
            elif j < 7:
                for mi in range(nch):
                    cc = (j - 4) * 4 + mi
                    bc = bin_o + 16 + cc
                    tk.op("dve", lambda e: e.tensor_scalar(out=uxst[:, cc, 0:N], in0=psum[pbs[mi]][:, 0:N], scalar1=vecs[:, bc:bc + 1],
                                                           scalar2=None, op0=ALU.add),
                          reads=[ps_b[pbs[mi]], vecs_b], parts=[uxst_b])
            elif j < 10:
                for mi in range(nch):
                    cc = (j - 7) * 4 + mi
                    bc = bin_o + 26 + cc
                    tk.op("act", lambda e: e.activation(out=gst[:, cc, 0:N], in_=psum[pbs[mi]][:, 0:N], func=AF.Gelu_apprx_tanh,
                                                        bias=vecs[:, bc:bc + 1], scale=1.0),
                          reads=[ps_b[pbs[mi]], vecs_b], parts=[gst_b])
            else:
                for mi in range(nch):
                    c16 = (j - 10) * 4 + mi
                    bc = bin_o + 36 + c16
                    tk.op("act", lambda e: e.activation(out=gclst[:, c16, 0:N], in_=psum[pbs[mi]][:, 0:N], func=AF.Sigmoid,
                                                        bias=vecs[:, bc:bc + 1], scale=1.0),
                          reads=[ps_b[pbs[mi]], vecs_b], parts=[gclst_b])
        if kind == "ctx":
            tk.dma("sp", CUX_d[:, :, 2:2 + N].rearrange("k p n -> p k n"), uxst[:, :, 0:N], uxst_b, reads=[uxst_b], parts=[CUX_b])
        else:
            tk.dma("sp", U_d[:, :, 15 + t0:15 + t0 + N].rearrange("k p n -> p k n"), ust[:, :, 0:N], ust_b, reads=[ust_b], parts=[U_b])
            tk.dma("sp", UX_d[:, :, 2 + t0:2 + t0 + N].rearrange("k p n -> p k n"), uxst[:, :, 0:N], uxst_b, reads=[uxst_b], parts=[UX_b])
            if kind == "lat":
                tk.dma("sp", G_d[:, :, t0:t0 + N].rearrange("k p n -> p k n"), gst[:, :, 0:N], gst_b, reads=[gst_b], parts=[G_b])
                tk.dma("sp", GCL_d[:, :, t0:t0 + N].rearrange("k p n -> p k n"), gclst[:, :, 0:N], gclst_b, reads=[gclst_b], parts=[GCL_b])

    end_phase()

    dbg = {}
    if debug:
        for nm, ap_ in (("X1", X1_d), ("U", U_d), ("UX", UX_d), ("CUX", CUX_d), ("G", G_d), ("GCL", GCL_d)):
            o = nc.dram_tensor("dbg_" + nm, list(ap_.shape), ap_.dtype, kind="ExternalOutput").ap()
            dbg[nm] = o
            b_ = tk.buf("dbg_" + nm)
            tk.dma("sp", o, ap_, b_, writes=[b_])
        for nm, ap_ in (("modp", modp), ("TR", TR), ("TC", TC), ("clam", clam)):
            o = nc.dram_tensor("dbg_" + nm, list(ap_.shape), ap_.dtype, kind="ExternalOutput").ap()
            b_ = tk.buf("dbg_" + nm)
            tk.dma("sp", o, ap_, b_, writes=[b_])
    if stop_after == "A":
        z = sb("zout", [128, D], F32); z_b = tk.buf("zout")
        tk.op("dve", lambda e: e.memset(z, 0.0), writes=[z_b])
        for i in range(LT // 128):
            tk.dma("sp", out_d[i * 128:(i + 1) * 128, :], z, z_b, reads=[z_b], parts=[out_b])
        tk.barrier()
        return nc


    begin_phase()
    uxb = sb("uxb", [128, WX], BF16); uxb_b = tk.buf("uxb")
    xr32 = sb("xr32", [128, LT], F32); xr32_b = tk.buf("xr32")
    xrb = sb("xrb", [128, LT], BF16); xrb_b = tk.buf("xrb")
    Rt = sb("Rt", [128, LT], F32); Rt_b = tk.buf("Rt")
    A2t = sb("A2t", [128, LT], F32); A2t_b = tk.buf("A2t")
    IGt = sb("IGt", [128, LT], F32); IGt_b = tk.buf("IGt")
    Ht = sb("Ht", [128, LT], F32); Ht_b = tk.buf("Ht")
    gw32 = sb("gw32", [128, 4, 128], F32); gw32_b = tk.buf("gw32")
    gwb = sb("gwb", [128, LC, 4, 128], BF16); gwb_b = tk.buf("gwb")
    dgl = sb("dgl", [128, 5, 128], BF16); dgl_b = tk.buf("dgl")
    dgc = sb("dgc", [128, 31, 128], BF16); dgc_b = tk.buf("dgc")
    ub = sb("ub", [128, WU], BF16); ub_b = tk.buf("ub")
    cst = sb("cst", [128, LT], F32); cst_b = tk.buf("cst")
    cux = sb("cux", [128, CTX + 4], BF16); cux_b = tk.buf("cux")
    gath = sb("gath", [128, NCORES, LC], F32); gath_b = tk.buf("gath")
    wlc_o = VOFF["w_lc"][0]; blc_o = VOFF["b_lc"][0]; ba_o = VOFF["b_a"][0]; bx_o = VOFF["b_x"][0]
    wdw_o = VOFF["w_dw"][0]; bdw_o = VOFF["b_dw"][0]

    import os as _os
    knob = _os.environ.get("KNOB", "ctx,lat,conf").split(",")
    ncc = int(_os.environ.get("NCC", str(LC)))

    def lru_conv(src, src_b, Ntok, cc):
        for t0 in range(0, Ntok, NT):
            n = min(NT, Ntok - t0)
            pb = psring.next()
            for tap in range(5):
                tk.op("pe", lambda e: e.matmul(psum[pb][:, 0:n], lhsT=dgl[:, tap, :], rhs=src[:, t0 + tap:t0 + tap + n],
                                               start=(tap == 0), stop=(tap == 4)),
                      reads=[dgl_b, src_b], **({"writes": [ps_b[pb]]} if tap == 0 else {"parts": [ps_b[pb]]}))
            tk.op("act", lambda e: e.activation(out=xr32[:, t0:t0 + n], in_=psum[pb][:, 0:n], func=AF.Identity,
                                                bias=vecs[:, blc_o + cc:blc_o + cc + 1], scale=1.0),
                  reads=[ps_b[pb], vecs_b], parts=[xr32_b])
            tk.op("pool", lambda e: e.tensor_copy(out=xrb[:, t0:t0 + n], in_=xr32[:, t0:t0 + n]),
                  reads=[xr32_b], parts=[xrb_b])

    def lru_dir(Ntok, d, cc, init, reverse):
        if "nodir" in knob:
            return
        for t0 in range(0, Ntok, NT):
            n = min(NT, Ntok - t0)
            pr_ = psring.next(); pi_ = psring.next()
            tk.op("pe", lambda e: e.matmul(psum[pr_][:, 0:n], lhsT=gwb[:, cc, 2 * d, :], rhs=xrb[:, t0:t0 + n], start=True, stop=True),
                  reads=[gwb_b, xrb_b], writes=[ps_b[pr_]])
            tk.op("pe", lambda e: e.matmul(psum[pi_][:, 0:n], lhsT=gwb[:, cc, 2 * d + 1, :], rhs=xrb[:, t0:t0 + n], start=True, stop=True),
                  reads=[gwb_b, xrb_b], writes=[ps_b[pi_]])
            tk.op("act", lambda e: e.activation(out=Rt[:, t0:t0 + n], in_=psum[pr_][:, 0:n], func=AF.Sigmoid,
                                                bias=vecs[:, ba_o + d * 10 + cc:ba_o + d * 10 + cc + 1], scale=1.0),
                  reads=[ps_b[pr_], vecs_b], parts=[Rt_b])
            tk.op("act", lambda e: e.activation(out=IGt[:, t0:t0 + n], in_=psum[pi_][:, 0:n], func=AF.Sigmoid,
                                                bias=vecs[:, bx_o + d * 10 + cc:bx_o + d * 10 + cc + 1], scale=1.0),
                  reads=[ps_b[pi_], vecs_b], parts=[IGt_b])
        k = d * 10 + cc
        if "noexp" in knob:
            return
        tk.op("act", lambda e: e.activation(out=A2t[:, 0:Ntok], in_=Rt[:, 0:Ntok], func=AF.Exp, scale=clam[:, 1, k:k + 1]),
              reads=[Rt_b, clam_b], writes=[A2t_b])
        tk.op("act", lambda e: e.activation(out=Rt[:, 0:Ntok], in_=Rt[:, 0:Ntok], func=AF.Exp, scale=clam[:, 0, k:k + 1]),
              reads=[Rt_b, clam_b], writes=[Rt_b])
        tk.op("act", lambda e: e.activation(out=A2t[:, 0:Ntok], in_=A2t[:, 0:Ntok], func=AF.Sqrt, scale=-1.0, bias=consts[:, 1:2]),
              reads=[A2t_b, consts_b], writes=[A2t_b])
        if "nomul" in knob:
            return
        tk.op("pool", lambda e: e.tensor_tensor(out=IGt[:, 0:Ntok], in0=IGt[:, 0:Ntok], in1=A2t[:, 0:Ntok], op=ALU.mult),
              reads=[IGt_b, A2t_b], writes=[IGt_b])
        tk.op("dve", lambda e: e.tensor_tensor(out=IGt[:, 0:Ntok], in0=IGt[:, 0:Ntok], in1=xr32[:, 0:Ntok], op=ALU.mult),
              reads=[IGt_b, xr32_b], writes=[IGt_b])
        if "noscan" in knob:
            return
        if reverse:
            tk.op("dve", lambda e: e.tensor_tensor_scan(out=Ht[:, 0:Ntok][:, ::-1], data0=Rt[:, 0:Ntok][:, ::-1], data1=IGt[:, 0:Ntok][:, ::-1],
                                                        initial=init, op0=ALU.mult, op1=ALU.add),
                  reads=[Rt_b, IGt_b, carry_b, s0_b], writes=[Ht_b])
        else:
            tk.op("dve", lambda e: e.tensor_tensor_scan(out=Ht[:, 0:Ntok], data0=Rt[:, 0:Ntok], data1=IGt[:, 0:Ntok],
                                                        initial=init, op0=ALU.mult, op1=ALU.add),
                  reads=[Rt_b, IGt_b, carry_b, s0_b], writes=[Ht_b])

    def build_dgl(cc):
        for tap in range(5):
            col = wlc_o + tap * 10 + cc
            tk.op("act", lambda e: e.activation(out=dgl[:, tap, :], in_=ident, func=AF.Identity, scale=vecs[:, col:col + 1], bias=consts[:, 3:4]),
                  reads=[ident_b, vecs_b, consts_b], **({"writes": [dgl_b]} if tap == 0 else {"parts": [dgl_b]}))

    for cc in range(ncc):
        tk.dma("sp", gw32, wg_d[cc], gw32_b, writes=[gw32_b])
        tk.op("dve", lambda e: e.tensor_copy(out=gwb[:, cc, :, :], in_=gw32), reads=[gw32_b], parts=[gwb_b])
        tk.dma("sp", cux, CUX_d[cc], cux_b, reads=[CUX_b], writes=[cux_b])
        tk.dma("sp", uxb, UX_d[cc], uxb_b, reads=[UX_b], writes=[uxb_b])
        build_dgl(cc)
        if "ctx" in knob:
            lru_conv(cux, cux_b, CTX, cc)
            lru_dir(CTX, 0, cc, 0.0, False)
            tk.op("dve", lambda e: e.tensor_copy(out=s0[:, cc:cc + 1], in_=Ht[:, CTX - 1:CTX]), reads=[Ht_b], parts=[s0_b])
        if "lat" in knob:
            lru_conv(uxb, uxb_b, LT, cc)
            lru_dir(LT, 0, cc, s0[:, cc:cc + 1], False)
            tk.op("dve", lambda e: e.tensor_copy(out=sfin[:, cc:cc + 1], in_=Ht[:, LT - 1:LT]), reads=[Ht_b], parts=[sfin_b])
            tk.dma("sp", HF_d[cc], Ht, Ht_b, reads=[Ht_b], parts=[HF_b])
        if cc < KC and "conf" in knob:
            kc = cc
            tk.dma("sp", ub, U_d[kc], ub_b, reads=[U_b], writes=[ub_b])
            for tap in range(31):
                col = wdw_o + tap * 8 + kc
                eng = "act" if tap % 2 == 0 else "pool"
                if eng == "act":
                    tk.op("act", lambda e: e.activation(out=dgc[:, tap, :], in_=ident, func=AF.Identity, scale=vecs[:, col:col + 1], bias=consts[:, 3:4]),
                          reads=[ident_b, vecs_b, consts_b], **({"writes": [dgc_b]} if tap == 0 else {"parts": [dgc_b]}))
                else:
                    tk.op("pool", lambda e: e.tensor_scalar(out=dgc[:, tap, :], in0=ident, scalar1=vecs[:, col:col + 1], scalar2=None, op0=ALU.mult),
                          reads=[ident_b, vecs_b], parts=[dgc_b])
            for ti in range(LT // NT):
                t0 = ti * NT
                pb = psring.next()
                for tap in range(31):
                    tk.op("pe", lambda e: e.matmul(psum[pb][:, :], lhsT=dgc[:, tap, :], rhs=ub[:, t0 + tap:t0 + tap + NT],
                                                   start=(tap == 0), stop=(tap == 30)),
                          reads=[dgc_b, ub_b], **({"writes": [ps_b[pb]]} if tap == 0 else {"parts": [ps_b[pb]]}))
                if ti % 2 == 0:
                    tk.op("act", lambda e: e.activation(out=cst[:, t0:t0 + NT], in_=psum[pb][:, :], func=AF.Identity,
                                                        bias=vecs[:, bdw_o + kc:bdw_o + kc + 1], scale=1.0),
                          reads=[ps_b[pb], vecs_b], parts=[cst_b])
                else:
                    tk.op("dve", lambda e: e.tensor_scalar(out=cst[:, t0:t0 + NT], in0=psum[pb][:, :], scalar1=vecs[:, bdw_o + kc:bdw_o + kc + 1],
                                                           scalar2=None, op0=ALU.add),
                          reads=[ps_b[pb], vecs_b], parts=[cst_b])
            tk.dma("sp", C_d[kc], cst, cst_b, reads=[cst_b], parts=[C_b])

    if stop_after == "B1":
        return finish_early([])
    tk.dma("sp", cci_d, sfin, sfin_b, reads=[sfin_b], writes=[cci_b])
    tk.coll(lambda e: e.collective_compute("AllGather", ALU.bypass, replica_groups=[list(range(NCORES))],
                                           ins=[cci_d.opt()], outs=[cco_d.opt()]), reads=[cci_b], writes=[cco_b])
    tk.dma("sp", gath, cco_d.rearrange("(r p) c -> p r c", p=128), gath_b, reads=[cco_b], writes=[gath_b])
    pm_o = VOFF["pmask"][0]
    tk.op("dve", lambda e: e.tensor_scalar(out=carry, in0=gath[:, 0, :], scalar1=vecs[:, pm_o:pm_o + 1], scalar2=None, op0=ALU.mult),
          reads=[gath_b, vecs_b], writes=[carry_b])
    for r_ in range(1, NCORES):
        tk.op("dve", lambda e: e.scalar_tensor_tensor(out=carry, in0=gath[:, r_, :], scalar=vecs[:, pm_o + r_:pm_o + r_ + 1], in1=carry,
                                                       op0=ALU.mult, op1=ALU.add), reads=[gath_b, vecs_b, carry_b], writes=[carry_b])

    if stop_after == "BX":
        return finish_early([])
    for cc in range(LC):
        tk.dma("sp", uxb, UX_d[cc], uxb_b, reads=[UX_b], writes=[uxb_b])
        build_dgl(cc)
        lru_conv(uxb, uxb_b, LT, cc)
        lru_dir(LT, 1, cc, carry[:, cc:cc + 1], True)
        tk.dma("sp", A2t, HF_d[cc], A2t_b, reads=[HF_b], writes=[A2t_b])
        tk.dma("sp", ub[:, 0:LT], G_d[cc], ub_b, reads=[G_b], writes=[ub_b])
        tk.op("pool", lambda e: e.tensor_tensor(out=Ht, in0=Ht, in1=A2t, op=ALU.add), reads=[Ht_b, A2t_b], writes=[Ht_b])
        tk.op("dve", lambda e: e.tensor_tensor(out=xrb, in0=Ht, in1=ub[:, 0:LT], op=ALU.mult), reads=[Ht_b, ub_b], writes=[xrb_b])
        tk.dma("sp", M_d[cc], xrb, xrb_b, reads=[xrb_b], parts=[M_b])
    end_phase()
    if stop_after == "B":
        return finish_early([("M", M_d), ("C", C_d), ("HF", HF_d)])
    if stop_after == "B0":
        return finish_early([])

    begin_phase()
    alloc_ring("C")
    c32 = sb("c32", [128, KC, NT], F32); c32_b = tk.buf("c32")
    xfC = sb("xfC", [128, KC, NT], F32); xfC_b = tk.buf("xfC")
    B1 = sb("B1", [128, KC, NT], BF16); B1_b = tk.buf("B1")
    B2 = sb("B2", [128, KC, NT], BF16); B2_b = tk.buf("B2")
    hidC = sb("hidC", [128, 22, NT], BF16); hidC_b = tk.buf("hidC")
    gcl = sb("gcl", [128, 16, NT], BF16); gcl_b = tk.buf("gcl")
    mrg = sb("mrg", [128, LC, NT], BF16); mrg_b = tk.buf("mrg")
    gcy = sb("gcy", [128, KC, NT], F32); gcy_b = tk.buf("gcy")
    ost = [sb("ost%d" % i, [128, D], F32) for i in range(2)]
    ost_b = [tk.buf("ost%d" % i) for i in range(2)]
    gln_o = VOFF["g_ln"][0]; bln_o = VOFF["b_ln"][0]; gf_o = VOFF["g_final"][0]
    seqC = []
    for i in range(LT // NT):
        seqC += [(BLK["CO"] + j, KC * 512) for j in range(2)] + [(BLK["LO"] + j, LC * 256) for j in range(4)]
        seqC += [(BLK["WO"] + j, KC * 512) for j in range(2)] + up_seq(1)
    wsC = WStream(seqC)
    for i in range(LT // NT):
        t0 = i * NT
        N = NT
        tk.dma("sp", c32, C_d[:, :, t0:t0 + N].rearrange("k p n -> p k n"), c32_b, reads=[C_b], writes=[c32_b])
        tk.dma("sp", gcl, GCL_d[:, :, t0:t0 + N].rearrange("k p n -> p k n"), gcl_b, reads=[GCL_b], writes=[gcl_b])
        tk.dma("sp", mrg, M_d[:, :, t0:t0 + N].rearrange("k p n -> p k n"), mrg_b, reads=[M_b], writes=[mrg_b])
        tk.dma("sp", xfC, X1_d[:, :, t0:t0 + N].rearrange("k p n -> p k n"), xfC_b, reads=[X1_b], writes=[xfC_b])
        tk.op("dve", lambda e: e.tensor_copy(out=B1, in_=c32), reads=[c32_b], writes=[B1_b])
        tk.op("act", lambda e: e.activation(out=B2, in_=c32, func=AF.Square), reads=[c32_b], writes=[B2_b])
        p1 = psring.next(); p2 = psring.next()
        for kc in range(KC):
            tk.op("pe", lambda e: e.matmul(psum[p1][:, :], lhsT=ones_bf, rhs=B1[:, kc, :], start=(kc == 0), stop=(kc == KC - 1)),
                  reads=[ones_b, B1_b], **({"writes": [ps_b[p1]]} if kc == 0 else {"parts": [ps_b[p1]]}))
        for kc in range(KC):
            tk.op("pe", lambda e: e.matmul(psum[p2][:, :], lhsT=ones_bf, rhs=B2[:, kc, :], start=(kc == 0), stop=(kc == KC - 1)),
                  reads=[ones_b, B2_b], **({"writes": [ps_b[p2]]} if kc == 0 else {"parts": [ps_b[p2]]}))
        tk.op("dve", lambda e: e.tensor_scalar(out=small[:, 0, :], in0=psum[p1][:, :], scalar1=1.0 / D, scalar2=None, op0=ALU.mult),
              reads=[ps_b[p1]], writes=[rs_b])
        tk.op("dve", lambda e: e.tensor_tensor(out=small[:, 1, :], in0=small[:, 0, :], in1=small[:, 0, :], op=ALU.mult), reads=[rs_b], writes=[rs_b])
        tk.op("dve", lambda e: e.scalar_tensor_tensor(out=small[:, 2, :], in0=psum[p2][:, :], scalar=1.0 / D, in1=small[:, 1, :],
                                                       op0=ALU.mult, op1=ALU.subtract), reads=[ps_b[p2], rs_b], writes=[rs_b])
        tk.op("act", lambda e: e.activation(out=small[:, 2, :], in_=small[:, 2, :], func=AF.Sqrt, scale=1.0, bias=consts[:, 0:1]),
              reads=[rs_b, consts_b], writes=[rs_b])
        pr = psring.next()
        tk.op("dve", lambda e: e.reciprocal(out=psum[pr][:, :], in_=small[:, 2, :]), reads=[rs_b], writes=[ps_b[pr]])
        for kc in range(KC):
            ti = tmpring.next()
            tk.op("dve", lambda e: e.tensor_tensor(out=tmpg[ti][:, 0, :], in0=c32[:, kc, :], in1=small[:, 0, :], op=ALU.subtract),
                  reads=[c32_b, rs_b], writes=[tmpg_b[ti]])
            tk.op("dve", lambda e: e.tensor_tensor(out=tmpg[ti][:, 1, :], in0=tmpg[ti][:, 0, :], in1=psum[pr][:, :], op=ALU.mult),
                  reads=[tmpg_b[ti], ps_b[pr]], writes=[tmpg_b[ti]])
            tk.op("act", lambda e: e.activation(out=B1[:, kc, :], in_=tmpg[ti][:, 1, :], func=AF.Silu,
                                                scale=vecs[:, gln_o + kc:gln_o + kc + 1], bias=vecs[:, bln_o + kc:bln_o + kc + 1]),
                  reads=[tmpg_b[ti], vecs_b], **({"writes": [B1_b]} if kc == 0 else {"parts": [B1_b]}))
        for j in range(2):
            wt, wb = wsC.next(BLK["CO"] + j)
            w3 = wt[:, 0:KC * 512].rearrange("p (k o) -> p k o", o=512)
            pbs = [psring.next() for _ in range(4)]
            for mi in range(4):
                pb = pbs[mi]
                for kc in range(KC):
                    tk.op("pe", lambda e: e.matmul(psum[pb][:, :], lhsT=w3[:, kc, mi * 128:(mi + 1) * 128], rhs=B1[:, kc, :],
                                                   start=(kc == 0), stop=(kc == KC - 1)),
                          reads=[wb, B1_b], **({"writes": [ps_b[pb]]} if kc == 0 else {"parts": [ps_b[pb]]}))
            for mi in range(4):
                m = 4 * j + mi
                tk.op("dve", lambda e: e.tensor_tensor(out=gcy[:, m, :], in0=psum[pbs[mi]][:, :], in1=gcl[:, m, :], op=ALU.mult),
                      reads=[ps_b[pbs[mi]], gcl_b], **({"writes": [gcy_b]} if m == 0 else {"parts": [gcy_b]}))
        for j in range(4):
            wt, wb = wsC.next(BLK["LO"] + j)
            w3 = wt[:, 0:LC * 256].rearrange("p (k o) -> p k o", o=256)
            pbs = [psring.next() for _ in range(2)]
            for mi in range(2):
                pb = pbs[mi]
                for kc in range(LC):
                    tk.op("pe", lambda e: e.matmul(psum[pb][:, :], lhsT=w3[:, kc, mi * 128:(mi + 1) * 128], rhs=mrg[:, kc, :],
                                                   start=(kc == 0), stop=(kc == LC - 1)),
                          reads=[wb, mrg_b], **({"writes": [ps_b[pb]]} if kc == 0 else {"parts": [ps_b[pb]]}))
            ti = tmpring.next()
            for mi in range(2):
                m = 2 * j + mi
                tk.op("dve", lambda e: e.tensor_tensor(out=tmpg[ti][:, mi, :], in0=psum[pbs[mi]][:, :], in1=gcl[:, 8 + m, :], op=ALU.mult),
                      reads=[ps_b[pbs[mi]], gcl_b], **({"writes": [tmpg_b[ti]]} if mi == 0 else {"parts": [tmpg_b[ti]]}))
            tk.op("pool", lambda e: e.tensor_tensor(out=B2[:, 2 * j:2 * j + 2, :], in0=tmpg[ti], in1=gcy[:, 2 * j:2 * j + 2, :], op=ALU.add),
                  reads=[tmpg_b[ti], gcy_b], **({"writes": [B2_b]} if j == 0 else {"parts": [B2_b]}))
        for j in range(2):
            wt, wb = wsC.next(BLK["WO"] + j)
            w3 = wt[:, 0:KC * 512].rearrange("p (k o) -> p k o", o=512)
            pbs = [psring.next() for _ in range(4)]
            for mi in range(4):
                pb = pbs[mi]
                for kc in range(KC):
                    tk.op("pe", lambda e: e.matmul(psum[pb][:, :], lhsT=w3[:, kc, mi * 128:(mi + 1) * 128], rhs=B2[:, kc, :],
                                                   start=(kc == 0), stop=(kc == KC - 1)),
                          reads=[wb, B2_b], **({"writes": [ps_b[pb]]} if kc == 0 else {"parts": [ps_b[pb]]}))
            for mi in range(4):
                m = 4 * j + mi
                tk.op("dve", lambda e: e.scalar_tensor_tensor(out=xfC[:, m, :], in0=psum[pbs[mi]][:, :], scalar=modp[:, 5, m:m + 1, 0],
                                                               in1=xfC[:, m, :], op0=ALU.mult, op1=ALU.add),
                      reads=[ps_b[pbs[mi]], modp_b, xfC_b], writes=[xfC_b])
        rms_to_h(xfC, xfC_b, B1, B1_b, B2, B2_b, N, modp[:, 6, :, 0], modp[:, 7, :, 0])
        ffn(wsC, 1, xfC, xfC_b, B1, B1_b, hidC, hidC_b, N, modp[:, 8, :, 0])
        tk.op("act", lambda e: e.activation(out=B2, in_=xfC, func=AF.Square), reads=[xfC_b], writes=[B2_b])
        pb = psring.next()
        for kc in range(KC):
            tk.op("pe", lambda e: e.matmul(psum[pb][:, :], lhsT=ones_bf, rhs=B2[:, kc, :], start=(kc == 0), stop=(kc == KC - 1)),
                  reads=[ones_b, B2_b], **({"writes": [ps_b[pb]]} if kc == 0 else {"parts": [ps_b[pb]]}))
        tk.op("act", lambda e: e.activation(out=small[:, 0, :], in_=psum[pb][:, :], func=AF.Sqrt, scale=1.0 / D, bias=consts[:, 0:1]),
              reads=[ps_b[pb], consts_b], writes=[rs_b])
        pr = psring.next()
        tk.op("dve", lambda e: e.reciprocal(out=psum[pr][:, :], in_=small[:, 0, :]), reads=[rs_b], writes=[ps_b[pr]])
        for kc in range(KC):
            tk.op("dve", lambda e: e.scalar_tensor_tensor(out=c32[:, kc, :], in0=xfC[:, kc, :], scalar=vecs[:, gf_o + kc:gf_o + kc + 1],
                                                           in1=psum[pr][:, :], op0=ALU.mult, op1=ALU.mult),
                  reads=[xfC_b, vecs_b, ps_b[pr]], **({"writes": [c32_b]} if kc == 0 else {"parts": [c32_b]}))
        for s_ in range(N // 128):
            oi = s_ % 2
            for half in range(2):
                pb = psring.next()
                for q in range(4):
                    kc = half * 4 + q
                    tk.op("pe", lambda e: e.transpose(out=psum[pb][:, q * 128:(q + 1) * 128], in_=c32[:, kc, s_ * 128:(s_ + 1) * 128], identity=ident),
                          reads=[c32_b, ident_b], **({"writes": [ps_b[pb]]} if q == 0 else {"parts": [ps_b[pb]]}))
                if half == 0:
                    tk.op("act", lambda e: e.activation(out=ost[oi][:, 0:512], in_=psum[pb][:, :], func=AF.Copy), reads=[ps_b[pb]], writes=[ost_b[oi]])
                else:
                    tk.op("dve", lambda e: e.tensor_copy(out=ost[oi][:, 512:1024], in_=psum[pb][:, :]), reads=[ps_b[pb]], parts=[ost_b[oi]])
            tk.dma("sp", out_d[t0 + s_ * 128:t0 + (s_ + 1) * 128, :], ost[oi], ost_b[oi], reads=[ost_b[oi]], parts=[out_b])
    end_phase()
    if debug:
        pass
    return nc


def _fm(v, k):
    return np.ascontiguousarray(np.asarray(v, np.float32).reshape(k, 128).T)


def make_in_maps(inp):
    x = inp["x"]; ctx = inp["ctx"]
    maps = []
    for c in range(NCORES):
        b, half = c // 2, c % 2
        if half == 0:
            xl = x[b, 0:LT + HALO]
            cl = ctx[b]
        else:
            xl = x[b, SEQ - LT - HALO:SEQ][::-1]
            cl = ctx[b][::-1]
        d = [half, 1 - half]
        vec = np.zeros((128, NV), np.float32)

        def put(name, arr):
            o, k = VOFF[name]
            assert arr.shape == (128, k), (name, arr.shape)
            vec[:, o:o + k] = arr

        put("g_n1", _fm(inp["g_n1"][0], 8)); put("g_n2", _fm(inp["g_n2"][0], 8)); put("g_n3", _fm(inp["g_n3"][0], 8))
        put("g_final", _fm(inp["g_final"], 8)); put("b_mod", _fm(inp["b_mod"][0], 72)); put("b_in", _fm(inp["b_in"][0], 52))
        wdw = inp["w_dw"][0]
        if half == 1:
            wdw = wdw[::-1]
        put("w_dw", np.concatenate([_fm(wdw[t], 8) for t in range(31)], axis=1))
        put("b_dw", _fm(inp["b_dw"][0], 8)); put("g_ln", _fm(inp["g_ln"][0], 8)); put("b_ln", _fm(inp["b_ln"][0], 8))
        wl = inp["w_lru_conv"][0]
        z = np.zeros((1, LRU), np.float32)
        w5 = np.concatenate([wl, z], 0) if half == 0 else np.concatenate([z, wl[::-1]], 0)
        put("w_lc", np.concatenate([_fm(w5[t], 10) for t in range(5)], axis=1))
        put("b_lc", _fm(inp["b_lru_conv"][0], 10))
        put("b_a", np.concatenate([_fm(inp["b_rec_gate"][0, dd], 10) for dd in d], axis=1))
        put("b_x", np.concatenate([_fm(inp["b_in_gate"][0, dd], 10) for dd in d], axis=1))
        put("lam", np.concatenate([_fm(inp["lru_lambda"][0, dd], 10) for dd in d], axis=1))
        c2 = np.zeros((128, 16), np.float32)
        cb = _fm(inp["c"][b], 8); cc_ = _fm(inp["c_ctx"], 8)
        c2[:, 0::2] = cb; c2[:, 1::2] = cc_
        put("c2", c2)
        fl = np.zeros((128, 2), np.float32); fl[:, 0] = half; fl[:, 1] = 1.0 - 2.0 * half
        put("flags", fl)
        pm = np.zeros((128, 8), np.float32); pm[:, c ^ 1] = 1.0
        put("pmask", pm)
        wa = inp["w_rec_gate"][0]; wx = inp["w_in_gate"][0]
        wg = np.stack([wa[d[0]], wx[d[0]], wa[d[1]], wx[d[1]]], axis=2)
        maps.append({
            "x_loc": np.ascontiguousarray(xl, np.float32), "ctx_loc": np.ascontiguousarray(cl, np.float32), "vecs": vec,
            "w_mod": inp["w_mod"][0], "w_ffn1_up": inp["w_ffn1_up"][0], "w_ffn2_up": inp["w_ffn2_up"][0],
            "w_ffn1_down": inp["w_ffn1_down"][0], "w_ffn2_down": inp["w_ffn2_down"][0], "w_in": inp["w_in"][0],
            "w_conf_out": inp["w_conf_out"][0], "w_lru_out": inp["w_lru_out"][0], "w_out": inp["w_out"][0],
            "w_gates": np.ascontiguousarray(wg, np.float32),
        })
    return maps


def kernel(**inputs):
    inp = {k: np.asarray(v) for k, v in inputs.items()}
    nc = build()
    maps = make_in_maps(inp)
    res = run_bass_kernel_spmd(nc, maps, core_ids=list(range(NCORES)))
    out = np.empty((4, SEQ, D), np.float32)
    for c in range(NCORES):
        b, half = c // 2, c % 2
        y = np.asarray(res.results[c]["y_loc"])
        if half == 0:
            out[b, 0:LT] = y
        else:
            out[b, LT:SEQ] = y[::-1]
    return out
```

```python
import numpy as np
import concourse.bass as bass
import concourse.mybir as mybir
from concourse.bass_utils import run_bass_kernel_spmd

F32 = mybir.dt.float32
BF16 = mybir.dt.bfloat16
AF = mybir.ActivationFunctionType
ALU = mybir.AluOpType

D = 1024
KC = 8
SEQ = 8192
LT = 4096
HALO = 16
CTX = 256
FF = 2816
LRU = 1280
LC = 10
NT = 512
EPS = 1e-6
NCORES = 8

_VEC_SPECS = [("g_n1", 8), ("g_n2", 8), ("g_n3", 8), ("g_final", 8), ("b_mod", 72), ("b_in", 52),
              ("w_dw", 31 * 8), ("b_dw", 8), ("g_ln", 8), ("b_ln", 8), ("w_lc", 5 * 10), ("b_lc", 10),
              ("b_a", 20), ("b_x", 20), ("lam", 20), ("c2", 16), ("flags", 2), ("pmask", 8)]
VOFF = {}
_o = 0
for _n, _k in _VEC_SPECS:
    VOFF[_n] = (_o, _k)
    _o += _k
NV = _o

BLK = {}
_b = 0
for _n, _k in [("UP1", 11), ("DN1", 8), ("IN", 14), ("CO", 2), ("LO", 4), ("WO", 2), ("UP2", 11), ("DN2", 8)]:
    BLK[_n] = _b
    _b += _k
NBLK = _b
WBE = 4096


class Buf:
    __slots__ = ("name", "w", "r", "prev", "dsem", "dcount", "excl")

    def __init__(self, name, excl=False):
        self.name = name
        self.excl = excl
        self.w = {}
        self.r = {}
        self.prev = {}
        self.dsem = None
        self.dcount = 0


class Tk:
    def __init__(self, nc):
        self.nc = nc
        self.eng = {"pe": nc.tensor, "act": nc.scalar, "dve": nc.vector, "pool": nc.gpsimd, "sp": nc.sync}
        self.sem = {k: nc.alloc_semaphore("sem_" + k) for k in self.eng}
        self.cnt = {k: 0 for k in self.eng}
        self.seen = {k: {} for k in self.eng}
        self.semobj = {}
        for k, s in self.sem.items():
            self.semobj[id(s)] = s
        self.dsems = []
        self.nwaits = 0

    def buf(self, name, excl=False):
        return Buf(name, excl)

    def _wait(self, e, deps):
        eng = self.eng[e]
        seen = self.seen[e]
        for sid, val in deps.items():
            if e == "pe" and sid == id(self.sem["pe"]):
                continue
            if seen.get(sid, 0) >= val:
                continue
            eng.wait_ge(self.semobj[sid], val)
            seen[sid] = val
            self.nwaits += 1

    @staticmethod
    def _merge(dst, src):
        for k, v in src.items():
            if dst.get(k, 0) < v:
                dst[k] = v

    def _collect(self, reads, writes, parts):
        deps = {}
        for b in reads:
            self._merge(deps, b.w)
            if b.excl:
                self._merge(deps, b.r)
        for b in writes:
            self._merge(deps, b.w)
            self._merge(deps, b.r)
        for b in parts:
            if b.r:
                self._merge(deps, b.w)
                self._merge(deps, b.r)
            else:
                self._merge(deps, b.prev)
        return deps

    def _record(self, tok, reads, writes, parts):
        sid, val = tok
        for b in reads:
            if b.r.get(sid, 0) < val:
                b.r[sid] = val
        for b in writes:
            prev = dict(b.w)
            self._merge(prev, b.r)
            b.prev = prev
            b.w = {sid: val}
            b.r = {}
        for b in parts:
            if b.r:
                prev = dict(b.w)
                self._merge(prev, b.r)
                b.prev = prev
                b.w = {sid: val}
                b.r = {}
            else:
                if b.w.get(sid, 0) < val:
                    b.w[sid] = val

    def op(self, e, fn, reads=(), writes=(), parts=()):
        deps = self._collect(reads, writes, parts)
        self._wait(e, deps)
        ins = fn(self.eng[e])
        ins.then_inc(self.sem[e], 1)
        self.cnt[e] += 1
        assert self.cnt[e] < 60000
        self._record((id(self.sem[e]), self.cnt[e]), reads, writes, parts)
        return ins

    def dma(self, e, out, in_, sb, reads=(), writes=(), parts=(), **kw):
        deps = self._collect(reads, writes, parts)
        self._wait(e, deps)
        if sb.dsem is None:
            sb.dsem = self.nc.alloc_semaphore("d_" + sb.name)
            self.semobj[id(sb.dsem)] = sb.dsem
            self.dsems.append(sb)
        ins = self.eng[e].dma_start(out=out, in_=in_, **kw)
        ins.then_inc(sb.dsem, 16)
        sb.dcount += 16
        assert sb.dcount < 60000
        self._record((id(sb.dsem), sb.dcount), reads, writes, parts)
        return ins

    def coll(self, fn, reads=(), writes=()):
        deps = self._collect(reads, writes, ())
        self._wait("pool", deps)
        if not hasattr(self, "ccsem"):
            self.ccsem = self.nc.alloc_semaphore("ccsem")
            self.semobj[id(self.ccsem)] = self.ccsem
            self.cccount = 0
        ins = fn(self.eng["pool"])
        ins.then_inc(self.ccsem)
        self.cccount += 1
        self._record((id(self.ccsem), self.cccount), reads, writes, ())
        return ins

    def wait_tokens(self, e, deps):
        self._wait(e, deps)

    def all_tokens(self):
        deps = {}
        for k in self.eng:
            if self.cnt[k] > 0:
                deps[id(self.sem[k])] = self.cnt[k]
        for sb in self.dsems:
            deps[id(sb.dsem)] = sb.dcount
        if hasattr(self, "ccsem") and self.cccount > 0:
            deps[id(self.ccsem)] = self.cccount
        return deps

    def barrier(self):
        deps = self.all_tokens()
        for k in self.eng:
            d = dict(deps)
            self._wait(k, d)


class Ring:
    def __init__(self, items):
        self.items = items
        self.i = 0

    def next(self):
        it = self.items[self.i % len(self.items)]
        self.i += 1
        return it


def build(debug=False, stop_after=None, tilesA_override=None):
    nc = bass.Bass("TRN2", target_bir_lowering=False)
    tk = Tk(nc)

    def din(name, shape, dt=F32):
        return nc.dram_tensor(name, list(shape), dt, kind="ExternalInput").ap()

    def dscr(name, shape, dt):
        return nc.dram_tensor(name, list(shape), dt).ap()

    x_d = din("x_loc", [LT + HALO, D])
    ctx_d = din("ctx_loc", [CTX, D])
    vecs_d = din("vecs", [128, NV])
    wmod_d = din("w_mod", [D, 9 * D])
    wup_d = [din("w_ffn1_up", [D, 2 * FF]), din("w_ffn2_up", [D, 2 * FF])]
    wdn_d = [din("w_ffn1_down", [FF, D]), din("w_ffn2_down", [FF, D])]
    win_d = din("w_in", [D, 6656])
    wco_d = din("w_conf_out", [D, D])
    wlo_d = din("w_lru_out", [LRU, D])
    wo_d = din("w_out", [D, D])
    wg_d = din("w_gates", [LC, 128, 4, 128])
    out_d = nc.dram_tensor("y_loc", [LT, D], F32, kind="ExternalOutput").ap()

    WB_d = dscr("WB", [NBLK, 128, WBE], BF16)
    X1_d = dscr("X1", [KC, 128, LT], F32)
    WU = 15 + LT + HALO
    U_d = dscr("U", [KC, 128, WU], BF16)
    WX = 2 + LT + HALO
    UX_d = dscr("UX", [LC, 128, WX], BF16)
    CUX_d = dscr("CUX", [LC, 128, CTX + 4], BF16)
    G_d = dscr("G", [LC, 128, LT], BF16)
    GCL_d = dscr("GCL", [16, 128, LT], BF16)
    M_d = dscr("M", [LC, 128, LT], BF16)
    C_d = dscr("C", [KC, 128, LT], F32)
    HF_d = dscr("HF", [LC, 128, LT], F32)
    cci_d = dscr("cc_in", [128, LC], F32)
    cco_d = dscr("cc_out", [NCORES * 128, LC], F32)

    wb_bufs = [tk.buf("wb%d" % i) for i in range(NBLK)]
    X1_b = tk.buf("X1"); U_b = tk.buf("U"); UX_b = tk.buf("UX"); CUX_b = tk.buf("CUX")
    G_b = tk.buf("G"); GCL_b = tk.buf("GCL"); M_b = tk.buf("M"); C_b = tk.buf("C"); HF_b = tk.buf("HF")
    cci_b = tk.buf("cci"); cco_b = tk.buf("cco"); out_b = tk.buf("out")

    import contextlib
    phase = {"st": None}

    def sb(name, shape, dt):
        if phase["st"] is None:
            return nc.alloc_sbuf_tensor("s_" + name, list(shape), dt).ap()
        return phase["st"].enter_context(nc.sbuf_tensor("s_" + name, list(shape), dt)).ap()

    def begin_phase():
        phase["st"] = contextlib.ExitStack()

    def end_phase():
        tk.barrier()
        phase["st"].close()
        phase["st"] = None

    vecs = sb("vecs", [128, NV], F32); vecs_b = tk.buf("vecs")
    ident = sb("ident", [128, 128], F32); ident_b = tk.buf("ident")
    ones_bf = sb("ones_bf", [128, 128], BF16); ones_b = tk.buf("ones")
    consts = sb("consts", [128, 8], F32); consts_b = tk.buf("consts")
    modp = sb("modp", [128, 9, KC, 2], F32); modp_b = tk.buf("modp")
    TR = sb("TR", [128, 4, 65], F32); TC = sb("TC", [128, 4, 64], F32); pos_b = tk.buf("pos")
    clam = sb("clam", [128, 2, 20], F32); clam_b = tk.buf("clam")
    s0 = sb("s0", [128, LC], F32); s0_b = tk.buf("s0")
    sfin = sb("sfin", [128, LC], F32); sfin_b = tk.buf("sfin")
    carry = sb("carry", [128, LC], F32); carry_b = tk.buf("carry")

    psum = [nc.alloc_psum_tensor("ps%d" % i, [128, 512], F32).ap() for i in range(8)]
    ps_b = [tk.buf("ps%d" % i, excl=True) for i in range(8)]
    psring = Ring(list(range(8)))

    def vcol(name, idx=0, n=1):
        o, k = VOFF[name]
        return vecs[:, o + idx:o + idx + n]

    def finish_early(items):
        tk.barrier()
        for nm, ap_ in items:
            o = nc.dram_tensor("dbg_" + nm, list(ap_.shape), ap_.dtype, kind="ExternalOutput").ap()
            b_ = tk.buf("dbg_" + nm)
            tk.dma("sp", o, ap_, b_, writes=[b_])
        z = nc.alloc_sbuf_tensor("s_zout", [128, D], F32).ap(); z_b = tk.buf("zout")
        tk.op("dve", lambda e: e.memset(z, 0.0), writes=[z_b])
        for i in range(LT // 128):
            tk.dma("sp", out_d[i * 128:(i + 1) * 128, :], z, z_b, reads=[z_b], parts=[out_b])
        tk.barrier()
        return nc

    tk.dma("sp", vecs, vecs_d, vecs_b, writes=[vecs_b])
    wring_t = [None] * 5
    wring_b = [tk.buf("wr%d" % i) for i in range(5)]

    def alloc_ring(tag):
        for i in range(5):
            wring_t[i] = sb("wr%s%d" % (tag, i), [128, WBE], BF16)

    small = sb("small", [128, 4, 512], F32)
    rs_b = tk.buf("rs_sb")
    tmpg = [sb("tmpg%d" % i, [128, 2, 512], F32) for i in range(2)]
    tmpg_b = [tk.buf("tmpg%d" % i) for i in range(2)]
    tmpring = Ring([0, 1])
    zt = sb("zt", [128, LC, 16], BF16); zt_b = tk.buf("zt")

    begin_phase()
    it = sb("iota_t", [128, 128], F32)
    it_b = tk.buf("iota_t")
    tk.op("pool", lambda e: e.iota(it, pattern=[[1, 128]], base=0, channel_multiplier=-1,
                                   allow_small_or_imprecise_dtypes=True), writes=[it_b])
    tk.op("dve", lambda e: e.tensor_single_scalar(out=ident, in_=it, scalar=0.0, op=ALU.is_equal),
          reads=[it_b], writes=[ident_b])
    tk.op("dve", lambda e: e.memset(ones_bf, 1.0), writes=[ones_b])
    tk.op("dve", lambda e: e.memset(consts[:, 0:1], EPS), parts=[consts_b])
    tk.op("dve", lambda e: e.memset(consts[:, 1:2], 1.0), parts=[consts_b])
    tk.op("dve", lambda e: e.memset(consts[:, 2:3], float(np.pi / 2)), parts=[consts_b])
    tk.op("dve", lambda e: e.memset(consts[:, 3:4], 0.0), parts=[consts_b])
    tk.op("dve", lambda e: e.memset(zt, 0.0), writes=[zt_b])

    def build_pos():
        om = sb("om", [128, 2], F32); om_b = tk.buf("om")
        pi_ = sb("pi_", [128, 2], F32); pi_b = tk.buf("pi_")
        Gs = sb("Gs", [128, 2, 128], F32); Gc = sb("Gc", [128, 2, 128], F32); G_bb = tk.buf("Gtab")
        sc = sb("sc_pos", [128, 2, 2], F32); sc_b = tk.buf("sc_pos")
        t1 = sb("t1_pos", [128, 128], F32); t1_b = tk.buf("t1_pos")
        sm = sb("sm_pos", [128, 16], F32); sm_b = tk.buf("sm_pos")
        tk.op("pool", lambda e: e.iota(pi_, pattern=[[128, 2]], base=0, channel_multiplier=1,
                                       allow_small_or_imprecise_dtypes=True), writes=[pi_b])
        tk.op("act", lambda e: e.activation(out=om, in_=pi_, func=AF.Exp, scale=float(-np.log(10000.0) / 256.0)),
              reads=[pi_b], writes=[om_b])
        tk.op("dve", lambda e: e.memset(Gs[:, :, 0:1], 0.0), parts=[G_bb])
        tk.op("dve", lambda e: e.memset(Gc[:, :, 0:1], 1.0), parts=[G_bb])
        tk.op("act", lambda e: e.activation(out=sc[:, :, 0], in_=om, func=AF.Sin, scale=1.0),
              reads=[om_b], parts=[sc_b])
        tk.op("act", lambda e: e.activation(out=sc[:, :, 1], in_=om, func=AF.Sin, scale=1.0, bias=consts[:, 2:3]),
              reads=[om_b, consts_b], parts=[sc_b])
        for b in range(7):
            w = 1 << b
            for j in range(2):
                sbv = sc[:, j, 0:1]; cbv = sc[:, j, 1:2]
                tk.op("dve", lambda e: e.tensor_scalar(out=t1[:, 0:w], in0=Gc[:, j, 0:w], scalar1=sbv, scalar2=None, op0=ALU.mult),
                      reads=[G_bb, sc_b], writes=[t1_b])
                tk.op("dve", lambda e: e.scalar_tensor_tensor(out=Gs[:, j, w:2 * w], in0=Gs[:, j, 0:w], scalar=cbv, in1=t1[:, 0:w],
                                                               op0=ALU.mult, op1=ALU.add),
                      reads=[G_bb, sc_b, t1_b], writes=[G_bb])
                tk.op("dve", lambda e: e.tensor_scalar(out=t1[:, 0:w], in0=Gs[:, j, 0:w], scalar1=sbv, scalar2=None, op0=ALU.mult),
                      reads=[G_bb, sc_b], writes=[t1_b])
                tk.op("dve", lambda e: e.scalar_tensor_tensor(out=Gc[:, j, w:2 * w], in0=Gc[:, j, 0:w], scalar=cbv, in1=t1[:, 0:w],
                                                               op0=ALU.mult, op1=ALU.subtract),
                      reads=[G_bb, sc_b, t1_b], writes=[G_bb])
            if b < 6:
                tk.op("dve", lambda e: e.tensor_tensor(out=sm[:, 0:2], in0=sc[:, :, 0], in1=sc[:, :, 0], op=ALU.mult),
                      reads=[sc_b], writes=[sm_b])
                tk.op("dve", lambda e: e.tensor_tensor(out=sm[:, 2:4], in0=sc[:, :, 0], in1=sc[:, :, 1], op=ALU.mult),
                      reads=[sc_b, sm_b], writes=[sm_b])
                tk.op("dve", lambda e: e.tensor_scalar(out=sc[:, :, 1], in0=sm[:, 0:2], scalar1=-2.0, scalar2=1.0,
                                                        op0=ALU.mult, op1=ALU.add), reads=[sm_b], writes=[sc_b])
                tk.op("dve", lambda e: e.tensor_scalar(out=sc[:, :, 0], in0=sm[:, 2:4], scalar1=2.0, scalar2=None,
                                                        op0=ALU.mult), reads=[sm_b, sc_b], writes=[sc_b])
        fo = VOFF["flags"][0]
        mfl = vecs[:, fo:fo + 1]; sfl = vecs[:, fo + 1:fo + 2]
        for (tab, r0max, n) in ((TR, 127, 65), (TC, 63, 64)):
            for j in range(2):
                S0 = sm[:, 4:5]; C0 = sm[:, 5:6]; sS0 = sm[:, 6:7]; sC0 = sm[:, 7:8]; om1 = sm[:, 8:9]
                tk.op("dve", lambda e: e.tensor_tensor(out=S0, in0=Gs[:, j, r0max:r0max + 1], in1=mfl, op=ALU.mult),
                      reads=[G_bb, vecs_b, sm_b], writes=[sm_b])
                tk.op("dve", lambda e: e.tensor_scalar(out=om1, in0=mfl, scalar1=-1.0, scalar2=1.0, op0=ALU.mult, op1=ALU.add),
                      reads=[vecs_b, sm_b], writes=[sm_b])
                tk.op("dve", lambda e: e.scalar_tensor_tensor(out=C0, in0=Gc[:, j, r0max:r0max + 1], scalar=mfl, in1=om1,
                                                               op0=ALU.mult, op1=ALU.add), reads=[G_bb, vecs_b, sm_b], writes=[sm_b])
                tk.op("dve", lambda e: e.tensor_tensor(out=sS0, in0=S0, in1=sfl, op=ALU.mult), reads=[sm_b, vecs_b], writes=[sm_b])
                tk.op("dve", lambda e: e.tensor_tensor(out=sC0, in0=C0, in1=sfl, op=ALU.mult), reads=[sm_b, vecs_b], writes=[sm_b])
                tk.op("dve", lambda e: e.tensor_scalar(out=t1[:, 0:n], in0=Gs[:, j, 0:n], scalar1=sC0, scalar2=None, op0=ALU.mult),
                      reads=[G_bb, sm_b], writes=[t1_b])
                tk.op("dve", lambda e: e.scalar_tensor_tensor(out=tab[:, j, 0:n], in0=Gc[:, j, 0:n], scalar=S0, in1=t1[:, 0:n],
                                                               op0=ALU.mult, op1=ALU.add), reads=[G_bb, sm_b, t1_b], writes=[pos_b])
                tk.op("dve", lambda e: e.tensor_scalar(out=t1[:, 0:n], in0=Gs[:, j, 0:n], scalar1=sS0, scalar2=None, op0=ALU.mult),
                      reads=[G_bb, sm_b], writes=[t1_b])
                tk.op("dve", lambda e: e.scalar_tensor_tensor(out=tab[:, 2 + j, 0:n], in0=Gc[:, j, 0:n], scalar=C0, in1=t1[:, 0:n],
                                                               op0=ALU.mult, op1=ALU.subtract), reads=[G_bb, sm_b, t1_b], writes=[pos_b])

    build_pos()

    def build_clam():
        e_ = sb("lam_e", [128, 20], F32); t_ = sb("lam_t", [128, 20], F32); lb = tk.buf("lamtmp")
        lo = VOFF["lam"][0]
        tk.op("act", lambda e: e.activation(out=e_, in_=vecs[:, lo:lo + 20], func=AF.Exp, scale=-1.0), reads=[vecs_b], writes=[lb])
        tk.op("dve", lambda e: e.tensor_scalar(out=t_, in0=e_, scalar1=-0.25, scalar2=1.0 / 3.0, op0=ALU.mult, op1=ALU.add), reads=[lb], writes=[lb])
        tk.op("dve", lambda e: e.tensor_tensor(out=t_, in0=t_, in1=e_, op=ALU.mult), reads=[lb], writes=[lb])
        tk.op("dve", lambda e: e.tensor_scalar(out=t_, in0=t_, scalar1=-0.5, scalar2=None, op0=ALU.add), reads=[lb], writes=[lb])
        tk.op("dve", lambda e: e.tensor_tensor(out=t_, in0=t_, in1=e_, op=ALU.mult), reads=[lb], writes=[lb])
        tk.op("dve", lambda e: e.tensor_scalar(out=t_, in0=t_, scalar1=1.0, scalar2=None, op0=ALU.add), reads=[lb], writes=[lb])
        tk.op("dve", lambda e: e.tensor_tensor(out=t_, in0=t_, in1=e_, op=ALU.mult), reads=[lb], writes=[lb])
        tk.op("dve", lambda e: e.tensor_scalar(out=clam[:, 0, :], in0=t_, scalar1=-8.0, scalar2=None, op0=ALU.mult), reads=[lb], parts=[clam_b])
        tk.op("dve", lambda e: e.tensor_scalar(out=clam[:, 1, :], in0=t_, scalar1=-16.0, scalar2=None, op0=ALU.mult), reads=[lb], parts=[clam_b])

    build_clam()

    def build_mod():
        scv = sb("silu_c", [128, 16], F32); scv_b = tk.buf("silu_c")
        wst = [sb("wmst%d" % i, [128, KC, 512], F32) for i in range(2)]
        wst_b = [tk.buf("wmst%d" % i) for i in range(2)]
        modfm = sb("modfm", [128, 72, 2], F32); modfm_b = tk.buf("modfm")
        co = VOFF["c2"][0]
        tk.op("act", lambda e: e.activation(out=scv, in_=vecs[:, co:co + 16], func=AF.Silu), reads=[vecs_b], writes=[scv_b])
        wm_v = wmod_d.rearrange("(kc p) o -> p kc o", p=128)
        pb = psring.next()
        first = True
        for nb in range(18):
            i = nb % 2
            tk.dma("sp", wst[i], wm_v[:, :, nb * 512:(nb + 1) * 512], wst_b[i], writes=[wst_b[i]])
            for mi in range(4):
                m = nb * 4 + mi
                for kc in range(KC):
                    tk.op("pe", lambda e: e.matmul(psum[pb][:, 2 * m:2 * m + 2], lhsT=wst[i][:, kc, mi * 128:(mi + 1) * 128],
                                                   rhs=scv[:, 2 * kc:2 * kc + 2], start=(kc == 0), stop=(kc == KC - 1)),
                          reads=[scv_b, wst_b[i]], **({"writes": [ps_b[pb]]} if first else {"parts": [ps_b[pb]]}))
                    first = False
        bo = VOFF["b_mod"][0]
        tk.op("dve", lambda e: e.tensor_tensor(out=modfm, in0=psum[pb][:, 0:144].rearrange("p (j t) -> p j t", t=2),
                                               in1=vecs[:, bo:bo + 72].unsqueeze(2).to_broadcast([128, 72, 2]), op=ALU.add),
              reads=[ps_b[pb], vecs_b], writes=[modfm_b])
        for n_i, (gname, i_sh, i_sc, i_gate, gscale) in enumerate((("g_n1", 0, 1, 2, 0.5), ("g_n2", 3, 4, 5, 1.0), ("g_n3", 6, 7, 8, 0.5))):
            go = VOFF[gname][0]
            base = 3 * n_i
            tk.op("dve", lambda e: e.tensor_scalar(out=modp[:, base, :, :], in0=modfm[:, i_sc * 8:(i_sc + 1) * 8, :], scalar1=1.0, scalar2=None,
                                                    op0=ALU.add), reads=[modfm_b], parts=[modp_b])
            tk.op("dve", lambda e: e.tensor_tensor(out=modp[:, base, :, :], in0=modp[:, base, :, :],
                                                    in1=vecs[:, go:go + 8].unsqueeze(2).to_broadcast([128, 8, 2]), op=ALU.mult),
                  reads=[modp_b, vecs_b], writes=[modp_b])
            tk.op("dve", lambda e: e.tensor_copy(out=modp[:, base + 1, :, :], in_=modfm[:, i_sh * 8:(i_sh + 1) * 8, :]),
                  reads=[modfm_b], parts=[modp_b])
            tk.op("dve", lambda e: e.tensor_scalar(out=modp[:, base + 2, :, :], in0=modfm[:, i_gate * 8:(i_gate + 1) * 8, :], scalar1=gscale,
                                                    scalar2=None, op0=ALU.mult), reads=[modfm_b], parts=[modp_b])

    if stop_after == "pos":
        return finish_early([("TR", TR), ("TC", TC), ("clam", clam)])
    build_mod()
    if stop_after == "mod":
        return finish_early([("TR", TR), ("TC", TC), ("clam", clam), ("modp", modp)])
    end_phase()
    begin_phase()

    def convert_weights():
        st32 = [sb("cv32_%d" % i, [128, WBE], F32) for i in range(3)]
        st16 = [sb("cv16_%d" % i, [128, WBE], BF16) for i in range(3)]
        st32_b = [tk.buf("cv32_%d" % i) for i in range(3)]
        st16_b = [tk.buf("cv16_%d" % i) for i in range(3)]
        state = {"i": 0}

        def job(blk, srcs, nel):
            i = state["i"] % 3
            eng = ("dve", "pool", "act")[state["i"] % 3]
            state["i"] += 1
            for k, (dv, src) in enumerate(srcs):
                tk.dma("sp", dv(st32[i]), src, st32_b[i], **({"writes": [st32_b[i]]} if k == 0 else {"parts": [st32_b[i]]}))
            if eng == "act":
                tk.op("act", lambda e: e.activation(out=st16[i][:, 0:nel], in_=st32[i][:, 0:nel], func=AF.Copy),
                      reads=[st32_b[i]], writes=[st16_b[i]])
            else:
                tk.op(eng, lambda e: e.tensor_copy(out=st16[i][:, 0:nel], in_=st32[i][:, 0:nel]),
                      reads=[st32_b[i]], writes=[st16_b[i]])
            tk.dma("sp", WB_d[blk, :, 0:nel], st16[i][:, 0:nel], st16_b[i], reads=[st16_b[i]], writes=[wb_bufs[blk]])

        def v3(kc, oc, c0=0, cw=None):
            cw = oc if cw is None else cw
            return lambda st: st[:, 0:kc * oc].rearrange("p (k o) -> p k o", o=oc)[:, :, c0:c0 + cw]

        for f in range(2):
            wv = wup_d[f].rearrange("(kc p) o -> p kc o", p=128)
            for j in range(11):
                job(BLK["UP%d" % (f + 1)] + j,
                    [(v3(KC, 512, 0, 256), wv[:, :, j * 256:(j + 1) * 256]),
                     (v3(KC, 512, 256, 256), wv[:, :, FF + j * 256:FF + (j + 1) * 256])], KC * 512)
            dv = wdn_d[f].rearrange("(kc p) o -> p kc o", p=128)
            for m in range(8):
                job(BLK["DN%d" % (f + 1)] + m, [(v3(22, 128), dv[:, :, m * 128:(m + 1) * 128])], 22 * 128)
            if f == 0:
                iv = win_d.rearrange("(kc p) o -> p kc o", p=128)
                for j in range(4):
                    job(BLK["IN"] + j, [(v3(KC, 512, 0, 256), iv[:, :, j * 256:(j + 1) * 256]),
                                        (v3(KC, 512, 256, 256), iv[:, :, D + j * 256:D + (j + 1) * 256])], KC * 512)
                for j in range(3):
                    cw = 512 if j < 2 else 256
                    job(BLK["IN"] + 4 + j, [(v3(KC, cw), iv[:, :, 2048 + j * 512:2048 + j * 512 + cw])], KC * cw)
                for j in range(3):
                    cw = 512 if j < 2 else 256
                    job(BLK["IN"] + 7 + j, [(v3(KC, cw), iv[:, :, 3328 + j * 512:3328 + j * 512 + cw])], KC * cw)
                for j in range(4):
                    job(BLK["IN"] + 10 + j, [(v3(KC, 512), iv[:, :, 4608 + j * 512:4608 + (j + 1) * 512])], KC * 512)
                cv = wco_d.rearrange("(kc p) o -> p kc o", p=128)
                for j in range(2):
                    job(BLK["CO"] + j, [(v3(KC, 512), cv[:, :, j * 512:(j + 1) * 512])], KC * 512)
                lv = wlo_d.rearrange("(kc p) o -> p kc o", p=128)
                for j in range(4):
                    job(BLK["LO"] + j, [(v3(LC, 256), lv[:, :, j * 256:(j + 1) * 256])], LC * 256)
                ov = wo_d.rearrange("(kc p) o -> p kc o", p=128)
                for j in range(2):
                    job(BLK["WO"] + j, [(v3(KC, 512), ov[:, :, j * 512:(j + 1) * 512])], KC * 512)

    convert_weights()
    end_phase()
    if stop_after == "conv":
        return finish_early([("TR", TR), ("TC", TC), ("clam", clam), ("modp", modp), ("WB", WB_d)])

    tk.dma("sp", U_d[:, :, 0:15].rearrange("k p n -> p k n"), zt[:, 0:KC, 0:15], zt_b, reads=[zt_b], parts=[U_b])
    tk.dma("sp", UX_d[:, :, 0:2].rearrange("k p n -> p k n"), zt[:, :, 0:2], zt_b, reads=[zt_b], parts=[UX_b])
    tk.dma("sp", CUX_d[:, :, 0:2].rearrange("k p n -> p k n"), zt[:, :, 0:2], zt_b, reads=[zt_b], parts=[CUX_b])
    tk.dma("sp", CUX_d[:, :, CTX + 2:CTX + 4].rearrange("k p n -> p k n"), zt[:, :, 0:2], zt_b, reads=[zt_b], parts=[CUX_b])

    class WStream:
        def __init__(self, seq, depth=4):
            self.seq = seq
            self.depth = depth
            self.issued = 0
            self.used = 0
            self.slot_of = {}

        def _issue(self):
            blk, nel = self.seq[self.issued]
            s = self.issued % 5
            tk.dma("sp", wring_t[s][:, 0:nel], WB_d[blk, :, 0:nel], wring_b[s], reads=[wb_bufs[blk]], writes=[wring_b[s]])
            self.issued += 1

        def next(self, blk):
            assert self.seq[self.used][0] == blk, (self.seq[self.used], blk)
            while self.issued < len(self.seq) and self.issued <= self.used + self.depth - 1:
                self._issue()
            s = self.used % 5
            self.used += 1
            return wring_t[s], wring_b[s]

    def rms_to_h(xf, xf_b, h, h_b, sq, sq_b, N, gm, sh):
        tk.op("act", lambda e: e.activation(out=sq[:, :, 0:N], in_=xf[:, :, 0:N], func=AF.Square), reads=[xf_b], writes=[sq_b])
        pb = psring.next()
        for kc in range(KC):
            tk.op("pe", lambda e: e.matmul(psum[pb][:, 0:N], lhsT=ones_bf, rhs=sq[:, kc, 0:N], start=(kc == 0), stop=(kc == KC - 1)),
                  reads=[ones_b, sq_b], **({"writes": [ps_b[pb]]} if kc == 0 else {"parts": [ps_b[pb]]}))
        tk.op("act", lambda e: e.activation(out=small[:, 0, 0:N], in_=psum[pb][:, 0:N], func=AF.Sqrt, scale=1.0 / D, bias=consts[:, 0:1]),
              reads=[ps_b[pb], consts_b], writes=[rs_b])
        pr = psring.next()
        tk.op("dve", lambda e: e.reciprocal(out=psum[pr][:, 0:N], in_=small[:, 0, 0:N]), reads=[rs_b], writes=[ps_b[pr]])
        for kc in range(KC):
            ti = tmpring.next()
            tk.op("dve", lambda e: e.tensor_tensor(out=tmpg[ti][:, 0, 0:N], in0=xf[:, kc, 0:N], in1=psum[pr][:, 0:N], op=ALU.mult),
                  reads=[xf_b, ps_b[pr]], writes=[tmpg_b[ti]])
            tk.op("act", lambda e: e.activation(out=h[:, kc, 0:N], in_=tmpg[ti][:, 0, 0:N], func=AF.Identity,
                                                scale=gm[:, kc:kc + 1], bias=sh[:, kc:kc + 1]),
                  reads=[tmpg_b[ti], modp_b], parts=[h_b])
        return pr

    def ffn(ws, f, xf, xf_b, h, h_b, hid, hid_b, N, gate):
        for j in range(11):
            wt, wb = ws.next(BLK["UP%d" % (f + 1)] + j)
            w3 = wt[:, 0:KC * 512].rearrange("p (k o) -> p k o", o=512)
            pbs = [psring.next() for _ in range(4)]
            for mi in range(4):
                pb = pbs[mi]
                for kc in range(KC):
                    tk.op("pe", lambda e: e.matmul(psum[pb][:, 0:N], lhsT=w3[:, kc, mi * 128:(mi + 1) * 128], rhs=h[:, kc, 0:N],
                                                   start=(kc == 0), stop=(kc == KC - 1)),
                          reads=[wb, h_b], **({"writes": [ps_b[pb]]} if kc == 0 else {"parts": [ps_b[pb]]}))
            ti = tmpring.next()
            for i in range(2):
                tk.op("act", lambda e: e.activation(out=tmpg[ti][:, i, 0:N], in_=psum[pbs[i]][:, 0:N], func=AF.Silu),
                      reads=[ps_b[pbs[i]]], **({"writes": [tmpg_b[ti]]} if i == 0 else {"parts": [tmpg_b[ti]]}))
            for i in range(2):
                tk.op("dve", lambda e: e.tensor_tensor(out=hid[:, 2 * j + i, 0:N], in0=psum[pbs[2 + i]][:, 0:N], in1=tmpg[ti][:, i, 0:N], op=ALU.mult),
                      reads=[ps_b[pbs[2 + i]], tmpg_b[ti]], parts=[hid_b])
        for m in range(8):
            wt, wb = ws.next(BLK["DN%d" % (f + 1)] + m)
            w3 = wt[:, 0:22 * 128].rearrange("p (k o) -> p k o", o=128)
            pb = psring.next()
            for kc in range(22):
                tk.op("pe", lambda e: e.matmul(psum[pb][:, 0:N], lhsT=w3[:, kc, :], rhs=hid[:, kc, 0:N], start=(kc == 0), stop=(kc == 21)),
                      reads=[wb, hid_b], **({"writes": [ps_b[pb]]} if kc == 0 else {"parts": [ps_b[pb]]}))
            tk.op("dve", lambda e: e.scalar_tensor_tensor(out=xf[:, m, 0:N], in0=psum[pb][:, 0:N], scalar=gate[:, m:m + 1], in1=xf[:, m, 0:N],
                                                           op0=ALU.mult, op1=ALU.add),
                  reads=[ps_b[pb], modp_b, xf_b], writes=[xf_b])

    def up_seq(f):
        return [(BLK["UP%d" % (f + 1)] + j, KC * 512) for j in range(11)] + [(BLK["DN%d" % (f + 1)] + m, 22 * 128) for m in range(8)]

    IN_NEL = [KC * 512] * 4 + [KC * 512, KC * 512, KC * 256] * 2 + [KC * 512] * 4

    begin_phase()
    alloc_ring("A")
    xin = [sb("xin%d" % i, [128, D], F32) for i in range(2)]
    xin_b = [tk.buf("xin%d" % i) for i in range(2)]
    xinring = Ring([0, 1])
    xfm = [sb("xfm%d" % i, [128, KC, NT], F32) for i in range(2)]
    xfm_b = [tk.buf("xfm%d" % i) for i in range(2)]
    hA = sb("hA", [128, KC, NT], BF16); hA_b = tk.buf("hA")
    sqA = sb("sqA", [128, KC, NT], BF16); sqA_b = tk.buf("sqA")
    hidA = sb("hidA", [128, 22, NT], BF16); hidA_b = tk.buf("hidA")
    ust = sb("ust", [128, KC, NT], BF16); ust_b = tk.buf("ust")
    uxst = sb("uxst", [128, LC, NT], BF16); uxst_b = tk.buf("uxst")
    gst = sb("gst", [128, LC, NT], BF16); gst_b = tk.buf("gst")
    gclst = sb("gclst", [128, 16, NT], BF16); gclst_b = tk.buf("gclst")

    tilesA = [("ctx", 0, CTX)] + [("lat", i * NT, NT) for i in range(LT // NT)] + [("halo", LT, HALO)]
    if tilesA_override is not None:
        tilesA = tilesA_override
    seqA = []
    for kind, t0, N in tilesA:
        seqA += up_seq(0)
        if kind == "lat":
            seqA += [(BLK["IN"] + j, IN_NEL[j]) for j in range(14)]
        elif kind == "halo":
            seqA += [(BLK["IN"] + j, IN_NEL[j]) for j in range(7)]
        else:
            seqA += [(BLK["IN"] + j, IN_NEL[j]) for j in range(4, 7)]
    wsA = WStream(seqA)
    bin_o = VOFF["b_in"][0]

    for ti_, (kind, t0, N) in enumerate(tilesA):
        mj = 1 if kind == "ctx" else 0
        xf = xfm[ti_ % 2]; xf_b = xfm_b[ti_ % 2]
        src = ctx_d if kind == "ctx" else x_d
        nsub = max(1, N // 128)
        sw = min(N, 128)
        for s in range(nsub):
            xi = xinring.next()
            tk.dma("sp", xin[xi][0:sw, :], src[t0 + s * 128:t0 + s * 128 + sw, :], xin_b[xi], writes=[xin_b[xi]])
            for half in range(2):
                pb = psring.next()
                for q in range(4):
                    kc = half * 4 + q
                    tk.op("pe", lambda e: e.transpose(out=psum[pb][:, q * 128:q * 128 + sw], in_=xin[xi][0:sw, kc * 128:(kc + 1) * 128],
                                                      identity=ident[0:sw, 0:sw]),
                          reads=[xin_b[xi], ident_b], **({"writes": [ps_b[pb]]} if q == 0 else {"parts": [ps_b[pb]]}))
                pv = psum[pb][:, :].rearrange("p (q n) -> p q n", n=128)[:, :, 0:sw]
                ov = xf[:, half * 4:half * 4 + 4, s * 128:s * 128 + sw]
                if kind == "ctx":
                    tk.op("dve", lambda e: e.tensor_copy(out=ov, in_=pv), reads=[ps_b[pb]], parts=[xf_b])
                else:
                    tl = t0 + s * 128
                    a0 = tl // 64
                    nr = max(1, sw // 64)
                    ncol = min(sw, 64)
                    if half == 0:
                        posv = TR[:, :, a0:a0 + nr].unsqueeze(3).to_broadcast([128, 4, nr, ncol])
                    else:
                        posv = TC[:, :, 0:ncol].unsqueeze(2).to_broadcast([128, 4, nr, ncol])
                    tk.op("dve", lambda e: e.tensor_tensor(out=ov.rearrange("p q (r c) -> p q r c", c=ncol),
                                                           in0=pv.rearrange("p q (r c) -> p q r c", c=ncol), in1=posv, op=ALU.add),
                          reads=[ps_b[pb], pos_b], parts=[xf_b])
        rms_to_h(xf, xf_b, hA, hA_b, sqA, sqA_b, N, modp[:, 0, :, mj], modp[:, 1, :, mj])
        ffn(wsA, 0, xf, xf_b, hA, hA_b, hidA, hidA_b, N, modp[:, 2, :, mj])
        if kind == "lat":
            tk.dma("sp", X1_d[:, :, t0:t0 + N].rearrange("k p n -> p k n"), xf[:, :, 0:N], xf_b, reads=[xf_b], parts=[X1_b])
        rms_to_h(xf, xf_b, hA, hA_b, sqA, sqA_b, N, modp[:, 3, :, mj], modp[:, 4, :, mj])
        blocks = list(range(14)) if kind == "lat" else (list(range(7)) if kind == "halo" else [4, 5, 6])
        for j in blocks:
            wt, wb = wsA.next(BLK["IN"] + j)
            ncols = IN_NEL[j] // KC
            nch = ncols // 128
            w3 = wt[:, 0:IN_NEL[j]].rearrange("p (k o) -> p k o", o=ncols)
            pbs = [psring.next() for _ in range(nch)]
            for mi in range(nch):
                pb = pbs[mi]
                for kc in range(KC):
                    tk.op("pe", lambda e: e.matmul(psum[pb][:, 0:N], lhsT=w3[:, kc, mi * 128:(mi + 1) * 128], rhs=hA[:, kc, 0:N],
                                                   start=(kc == 0), stop=(kc == KC - 1)),
                          reads=[wb, hA_b], **({"writes": [ps_b[pb]]} if kc == 0 else {"parts": [ps_b[pb]]}))
            if j < 4:
                ti = tmpring.next()
                for i in range(2):
                    bc = bin_o + 8 + 2 * j + i
                    tk.op("act", lambda e: e.activation(out=tmpg[ti][:, i, 0:N], in_=psum[pbs[2 + i]][:, 0:N], func=AF.Sigmoid,
                                                        bias=vecs[:, bc:bc + 1], scale=1.0),
                          reads=[ps_b[pbs[2 + i]], vecs_b], **({"writes": [tmpg_b[ti]]} if i == 0 else {"parts": [tmpg_b[ti]]}))
                for i in range(2):
                    bc = bin_o + 2 * j + i
                    tk.op("dve", lambda e: e.scalar_tensor_tensor(out=ust[:, 2 * j + i, 0:N], in0=psum[pbs[i]][:, 0:N], scalar=vecs[:, bc:bc + 1],
                                                                   in1=tmpg[ti][:, i, 0:N], op0=ALU.add, op1=ALU.mult),
                          reads=[ps_b[pbs[i]], vecs_b, tmpg_b[ti]], parts=[ust_b])
            elif j < 7:
                for mi in range(nch):
                    cc = (j - 4) * 4 + mi
                    bc = bin_o + 16 + cc
                    tk.op("dve", lambda e: e.tensor_scalar(out=uxst[:, cc, 0:N], in0=psum[pbs[mi]][:, 0:N], scalar1=vecs[:, bc:bc + 1],
                                                           scalar2=None, op0=ALU.add),
                          reads=[ps_b[pbs[mi]], vecs_b], parts=[uxst_b])
            elif j < 10:
                for mi in range(nch):
                    cc = (j - 7) * 4 + mi
                    bc = bin_o + 26 + cc
                    tk.op("act", lambda e: e.activation(out=gst[:, cc, 0:N], in_=psum[pbs[mi]][:, 0:N], func=AF.Gelu_apprx_tanh,
                                                        bias=vecs[:, bc:bc + 1], scale=1.0),
                          reads=[ps_b[pbs[mi]], vecs_b], parts=[gst_b])
            else:
                for mi in range(nch):
                    c16 = (j - 10) * 4 + mi
                    bc = bin_o + 36 + c16
                    tk.op("act", lambda e: e.activation(out=gclst[:, c16, 0:N], in_=psum[pbs[mi]][:, 0:N], func=AF.Sigmoid,
                                                        bias=vecs[:, bc:bc + 1], scale=1.0),
                          reads=[ps_b[pbs[mi]], vecs_b], parts=[gclst_b])
        if kind == "ctx":
            tk.dma("sp", CUX_d[:, :, 2:2 + N].rearrange("k p n -> p k n"), uxst[:, :, 0:N], uxst_b, reads=[uxst_b], parts=[CUX_b])
        else:
            tk.dma("sp", U_d[:, :, 15 + t0:15 + t0 + N].rearrange("k p n -> p k n"), ust[:, :, 0:N], ust_b, reads=[ust_b], parts=[U_b])
            tk.dma("sp", UX_d[:, :, 2 + t0:2 + t0 + N].rearrange("k p n -> p k n"), uxst[:, :, 0:N], uxst_b, reads=[uxst_b], parts=[UX_b])
            if kind == "lat":
                tk.dma("sp", G_d[:, :, t0:t0 + N].rearrange("k p n -> p k n"), gst[:, :, 0:N], gst_b, reads=[gst_b], parts=[G_b])
                tk.dma("sp", GCL_d[:, :, t0:t0 + N].rearrange("k p n -> p k n"), gclst[:, :, 0:N], gclst_b, reads=[gclst_b], parts=[GCL_b])

    end_phase()

    dbg = {}
    if debug:
        for nm, ap_ in (("X1", X1_d), ("U", U_d), ("UX", UX_d), ("CUX", CUX_d), ("G", G_d), ("GCL", GCL_d)):
            o = nc.dram_tensor("dbg_" + nm, list(ap_.shape), ap_.dtype, kind="ExternalOutput").ap()
            dbg[nm] = o
            b_ = tk.buf("dbg_" + nm)
            tk.dma("sp", o, ap_, b_, writes=[b_])
        for nm, ap_ in (("modp", modp), ("TR", TR), ("TC", TC), ("clam", clam)):
            o = nc.dram_tensor("dbg_" + nm, list(ap_.shape), ap_.dtype, kind="ExternalOutput").ap()
            b_ = tk.buf("dbg_" + nm)
            tk.dma("sp", o, ap_, b_, writes=[b_])
    if stop_after == "A":
        z = sb("zout", [128, D], F32); z_b = tk.buf("zout")
        tk.op("dve", lambda e: e.memset(z, 0.0), writes=[z_b])
        for i in range(LT // 128):
            tk.dma("sp", out_d[i * 128:(i + 1) * 128, :], z, z_b, reads=[z_b], parts=[out_b])
        tk.barrier()
        return nc


    begin_phase()
    uxb = sb("uxb", [128, WX], BF16); uxb_b = tk.buf("uxb")
    xr32 = sb("xr32", [128, LT], F32); xr32_b = tk.buf("xr32")
    xrb = sb("xrb", [128, LT], BF16); xrb_b = tk.buf("xrb")
    Rt = sb("Rt", [128, LT], F32); Rt_b = tk.buf("Rt")
    A2t = sb("A2t", [128, LT], F32); A2t_b = tk.buf("A2t")
    IGt = sb("IGt", [128, LT], F32); IGt_b = tk.buf("IGt")
    Ht = sb("Ht", [128, LT], F32); Ht_b = tk.buf("Ht")
    gw32 = sb("gw32", [128, 4, 128], F32); gw32_b = tk.buf("gw32")
    gwb = sb("gwb", [128, LC, 4, 128], BF16); gwb_b = tk.buf("gwb")
    dgl = sb("dgl", [128, 5, 128], BF16); dgl_b = tk.buf("dgl")
    dgc = sb("dgc", [128, 31, 128], BF16); dgc_b = tk.buf("dgc")
    ub = sb("ub", [128, WU], BF16); ub_b = tk.buf("ub")
    cst = sb("cst", [128, LT], F32); cst_b = tk.buf("cst")
    cux = sb("cux", [128, CTX + 4], BF16); cux_b = tk.buf("cux")
    gath = sb("gath", [128, NCORES, LC], F32); gath_b = tk.buf("gath")
    wlc_o = VOFF["w_lc"][0]; blc_o = VOFF["b_lc"][0]; ba_o = VOFF["b_a"][0]; bx_o = VOFF["b_x"][0]
    wdw_o = VOFF["w_dw"][0]; bdw_o = VOFF["b_dw"][0]

    def lru_conv(src, src_b, Ntok, cc):
        for t0 in range(0, Ntok, NT):
            n = min(NT, Ntok - t0)
            pb = psring.next()
            for tap in range(5):
                tk.op("pe", lambda e: e.matmul(psum[pb][:, 0:n], lhsT=dgl[:, tap, :], rhs=src[:, t0 + tap:t0 + tap + n],
                                               start=(tap == 0), stop=(tap == 4)),
                      reads=[dgl_b, src_b], **({"writes": [ps_b[pb]]} if tap == 0 else {"parts": [ps_b[pb]]}))
            tk.op("act", lambda e: e.activation(out=xr32[:, t0:t0 + n], in_=psum[pb][:, 0:n], func=AF.Identity,
                                                bias=vecs[:, blc_o + cc:blc_o + cc + 1], scale=1.0),
                  reads=[ps_b[pb], vecs_b], parts=[xr32_b])
            tk.op("act", lambda e: e.activation(out=xrb[:, t0:t0 + n], in_=psum[pb][:, 0:n], func=AF.Identity,
                                                bias=vecs[:, blc_o + cc:blc_o + cc + 1], scale=1.0),
                  reads=[ps_b[pb], vecs_b], parts=[xrb_b])

    def lru_dir(Ntok, d, cc, init, reverse):
        lru_gates(Ntok, d, cc)
        lru_elem(Ntok, d, cc, init, reverse)

    def lru_gates(Ntok, d, cc):
        for t0 in range(0, Ntok, NT):
            n = min(NT, Ntok - t0)
            pr_ = psring.next(); pi_ = psring.next()
            tk.op("pe", lambda e: e.matmul(psum[pr_][:, 0:n], lhsT=gwb[:, cc, 2 * d, :], rhs=xrb[:, t0:t0 + n], start=True, stop=True),
                  reads=[gwb_b, xrb_b], writes=[ps_b[pr_]])
            tk.op("pe", lambda e: e.matmul(psum[pi_][:, 0:n], lhsT=gwb[:, cc, 2 * d + 1, :], rhs=xrb[:, t0:t0 + n], start=True, stop=True),
                  reads=[gwb_b, xrb_b], writes=[ps_b[pi_]])
            tk.op("act", lambda e: e.activation(out=Rt[:, t0:t0 + n], in_=psum[pr_][:, 0:n], func=AF.Sigmoid,
                                                bias=vecs[:, ba_o + d * 10 + cc:ba_o + d * 10 + cc + 1], scale=1.0),
                  reads=[ps_b[pr_], vecs_b], parts=[Rt_b])
            tk.op("act", lambda e: e.activation(out=IGt[:, t0:t0 + n], in_=psum[pi_][:, 0:n], func=AF.Sigmoid,
                                                bias=vecs[:, bx_o + d * 10 + cc:bx_o + d * 10 + cc + 1], scale=1.0),
                  reads=[ps_b[pi_], vecs_b], parts=[IGt_b])

    def lru_elem(Ntok, d, cc, init, reverse):
        k = d * 10 + cc
        tk.op("act", lambda e: e.activation(out=A2t[:, 0:Ntok], in_=Rt[:, 0:Ntok], func=AF.Exp, scale=clam[:, 1, k:k + 1]),
              reads=[Rt_b, clam_b], writes=[A2t_b])
        tk.op("act", lambda e: e.activation(out=Rt[:, 0:Ntok], in_=Rt[:, 0:Ntok], func=AF.Exp, scale=clam[:, 0, k:k + 1]),
              reads=[Rt_b, clam_b], writes=[Rt_b])
        tk.op("act", lambda e: e.activation(out=A2t[:, 0:Ntok], in_=A2t[:, 0:Ntok], func=AF.Sqrt, scale=-1.0, bias=consts[:, 1:2]),
              reads=[A2t_b, consts_b], writes=[A2t_b])
        tk.op("pool", lambda e: e.tensor_tensor(out=IGt[:, 0:Ntok], in0=IGt[:, 0:Ntok], in1=A2t[:, 0:Ntok], op=ALU.mult),
              reads=[IGt_b, A2t_b], writes=[IGt_b])
        tk.op("dve", lambda e: e.tensor_tensor(out=IGt[:, 0:Ntok], in0=IGt[:, 0:Ntok], in1=xr32[:, 0:Ntok], op=ALU.mult),
              reads=[IGt_b, xr32_b], writes=[IGt_b])
        if reverse:
            tk.op("dve", lambda e: e.tensor_tensor_scan(out=Ht[:, 0:Ntok][:, ::-1], data0=Rt[:, 0:Ntok][:, ::-1], data1=IGt[:, 0:Ntok][:, ::-1],
                                                        initial=init, op0=ALU.mult, op1=ALU.add),
                  reads=[Rt_b, IGt_b, carry_b, s0_b], writes=[Ht_b])
        else:
            tk.op("dve", lambda e: e.tensor_tensor_scan(out=Ht[:, 0:Ntok], data0=Rt[:, 0:Ntok], data1=IGt[:, 0:Ntok],
                                                        initial=init, op0=ALU.mult, op1=ALU.add),
                  reads=[Rt_b, IGt_b, carry_b, s0_b], writes=[Ht_b])

    def build_dgl(cc):
        for tap in range(5):
            col = wlc_o + tap * 10 + cc
            tk.op("act", lambda e: e.activation(out=dgl[:, tap, :], in_=ident, func=AF.Identity, scale=vecs[:, col:col + 1], bias=consts[:, 3:4]),
                  reads=[ident_b, vecs_b, consts_b], **({"writes": [dgl_b]} if tap == 0 else {"parts": [dgl_b]}))

    for cc in range(LC):
        tk.dma("sp", gw32, wg_d[cc], gw32_b, writes=[gw32_b])
        tk.op("dve", lambda e: e.tensor_copy(out=gwb[:, cc, :, :], in_=gw32), reads=[gw32_b], parts=[gwb_b])
        tk.dma("sp", cux, CUX_d[cc], cux_b, reads=[CUX_b], writes=[cux_b])
        tk.dma("sp", uxb, UX_d[cc], uxb_b, reads=[UX_b], writes=[uxb_b])
        do_conf = cc < KC
        if do_conf:
            kc = cc
            tk.dma("sp", ub, U_d[kc], ub_b, reads=[U_b], writes=[ub_b])
        build_dgl(cc)
        lru_conv(cux, cux_b, CTX, cc)
        lru_dir(CTX, 0, cc, 0.0, False)
        tk.op("dve", lambda e: e.tensor_copy(out=s0[:, cc:cc + 1], in_=Ht[:, CTX - 1:CTX]), reads=[Ht_b], parts=[s0_b])
        if do_conf:
            for tap in range(31):
                col = wdw_o + tap * 8 + kc
                if tap % 2 == 0:
                    tk.op("act", lambda e: e.activation(out=dgc[:, tap, :], in_=ident, func=AF.Identity, scale=vecs[:, col:col + 1], bias=consts[:, 3:4]),
                          reads=[ident_b, vecs_b, consts_b], **({"writes": [dgc_b]} if tap == 0 else {"parts": [dgc_b]}))
                else:
                    tk.op("pool", lambda e: e.tensor_scalar(out=dgc[:, tap, :], in0=ident, scalar1=vecs[:, col:col + 1], scalar2=None, op0=ALU.mult),
                          reads=[ident_b, vecs_b], parts=[dgc_b])
        lru_conv(uxb, uxb_b, LT, cc)
        lru_gates(LT, 0, cc)
        cbanks = []
        if do_conf:
            for ti in range(LT // NT):
                t0 = ti * NT
                pb = psring.next()
                cbanks.append(pb)
                for tap in range(31):
                    tk.op("pe", lambda e: e.matmul(psum[pb][:, :], lhsT=dgc[:, tap, :], rhs=ub[:, t0 + tap:t0 + tap + NT],
                                                   start=(tap == 0), stop=(tap == 30)),
                          reads=[dgc_b, ub_b], **({"writes": [ps_b[pb]]} if tap == 0 else {"parts": [ps_b[pb]]}))
        lru_elem(LT, 0, cc, s0[:, cc:cc + 1], False)
        tk.op("dve", lambda e: e.tensor_copy(out=sfin[:, cc:cc + 1], in_=Ht[:, LT - 1:LT]), reads=[Ht_b], parts=[sfin_b])
        tk.dma("sp", HF_d[cc], Ht, Ht_b, reads=[Ht_b], parts=[HF_b])
        if do_conf:
            for ti, pb in enumerate(cbanks):
                t0 = ti * NT
                if ti % 2 == 0:
                    tk.op("act", lambda e: e.activation(out=cst[:, t0:t0 + NT], in_=psum[pb][:, :], func=AF.Identity,
                                                        bias=vecs[:, bdw_o + kc:bdw_o + kc + 1], scale=1.0),
                          reads=[ps_b[pb], vecs_b], parts=[cst_b])
                else:
                    tk.op("dve", lambda e: e.tensor_scalar(out=cst[:, t0:t0 + NT], in0=psum[pb][:, :], scalar1=vecs[:, bdw_o + kc:bdw_o + kc + 1],
                                                           scalar2=None, op0=ALU.add),
                          reads=[ps_b[pb], vecs_b], parts=[cst_b])
            tk.dma("sp", C_d[kc], cst, cst_b, reads=[cst_b], parts=[C_b])

    if stop_after == "B1":
        return finish_early([])
    tk.dma("sp", cci_d, sfin, sfin_b, reads=[sfin_b], writes=[cci_b])
    tk.coll(lambda e: e.collective_compute("AllGather", ALU.bypass, replica_groups=[list(range(NCORES))],
                                           ins=[cci_d.opt()], outs=[cco_d.opt()]), reads=[cci_b], writes=[cco_b])
    tk.dma("sp", gath, cco_d.rearrange("(r p) c -> p r c", p=128), gath_b, reads=[cco_b], writes=[gath_b])
    pm_o = VOFF["pmask"][0]
    tk.op("dve", lambda e: e.tensor_scalar(out=carry, in0=gath[:, 0, :], scalar1=vecs[:, pm_o:pm_o + 1], scalar2=None, op0=ALU.mult),
          reads=[gath_b, vecs_b], writes=[carry_b])
    for r_ in range(1, NCORES):
        tk.op("dve", lambda e: e.scalar_tensor_tensor(out=carry, in0=gath[:, r_, :], scalar=vecs[:, pm_o + r_:pm_o + r_ + 1], in1=carry,
                                                       op0=ALU.mult, op1=ALU.add), reads=[gath_b, vecs_b, carry_b], writes=[carry_b])

    if stop_after == "BX":
        return finish_early([])
    for cc in range(LC):
        tk.dma("sp", uxb, UX_d[cc], uxb_b, reads=[UX_b], writes=[uxb_b])
        build_dgl(cc)
        tk.dma("sp", cst, HF_d[cc], cst_b, reads=[HF_b], writes=[cst_b])
        tk.dma("sp", ub[:, 0:LT], G_d[cc], ub_b, reads=[G_b], writes=[ub_b])
        lru_conv(uxb, uxb_b, LT, cc)
        lru_dir(LT, 1, cc, carry[:, cc:cc + 1], True)
        tk.op("pool", lambda e: e.tensor_tensor(out=Ht, in0=Ht, in1=cst, op=ALU.add), reads=[Ht_b, cst_b], writes=[Ht_b])
        tk.op("dve", lambda e: e.tensor_tensor(out=xrb, in0=Ht, in1=ub[:, 0:LT], op=ALU.mult), reads=[Ht_b, ub_b], writes=[xrb_b])
        tk.dma("sp", M_d[cc], xrb, xrb_b, reads=[xrb_b], parts=[M_b])
    end_phase()
    if stop_after == "B":
        return finish_early([("M", M_d), ("C", C_d), ("HF", HF_d)])
    if stop_after == "B0":
        return finish_early([])

    begin_phase()
    alloc_ring("C")
    c32 = sb("c32", [128, KC, NT], F32); c32_b = tk.buf("c32")
    xfC = sb("xfC", [128, KC, NT], F32); xfC_b = tk.buf("xfC")
    B1 = sb("B1", [128, KC, NT], BF16); B1_b = tk.buf("B1")
    B2 = sb("B2", [128, KC, NT], BF16); B2_b = tk.buf("B2")
    hidC = sb("hidC", [128, 22, NT], BF16); hidC_b = tk.buf("hidC")
    gcl = sb("gcl", [128, 16, NT], BF16); gcl_b = tk.buf("gcl")
    mrg = sb("mrg", [128, LC, NT], BF16); mrg_b = tk.buf("mrg")
    gcy = sb("gcy", [128, KC, NT], F32); gcy_b = tk.buf("gcy")
    ost = [sb("ost%d" % i, [128, D], F32) for i in range(2)]
    ost_b = [tk.buf("ost%d" % i) for i in range(2)]
    gln_o = VOFF["g_ln"][0]; bln_o = VOFF["b_ln"][0]; gf_o = VOFF["g_final"][0]
    seqC = []
    for i in range(LT // NT):
        seqC += [(BLK["CO"] + j, KC * 512) for j in range(2)] + [(BLK["LO"] + j, LC * 256) for j in range(4)]
        seqC += [(BLK["WO"] + j, KC * 512) for j in range(2)] + up_seq(1)
    wsC = WStream(seqC)
    for i in range(LT // NT):
        t0 = i * NT
        N = NT
        tk.dma("sp", c32, C_d[:, :, t0:t0 + N].rearrange("k p n -> p k n"), c32_b, reads=[C_b], writes=[c32_b])
        tk.dma("sp", gcl, GCL_d[:, :, t0:t0 + N].rearrange("k p n -> p k n"), gcl_b, reads=[GCL_b], writes=[gcl_b])
        tk.dma("sp", mrg, M_d[:, :, t0:t0 + N].rearrange("k p n -> p k n"), mrg_b, reads=[M_b], writes=[mrg_b])
        tk.dma("sp", xfC, X1_d[:, :, t0:t0 + N].rearrange("k p n -> p k n"), xfC_b, reads=[X1_b], writes=[xfC_b])
        tk.op("dve", lambda e: e.tensor_copy(out=B1, in_=c32), reads=[c32_b], writes=[B1_b])
        tk.op("act", lambda e: e.activation(out=B2, in_=c32, func=AF.Square), reads=[c32_b], writes=[B2_b])
        p1 = psring.next(); p2 = psring.next()
        for kc in range(KC):
            tk.op("pe", lambda e: e.matmul(psum[p1][:, :], lhsT=ones_bf, rhs=B1[:, kc, :], start=(kc == 0), stop=(kc == KC - 1)),
                  reads=[ones_b, B1_b], **({"writes": [ps_b[p1]]} if kc == 0 else {"parts": [ps_b[p1]]}))
        for kc in range(KC):
            tk.op("pe", lambda e: e.matmul(psum[p2][:, :], lhsT=ones_bf, rhs=B2[:, kc, :], start=(kc == 0), stop=(kc == KC - 1)),
                  reads=[ones_b, B2_b], **({"writes": [ps_b[p2]]} if kc == 0 else {"parts": [ps_b[p2]]}))
        tk.op("dve", lambda e: e.tensor_scalar(out=small[:, 0, :], in0=psum[p1][:, :], scalar1=1.0 / D, scalar2=None, op0=ALU.mult),
              reads=[ps_b[p1]], writes=[rs_b])
        tk.op("dve", lambda e: e.tensor_tensor(out=small[:, 1, :], in0=small[:, 0, :], in1=small[:, 0, :], op=ALU.mult), reads=[rs_b], writes=[rs_b])
        tk.op("dve", lambda e: e.scalar_tensor_tensor(out=small[:, 2, :], in0=psum[p2][:, :], scalar=1.0 / D, in1=small[:, 1, :],
                                                       op0=ALU.mult, op1=ALU.subtract), reads=[ps_b[p2], rs_b], writes=[rs_b])
        tk.op("act", lambda e: e.activation(out=small[:, 2, :], in_=small[:, 2, :], func=AF.Sqrt, scale=1.0, bias=consts[:, 0:1]),
              reads=[rs_b, consts_b], writes=[rs_b])
        pr = psring.next()
        tk.op("dve", lambda e: e.reciprocal(out=psum[pr][:, :], in_=small[:, 2, :]), reads=[rs_b], writes=[ps_b[pr]])
        for kc in range(KC):
            ti = tmpring.next()
            tk.op("dve", lambda e: e.tensor_tensor(out=tmpg[ti][:, 0, :], in0=c32[:, kc, :], in1=small[:, 0, :], op=ALU.subtract),
                  reads=[c32_b, rs_b], writes=[tmpg_b[ti]])
            tk.op("dve", lambda e: e.tensor_tensor(out=tmpg[ti][:, 1, :], in0=tmpg[ti][:, 0, :], in1=psum[pr][:, :], op=ALU.mult),
                  reads=[tmpg_b[ti], ps_b[pr]], writes=[tmpg_b[ti]])
            tk.op("act", lambda e: e.activation(out=B1[:, kc, :], in_=tmpg[ti][:, 1, :], func=AF.Silu,
                                                scale=vecs[:, gln_o + kc:gln_o + kc + 1], bias=vecs[:, bln_o + kc:bln_o + kc + 1]),
                  reads=[tmpg_b[ti], vecs_b], **({"writes": [B1_b]} if kc == 0 else {"parts": [B1_b]}))
        for j in range(2):
            wt, wb = wsC.next(BLK["CO"] + j)
            w3 = wt[:, 0:KC * 512].rearrange("p (k o) -> p k o", o=512)
            pbs = [psring.next() for _ in range(4)]
            for mi in range(4):
                pb = pbs[mi]
                for kc in range(KC):
                    tk.op("pe", lambda e: e.matmul(psum[pb][:, :], lhsT=w3[:, kc, mi * 128:(mi + 1) * 128], rhs=B1[:, kc, :],
                                                   start=(kc == 0), stop=(kc == KC - 1)),
                          reads=[wb, B1_b], **({"writes": [ps_b[pb]]} if kc == 0 else {"parts": [ps_b[pb]]}))
            for mi in range(4):
                m = 4 * j + mi
                tk.op("dve", lambda e: e.tensor_tensor(out=gcy[:, m, :], in0=psum[pbs[mi]][:, :], in1=gcl[:, m, :], op=ALU.mult),
                      reads=[ps_b[pbs[mi]], gcl_b], **({"writes": [gcy_b]} if m == 0 else {"parts": [gcy_b]}))
        for j in range(4):
            wt, wb = wsC.next(BLK["LO"] + j)
            w3 = wt[:, 0:LC * 256].rearrange("p (k o) -> p k o", o=256)
            pbs = [psring.next() for _ in range(2)]
            for mi in range(2):
                pb = pbs[mi]
                for kc in range(LC):
                    tk.op("pe", lambda e: e.matmul(psum[pb][:, :], lhsT=w3[:, kc, mi * 128:(mi + 1) * 128], rhs=mrg[:, kc, :],
                                                   start=(kc == 0), stop=(kc == LC - 1)),
                          reads=[wb, mrg_b], **({"writes": [ps_b[pb]]} if kc == 0 else {"parts": [ps_b[pb]]}))
            ti = tmpring.next()
            for mi in range(2):
                m = 2 * j + mi
                tk.op("dve", lambda e: e.tensor_tensor(out=tmpg[ti][:, mi, :], in0=psum[pbs[mi]][:, :], in1=gcl[:, 8 + m, :], op=ALU.mult),
                      reads=[ps_b[pbs[mi]], gcl_b], **({"writes": [tmpg_b[ti]]} if mi == 0 else {"parts": [tmpg_b[ti]]}))
            tk.op("pool", lambda e: e.tensor_tensor(out=B2[:, 2 * j:2 * j + 2, :], in0=tmpg[ti], in1=gcy[:, 2 * j:2 * j + 2, :], op=ALU.add),
                  reads=[tmpg_b[ti], gcy_b], **({"writes": [B2_b]} if j == 0 else {"parts": [B2_b]}))
        for j in range(2):
            wt, wb = wsC.next(BLK["WO"] + j)
            w3 = wt[:, 0:KC * 512].rearrange("p (k o) -> p k o", o=512)
            pbs = [psring.next() for _ in range(4)]
            for mi in range(4):
                pb = pbs[mi]
                for kc in range(KC):
                    tk.op("pe", lambda e: e.matmul(psum[pb][:, :], lhsT=w3[:, kc, mi * 128:(mi + 1) * 128], rhs=B2[:, kc, :],
                                                   start=(kc == 0), stop=(kc == KC - 1)),
                          reads=[wb, B2_b], **({"writes": [ps_b[pb]]} if kc == 0 else {"parts": [ps_b[pb]]}))
            for mi in range(4):
                m = 4 * j + mi
                tk.op("dve", lambda e: e.scalar_tensor_tensor(out=xfC[:, m, :], in0=psum[pbs[mi]][:, :], scalar=modp[:, 5, m:m + 1, 0],
                                                               in1=xfC[:, m, :], op0=ALU.mult, op1=ALU.add),
                      reads=[ps_b[pbs[mi]], modp_b, xfC_b], writes=[xfC_b])
        rms_to_h(xfC, xfC_b, B1, B1_b, B2, B2_b, N, modp[:, 6, :, 0], modp[:, 7, :, 0])
        ffn(wsC, 1, xfC, xfC_b, B1, B1_b, hidC, hidC_b, N, modp[:, 8, :, 0])
        tk.op("act", lambda e: e.activation(out=B2, in_=xfC, func=AF.Square), reads=[xfC_b], writes=[B2_b])
        pb = psring.next()
        for kc in range(KC):
            tk.op("pe", lambda e: e.matmul(psum[pb][:, :], lhsT=ones_bf, rhs=B2[:, kc, :], start=(kc == 0), stop=(kc == KC - 1)),
                  reads=[ones_b, B2_b], **({"writes": [ps_b[pb]]} if kc == 0 else {"parts": [ps_b[pb]]}))
        tk.op("act", lambda e: e.activation(out=small[:, 0, :], in_=psum[pb][:, :], func=AF.Sqrt, scale=1.0 / D, bias=consts[:, 0:1]),
              reads=[ps_b[pb], consts_b], writes=[rs_b])
        pr = psring.next()
        tk.op("dve", lambda e: e.reciprocal(out=psum[pr][:, :], in_=small[:, 0, :]), reads=[rs_b], writes=[ps_b[pr]])
        for kc in range(KC):
            tk.op("dve", lambda e: e.scalar_tensor_tensor(out=c32[:, kc, :], in0=xfC[:, kc, :], scalar=vecs[:, gf_o + kc:gf_o + kc + 1],
                                                           in1=psum[pr][:, :], op0=ALU.mult, op1=ALU.mult),
                  reads=[xfC_b, vecs_b, ps_b[pr]], **({"writes": [c32_b]} if kc == 0 else {"parts": [c32_b]}))
        for s_ in range(N // 128):
            oi = s_ % 2
            for half in range(2):
                pb = psring.next()
                for q in range(4):
                    kc = half * 4 + q
                    tk.op("pe", lambda e: e.transpose(out=psum[pb][:, q * 128:(q + 1) * 128], in_=c32[:, kc, s_ * 128:(s_ + 1) * 128], identity=ident),
                          reads=[c32_b, ident_b], **({"writes": [ps_b[pb]]} if q == 0 else {"parts": [ps_b[pb]]}))
                if half == 0:
                    tk.op("act", lambda e: e.activation(out=ost[oi][:, 0:512], in_=psum[pb][:, :], func=AF.Copy), reads=[ps_b[pb]], writes=[ost_b[oi]])
                else:
                    tk.op("dve", lambda e: e.tensor_copy(out=ost[oi][:, 512:1024], in_=psum[pb][:, :]), reads=[ps_b[pb]], parts=[ost_b[oi]])
            tk.dma("sp", out_d[t0 + s_ * 128:t0 + (s_ + 1) * 128, :], ost[oi], ost_b[oi], reads=[ost_b[oi]], parts=[out_b])
    end_phase()
    if debug:
        pass
    return nc


def _fm(v, k):
    return np.ascontiguousarray(np.asarray(v, np.float32).reshape(k, 128).T)


def make_in_maps(inp):
    x = inp["x"]; ctx = inp["ctx"]
    maps = []
    for c in range(NCORES):
        b, half = c // 2, c % 2
        if half == 0:
            xl = x[b, 0:LT + HALO]
            cl = ctx[b]
        else:
            xl = x[b, SEQ - LT - HALO:SEQ][::-1]
            cl = ctx[b][::-1]
        d = [half, 1 - half]
        vec = np.zeros((128, NV), np.float32)

        def put(name, arr):
            o, k = VOFF[name]
            assert arr.shape == (128, k), (name, arr.shape)
            vec[:, o:o + k] = arr

        put("g_n1", _fm(inp["g_n1"][0], 8)); put("g_n2", _fm(inp["g_n2"][0], 8)); put("g_n3", _fm(inp["g_n3"][0], 8))
        put("g_final", _fm(inp["g_final"], 8)); put("b_mod", _fm(inp["b_mod"][0], 72)); put("b_in", _fm(inp["b_in"][0], 52))
        wdw = inp["w_dw"][0]
        if half == 1:
            wdw = wdw[::-1]
        put("w_dw", np.concatenate([_fm(wdw[t], 8) for t in range(31)], axis=1))
        put("b_dw", _fm(inp["b_dw"][0], 8)); put("g_ln", _fm(inp["g_ln"][0], 8)); put("b_ln", _fm(inp["b_ln"][0], 8))
        wl = inp["w_lru_conv"][0]
        z = np.zeros((1, LRU), np.float32)
        w5 = np.concatenate([wl, z], 0) if half == 0 else np.concatenate([z, wl[::-1]], 0)
        put("w_lc", np.concatenate([_fm(w5[t], 10) for t in range(5)], axis=1))
        put("b_lc", _fm(inp["b_lru_conv"][0], 10))
        put("b_a", np.concatenate([_fm(inp["b_rec_gate"][0, dd], 10) for dd in d], axis=1))
        put("b_x", np.concatenate([_fm(inp["b_in_gate"][0, dd], 10) for dd in d], axis=1))
        put("lam", np.concatenate([_fm(inp["lru_lambda"][0, dd], 10) for dd in d], axis=1))
        c2 = np.zeros((128, 16), np.float32)
        cb = _fm(inp["c"][b], 8); cc_ = _fm(inp["c_ctx"], 8)
        c2[:, 0::2] = cb; c2[:, 1::2] = cc_
        put("c2", c2)
        fl = np.zeros((128, 2), np.float32); fl[:, 0] = half; fl[:, 1] = 1.0 - 2.0 * half
        put("flags", fl)
        pm = np.zeros((128, 8), np.float32); pm[:, c ^ 1] = 1.0
        put("pmask", pm)
        wa = inp["w_rec_gate"][0]; wx = inp["w_in_gate"][0]
        wg = np.stack([wa[d[0]], wx[d[0]], wa[d[1]], wx[d[1]]], axis=2)
        maps.append({
            "x_loc": np.ascontiguousarray(xl, np.float32), "ctx_loc": np.ascontiguousarray(cl, np.float32), "vecs": vec,
            "w_mod": inp["w_mod"][0], "w_ffn1_up": inp["w_ffn1_up"][0], "w_ffn2_up": inp["w_ffn2_up"][0],
            "w_ffn1_down": inp["w_ffn1_down"][0], "w_ffn2_down": inp["w_ffn2_down"][0], "w_in": inp["w_in"][0],
            "w_conf_out": inp["w_conf_out"][0], "w_lru_out": inp["w_lru_out"][0], "w_out": inp["w_out"][0],
            "w_gates": np.ascontiguousarray(wg, np.float32),
        })
    return maps


def kernel(**inputs):
    inp = {k: np.asarray(v) for k, v in inputs.items()}
    nc = build()
    maps = make_in_maps(inp)
    res = run_bass_kernel_spmd(nc, maps, core_ids=list(range(NCORES)))
    out = np.empty((4, SEQ, D), np.float32)
    for c in range(NCORES):
        b, half = c // 2, c % 2
        y = np.asarray(res.results[c]["y_loc"])
        if half == 0:
            out[b, 0:LT] = y
        else:
            out[b, LT:SEQ] = y[::-1]
    return out
```

```python
import numpy as np
import concourse.bass as bass
import concourse.mybir as mybir
from concourse.bass_utils import run_bass_kernel_spmd

F32 = mybir.dt.float32
BF16 = mybir.dt.bfloat16
AF = mybir.ActivationFunctionType
ALU = mybir.AluOpType

D = 1024
KC = 8
SEQ = 8192
LT = 4096
HALO = 16
CTX = 256
FF = 2816
LRU = 1280
LC = 10
NT = 512
EPS = 1e-6
NCORES = 8

_VEC_SPECS = [("g_n1", 8), ("g_n2", 8), ("g_n3", 8), ("g_final", 8), ("b_mod", 72), ("b_in", 52),
              ("w_dw", 31 * 8), ("b_dw", 8), ("g_ln", 8), ("b_ln", 8), ("w_lc", 5 * 10), ("b_lc", 10),
              ("b_a", 20), ("b_x", 20), ("lam", 20), ("c2", 16), ("flags", 2), ("pmask", 8)]
VOFF = {}
_o = 0
for _n, _k in _VEC_SPECS:
    VOFF[_n] = (_o, _k)
    _o += _k
NV = _o

BLK = {}
_b = 0
for _n, _k in [("UP1", 11), ("DN1", 8), ("IN", 14), ("CO", 2), ("LO", 4), ("WO", 2), ("UP2", 11), ("DN2", 8)]:
    BLK[_n] = _b
    _b += _k
NBLK = _b
WBE = 4096


class Buf:
    __slots__ = ("name", "w", "r", "prev", "dsem", "dcount", "excl")

    def __init__(self, name, excl=False):
        self.name = name
        self.excl = excl
        self.w = {}
        self.r = {}
        self.prev = {}
        self.dsem = None
        self.dcount = 0


class Tk:
    def __init__(self, nc):
        self.nc = nc
        self.eng = {"pe": nc.tensor, "act": nc.scalar, "dve": nc.vector, "pool": nc.gpsimd, "sp": nc.sync}
        self.sem = {k: nc.alloc_semaphore("sem_" + k) for k in self.eng}
        self.cnt = {k: 0 for k in self.eng}
        self.seen = {k: {} for k in self.eng}
        self.semobj = {}
        for k, s in self.sem.items():
            self.semobj[id(s)] = s
        self.dsems = []
        self.nwaits = 0

    def buf(self, name, excl=False):
        return Buf(name, excl)

    def _wait(self, e, deps):
        eng = self.eng[e]
        seen = self.seen[e]
        for sid, val in deps.items():
            if e == "pe" and sid == id(self.sem["pe"]):
                continue
            if seen.get(sid, 0) >= val:
                continue
            eng.wait_ge(self.semobj[sid], val)
            seen[sid] = val
            self.nwaits += 1

    @staticmethod
    def _merge(dst, src):
        for k, v in src.items():
            if dst.get(k, 0) < v:
                dst[k] = v

    def _collect(self, reads, writes, parts):
        deps = {}
        for b in reads:
            self._merge(deps, b.w)
            if b.excl:
                self._merge(deps, b.r)
        for b in writes:
            self._merge(deps, b.w)
            self._merge(deps, b.r)
        for b in parts:
            if b.r:
                self._merge(deps, b.w)
                self._merge(deps, b.r)
            else:
                self._merge(deps, b.prev)
        return deps

    def _record(self, tok, reads, writes, parts):
        sid, val = tok
        for b in reads:
            if b.r.get(sid, 0) < val:
                b.r[sid] = val
        for b in writes:
            prev = dict(b.w)
            self._merge(prev, b.r)
            b.prev = prev
            b.w = {sid: val}
            b.r = {}
        for b in parts:
            if b.r:
                prev = dict(b.w)
                self._merge(prev, b.r)
                b.prev = prev
                b.w = {sid: val}
                b.r = {}
            else:
                if b.w.get(sid, 0) < val:
                    b.w[sid] = val

    def op(self, e, fn, reads=(), writes=(), parts=()):
        deps = self._collect(reads, writes, parts)
        self._wait(e, deps)
        ins = fn(self.eng[e])
        ins.then_inc(self.sem[e], 1)
        self.cnt[e] += 1
        assert self.cnt[e] < 60000
        self._record((id(self.sem[e]), self.cnt[e]), reads, writes, parts)
        return ins

    def dma(self, e, out, in_, sb, reads=(), writes=(), parts=(), **kw):
        deps = self._collect(reads, writes, parts)
        self._wait(e, deps)
        if sb.dsem is None:
            sb.dsem = self.nc.alloc_semaphore("d_" + sb.name)
            self.semobj[id(sb.dsem)] = sb.dsem
            self.dsems.append(sb)
        ins = self.eng[e].dma_start(out=out, in_=in_, **kw)
        ins.then_inc(sb.dsem, 16)
        sb.dcount += 16
        assert sb.dcount < 60000
        self._record((id(sb.dsem), sb.dcount), reads, writes, parts)
        return ins

    def coll(self, fn, reads=(), writes=()):
        deps = self._collect(reads, writes, ())
        self._wait("pool", deps)
        if not hasattr(self, "ccsem"):
            self.ccsem = self.nc.alloc_semaphore("ccsem")
            self.semobj[id(self.ccsem)] = self.ccsem
            self.cccount = 0
        ins = fn(self.eng["pool"])
        ins.then_inc(self.ccsem)
        self.cccount += 1
        self._record((id(self.ccsem), self.cccount), reads, writes, ())
        return ins

    def wait_tokens(self, e, deps):
        self._wait(e, deps)

    def all_tokens(self):
        deps = {}
        for k in self.eng:
            if self.cnt[k] > 0:
                deps[id(self.sem[k])] = self.cnt[k]
        for sb in self.dsems:
            deps[id(sb.dsem)] = sb.dcount
        if hasattr(self, "ccsem") and self.cccount > 0:
            deps[id(self.ccsem)] = self.cccount
        return deps

    def barrier(self):
        deps = self.all_tokens()
        for k in self.eng:
            d = dict(deps)
            self._wait(k, d)


class Ring:
    def __init__(self, items):
        self.items = items
        self.i = 0

    def next(self):
        it = self.items[self.i % len(self.items)]
        self.i += 1
        return it


def build(debug=False, stop_after=None, tilesA_override=None, R=NCORES):
    nc = bass.Bass("TRN2", target_bir_lowering=False)
    tk = Tk(nc)

    def din(name, shape, dt=F32):
        return nc.dram_tensor(name, list(shape), dt, kind="ExternalInput").ap()

    def dscr(name, shape, dt):
        return nc.dram_tensor(name, list(shape), dt).ap()

    x_d = din("x_loc", [LT + HALO, D])
    ctx_d = din("ctx_loc", [CTX, D])
    vecs_d = din("vecs", [128, NV])
    wmod_d = din("w_mod", [D, 9 * D])
    NS = (NBLK + R - 1) // R
    G1 = min(NS, (40 + R - 1) // R)
    wshare_d = din("wshare", [NS, 128, WBE])
    wg_d = din("w_gates", [LC, 128, 4, 128])
    out_d = nc.dram_tensor("y_loc", [LT, D], F32, kind="ExternalOutput").ap()

    grp_slots = [(0, G1), (G1, NS)]
    WBin_d = [dscr("WBin%d" % g, [max(1, b_ - a_) * 128, WBE], BF16) for g, (a_, b_) in enumerate(grp_slots)]
    WBall_d = [dscr("WBall%d" % g, [R * max(1, b_ - a_) * 128, WBE], BF16) for g, (a_, b_) in enumerate(grp_slots)]
    wbin_b = [tk.buf("wbin%d" % g) for g in range(2)]

    def wb_ap(blk):
        r_, s_ = blk % R, blk // R
        g = 0 if s_ < G1 else 1
        a_, b_ = grp_slots[g]
        row = (r_ * (b_ - a_) + (s_ - a_)) * 128
        return WBall_d[g][row:row + 128, :]
    X1_d = dscr("X1", [KC, 128, LT], F32)
    WU = 15 + LT + HALO
    U_d = dscr("U", [KC, 128, WU], BF16)
    WX = 2 + LT + HALO
    UX_d = dscr("UX", [LC, 128, WX], BF16)
    CUX_d = dscr("CUX", [LC, 128, CTX + 4], BF16)
    G_d = dscr("G", [LC, 128, LT], BF16)
    GCL_d = dscr("GCL", [16, 128, LT], BF16)
    M_d = dscr("M", [LC, 128, LT], BF16)
    C_d = dscr("C", [KC, 128, LT], F32)
    HF_d = dscr("HF", [LC, 128, LT], F32)
    cci_d = dscr("cc_in", [128, LC], F32)
    cco_d = dscr("cc_out", [NCORES * 128, LC], F32)

    wb_bufs = [tk.buf("wb%d" % i) for i in range(NBLK)]
    X1_b = tk.buf("X1"); U_b = tk.buf("U"); UX_b = tk.buf("UX"); CUX_b = tk.buf("CUX")
    G_b = tk.buf("G"); GCL_b = tk.buf("GCL"); M_b = tk.buf("M"); C_b = tk.buf("C"); HF_b = tk.buf("HF")
    cci_b = tk.buf("cci"); cco_b = tk.buf("cco"); out_b = tk.buf("out")

    import contextlib
    phase = {"st": None}

    def sb(name, shape, dt):
        if phase["st"] is None:
            return nc.alloc_sbuf_tensor("s_" + name, list(shape), dt).ap()
        return phase["st"].enter_context(nc.sbuf_tensor("s_" + name, list(shape), dt)).ap()

    def begin_phase():
        phase["st"] = contextlib.ExitStack()

    def end_phase():
        tk.barrier()
        phase["st"].close()
        phase["st"] = None

    vecs = sb("vecs", [128, NV], F32); vecs_b = tk.buf("vecs")
    ident = sb("ident", [128, 128], F32); ident_b = tk.buf("ident")
    ones_bf = sb("ones_bf", [128, 128], BF16); ones_b = tk.buf("ones")
    consts = sb("consts", [128, 8], F32); consts_b = tk.buf("consts")
    modp = sb("modp", [128, 9, KC, 2], F32); modp_b = tk.buf("modp")
    TR = sb("TR", [128, 4, 65], F32); TC = sb("TC", [128, 4, 64], F32); pos_b = tk.buf("pos")
    clam = sb("clam", [128, 2, 20], F32); clam_b = tk.buf("clam")
    s0 = sb("s0", [128, LC], F32); s0_b = tk.buf("s0")
    sfin = sb("sfin", [128, LC], F32); sfin_b = tk.buf("sfin")
    carry = sb("carry", [128, LC], F32); carry_b = tk.buf("carry")

    psum = [nc.alloc_psum_tensor("ps%d" % i, [128, 512], F32).ap() for i in range(8)]
    ps_b = [tk.buf("ps%d" % i, excl=True) for i in range(8)]
    psring = Ring(list(range(8)))

    def vcol(name, idx=0, n=1):
        o, k = VOFF[name]
        return vecs[:, o + idx:o + idx + n]

    def finish_early(items):
        tk.barrier()
        for nm, ap_ in items:
            o = nc.dram_tensor("dbg_" + nm, list(ap_.shape), ap_.dtype, kind="ExternalOutput").ap()
            b_ = tk.buf("dbg_" + nm)
            tk.dma("sp", o, ap_, b_, writes=[b_])
        z = nc.alloc_sbuf_tensor("s_zout", [128, D], F32).ap(); z_b = tk.buf("zout")
        tk.op("dve", lambda e: e.memset(z, 0.0), writes=[z_b])
        for i in range(LT // 128):
            tk.dma("sp", out_d[i * 128:(i + 1) * 128, :], z, z_b, reads=[z_b], parts=[out_b])
        tk.barrier()
        return nc

    tk.dma("sp", vecs, vecs_d, vecs_b, writes=[vecs_b])
    wring_t = [None] * 5
    wring_b = [tk.buf("wr%d" % i) for i in range(5)]

    def alloc_ring(tag):
        for i in range(5):
            wring_t[i] = sb("wr%s%d" % (tag, i), [128, WBE], BF16)

    small = sb("small", [128, 4, 512], F32)
    rs_b = tk.buf("rs_sb")
    tmpg = [sb("tmpg%d" % i, [128, 2, 512], F32) for i in range(2)]
    tmpg_b = [tk.buf("tmpg%d" % i) for i in range(2)]
    tmpring = Ring([0, 1])
    zt = sb("zt", [128, LC, 16], BF16); zt_b = tk.buf("zt")

    begin_phase()
    it = sb("iota_t", [128, 128], F32)
    it_b = tk.buf("iota_t")
    tk.op("pool", lambda e: e.iota(it, pattern=[[1, 128]], base=0, channel_multiplier=-1,
                                   allow_small_or_imprecise_dtypes=True), writes=[it_b])
    tk.op("dve", lambda e: e.tensor_single_scalar(out=ident, in_=it, scalar=0.0, op=ALU.is_equal),
          reads=[it_b], writes=[ident_b])
    tk.op("dve", lambda e: e.memset(ones_bf, 1.0), writes=[ones_b])
    tk.op("dve", lambda e: e.memset(consts[:, 0:1], EPS), parts=[consts_b])
    tk.op("dve", lambda e: e.memset(consts[:, 1:2], 1.0), parts=[consts_b])
    tk.op("dve", lambda e: e.memset(consts[:, 2:3], float(np.pi / 2)), parts=[consts_b])
    tk.op("dve", lambda e: e.memset(consts[:, 3:4], 0.0), parts=[consts_b])
    tk.op("dve", lambda e: e.memset(zt, 0.0), writes=[zt_b])

    def convert_weights():
        st32 = [sb("cv32_%d" % i, [128, WBE], F32) for i in range(3)]
        st16 = [sb("cv16_%d" % i, [128, WBE], BF16) for i in range(3)]
        st32_b = [tk.buf("cv32_%d" % i) for i in range(3)]
        st16_b = [tk.buf("cv16_%d" % i) for i in range(3)]
        for s_ in range(NS):
            i = s_ % 3
            eng = ("dve", "pool", "act")[s_ % 3]
            g = 0 if s_ < G1 else 1
            a_, b_ = grp_slots[g]
            tk.dma("sp", st32[i], wshare_d[s_], st32_b[i], writes=[st32_b[i]])
            if eng == "act":
                tk.op("act", lambda e: e.activation(out=st16[i], in_=st32[i], func=AF.Copy), reads=[st32_b[i]], writes=[st16_b[i]])
            else:
                tk.op(eng, lambda e: e.tensor_copy(out=st16[i], in_=st32[i]), reads=[st32_b[i]], writes=[st16_b[i]])
            tk.dma("sp", WBin_d[g][(s_ - a_) * 128:(s_ - a_ + 1) * 128, :], st16[i], st16_b[i], reads=[st16_b[i]], parts=[wbin_b[g]])
        for g, (a_, b_) in enumerate(grp_slots):
            if b_ <= a_:
                continue
            blks = [b for b in range(NBLK) if a_ <= b // R < b_]
            tk.coll(lambda e: e.collective_compute("AllGather", ALU.bypass, replica_groups=[list(range(R))],
                                                   ins=[WBin_d[g].opt()], outs=[WBall_d[g].opt()]),
                    reads=[wbin_b[g]], writes=[wb_bufs[b] for b in blks])

    convert_weights()

    def build_pos():
        om = sb("om", [128, 2], F32); om_b = tk.buf("om")
        pi_ = sb("pi_", [128, 2], F32); pi_b = tk.buf("pi_")
        Gs = sb("Gs", [128, 2, 128], F32); Gc = sb("Gc", [128, 2, 128], F32); G_bb = tk.buf("Gtab")
        sc = sb("sc_pos", [128, 2, 2], F32); sc_b = tk.buf("sc_pos")
        t1 = sb("t1_pos", [128, 128], F32); t1_b = tk.buf("t1_pos")
        sm = sb("sm_pos", [128, 16], F32); sm_b = tk.buf("sm_pos")
        tk.op("pool", lambda e: e.iota(pi_, pattern=[[128, 2]], base=0, channel_multiplier=1,
                                       allow_small_or_imprecise_dtypes=True), writes=[pi_b])
        tk.op("act", lambda e: e.activation(out=om, in_=pi_, func=AF.Exp, scale=float(-np.log(10000.0) / 256.0)),
              reads=[pi_b], writes=[om_b])
        tk.op("dve", lambda e: e.memset(Gs[:, :, 0:1], 0.0), parts=[G_bb])
        tk.op("dve", lambda e: e.memset(Gc[:, :, 0:1], 1.0), parts=[G_bb])
        tk.op("act", lambda e: e.activation(out=sc[:, :, 0], in_=om, func=AF.Sin, scale=1.0),
              reads=[om_b], parts=[sc_b])
        tk.op("act", lambda e: e.activation(out=sc[:, :, 1], in_=om, func=AF.Sin, scale=1.0, bias=consts[:, 2:3]),
              reads=[om_b, consts_b], parts=[sc_b])
        for b in range(7):
            w = 1 << b
            for j in range(2):
                sbv = sc[:, j, 0:1]; cbv = sc[:, j, 1:2]
                tk.op("dve", lambda e: e.tensor_scalar(out=t1[:, 0:w], in0=Gc[:, j, 0:w], scalar1=sbv, scalar2=None, op0=ALU.mult),
                      reads=[G_bb, sc_b], writes=[t1_b])
                tk.op("dve", lambda e: e.scalar_tensor_tensor(out=Gs[:, j, w:2 * w], in0=Gs[:, j, 0:w], scalar=cbv, in1=t1[:, 0:w],
                                                               op0=ALU.mult, op1=ALU.add),
                      reads=[G_bb, sc_b, t1_b], writes=[G_bb])
                tk.op("dve", lambda e: e.tensor_scalar(out=t1[:, 0:w], in0=Gs[:, j, 0:w], scalar1=sbv, scalar2=None, op0=ALU.mult),
                      reads=[G_bb, sc_b], writes=[t1_b])
                tk.op("dve", lambda e: e.scalar_tensor_tensor(out=Gc[:, j, w:2 * w], in0=Gc[:, j, 0:w], scalar=cbv, in1=t1[:, 0:w],
                                                               op0=ALU.mult, op1=ALU.subtract),
                      reads=[G_bb, sc_b, t1_b], writes=[G_bb])
            if b < 6:
                tk.op("dve", lambda e: e.tensor_tensor(out=sm[:, 0:2], in0=sc[:, :, 0], in1=sc[:, :, 0], op=ALU.mult),
                      reads=[sc_b], writes=[sm_b])
                tk.op("dve", lambda e: e.tensor_tensor(out=sm[:, 2:4], in0=sc[:, :, 0], in1=sc[:, :, 1], op=ALU.mult),
                      reads=[sc_b, sm_b], writes=[sm_b])
                tk.op("dve", lambda e: e.tensor_scalar(out=sc[:, :, 1], in0=sm[:, 0:2], scalar1=-2.0, scalar2=1.0,
                                                        op0=ALU.mult, op1=ALU.add), reads=[sm_b], writes=[sc_b])
                tk.op("dve", lambda e: e.tensor_scalar(out=sc[:, :, 0], in0=sm[:, 2:4], scalar1=2.0, scalar2=None,
                                                        op0=ALU.mult), reads=[sm_b, sc_b], writes=[sc_b])
        fo = VOFF["flags"][0]
        mfl = vecs[:, fo:fo + 1]; sfl = vecs[:, fo + 1:fo + 2]
        for (tab, r0max, n) in ((TR, 127, 65), (TC, 63, 64)):
            for j in range(2):
                S0 = sm[:, 4:5]; C0 = sm[:, 5:6]; sS0 = sm[:, 6:7]; sC0 = sm[:, 7:8]; om1 = sm[:, 8:9]
                tk.op("dve", lambda e: e.tensor_tensor(out=S0, in0=Gs[:, j, r0max:r0max + 1], in1=mfl, op=ALU.mult),
                      reads=[G_bb, vecs_b, sm_b], writes=[sm_b])
                tk.op("dve", lambda e: e.tensor_scalar(out=om1, in0=mfl, scalar1=-1.0, scalar2=1.0, op0=ALU.mult, op1=ALU.add),
                      reads=[vecs_b, sm_b], writes=[sm_b])
                tk.op("dve", lambda e: e.scalar_tensor_tensor(out=C0, in0=Gc[:, j, r0max:r0max + 1], scalar=mfl, in1=om1,
                                                               op0=ALU.mult, op1=ALU.add), reads=[G_bb, vecs_b, sm_b], writes=[sm_b])
                tk.op("dve", lambda e: e.tensor_tensor(out=sS0, in0=S0, in1=sfl, op=ALU.mult), reads=[sm_b, vecs_b], writes=[sm_b])
                tk.op("dve", lambda e: e.tensor_tensor(out=sC0, in0=C0, in1=sfl, op=ALU.mult), reads=[sm_b, vecs_b], writes=[sm_b])
                tk.op("dve", lambda e: e.tensor_scalar(out=t1[:, 0:n], in0=Gs[:, j, 0:n], scalar1=sC0, scalar2=None, op0=ALU.mult),
                      reads=[G_bb, sm_b], writes=[t1_b])
                tk.op("dve", lambda e: e.scalar_tensor_tensor(out=tab[:, j, 0:n], in0=Gc[:, j, 0:n], scalar=S0, in1=t1[:, 0:n],
                                                               op0=ALU.mult, op1=ALU.add), reads=[G_bb, sm_b, t1_b], writes=[pos_b])
                tk.op("dve", lambda e: e.tensor_scalar(out=t1[:, 0:n], in0=Gs[:, j, 0:n], scalar1=sS0, scalar2=None, op0=ALU.mult),
                      reads=[G_bb, sm_b], writes=[t1_b])
                tk.op("dve", lambda e: e.scalar_tensor_tensor(out=tab[:, 2 + j, 0:n], in0=Gc[:, j, 0:n], scalar=C0, in1=t1[:, 0:n],
                                                               op0=ALU.mult, op1=ALU.subtract), reads=[G_bb, sm_b, t1_b], writes=[pos_b])

    build_pos()

    def build_clam():
        e_ = sb("lam_e", [128, 20], F32); t_ = sb("lam_t", [128, 20], F32); lb = tk.buf("lamtmp")
        lo = VOFF["lam"][0]
        tk.op("act", lambda e: e.activation(out=e_, in_=vecs[:, lo:lo + 20], func=AF.Exp, scale=-1.0), reads=[vecs_b], writes=[lb])
        tk.op("dve", lambda e: e.tensor_scalar(out=t_, in0=e_, scalar1=-0.25, scalar2=1.0 / 3.0, op0=ALU.mult, op1=ALU.add), reads=[lb], writes=[lb])
        tk.op("dve", lambda e: e.tensor_tensor(out=t_, in0=t_, in1=e_, op=ALU.mult), reads=[lb], writes=[lb])
        tk.op("dve", lambda e: e.tensor_scalar(out=t_, in0=t_, scalar1=-0.5, scalar2=None, op0=ALU.add), reads=[lb], writes=[lb])
        tk.op("dve", lambda e: e.tensor_tensor(out=t_, in0=t_, in1=e_, op=ALU.mult), reads=[lb], writes=[lb])
        tk.op("dve", lambda e: e.tensor_scalar(out=t_, in0=t_, scalar1=1.0, scalar2=None, op0=ALU.add), reads=[lb], writes=[lb])
        tk.op("dve", lambda e: e.tensor_tensor(out=t_, in0=t_, in1=e_, op=ALU.mult), reads=[lb], writes=[lb])
        tk.op("dve", lambda e: e.tensor_scalar(out=clam[:, 0, :], in0=t_, scalar1=-8.0, scalar2=None, op0=ALU.mult), reads=[lb], parts=[clam_b])
        tk.op("dve", lambda e: e.tensor_scalar(out=clam[:, 1, :], in0=t_, scalar1=-16.0, scalar2=None, op0=ALU.mult), reads=[lb], parts=[clam_b])

    build_clam()

    def build_mod():
        scv = sb("silu_c", [128, 16], F32); scv_b = tk.buf("silu_c")
        wst = [sb("wmst%d" % i, [128, KC, 512], F32) for i in range(2)]
        wst_b = [tk.buf("wmst%d" % i) for i in range(2)]
        modfm = sb("modfm", [128, 72, 2], F32); modfm_b = tk.buf("modfm")
        co = VOFF["c2"][0]
        tk.op("act", lambda e: e.activation(out=scv, in_=vecs[:, co:co + 16], func=AF.Silu), reads=[vecs_b], writes=[scv_b])
        wm_v = wmod_d.rearrange("(kc p) o -> p kc o", p=128)
        pb = psring.next()
        first = True
        for nb in range(18):
            i = nb % 2
            tk.dma("sp", wst[i], wm_v[:, :, nb * 512:(nb + 1) * 512], wst_b[i], writes=[wst_b[i]])
            for mi in range(4):
                m = nb * 4 + mi
                for kc in range(KC):
                    tk.op("pe", lambda e: e.matmul(psum[pb][:, 2 * m:2 * m + 2], lhsT=wst[i][:, kc, mi * 128:(mi + 1) * 128],
                                                   rhs=scv[:, 2 * kc:2 * kc + 2], start=(kc == 0), stop=(kc == KC - 1)),
                          reads=[scv_b, wst_b[i]], **({"writes": [ps_b[pb]]} if first else {"parts": [ps_b[pb]]}))
                    first = False
        bo = VOFF["b_mod"][0]
        tk.op("dve", lambda e: e.tensor_tensor(out=modfm, in0=psum[pb][:, 0:144].rearrange("p (j t) -> p j t", t=2),
                                               in1=vecs[:, bo:bo + 72].unsqueeze(2).to_broadcast([128, 72, 2]), op=ALU.add),
              reads=[ps_b[pb], vecs_b], writes=[modfm_b])
        for n_i, (gname, i_sh, i_sc, i_gate, gscale) in enumerate((("g_n1", 0, 1, 2, 0.5), ("g_n2", 3, 4, 5, 1.0), ("g_n3", 6, 7, 8, 0.5))):
            go = VOFF[gname][0]
            base = 3 * n_i
            tk.op("dve", lambda e: e.tensor_scalar(out=modp[:, base, :, :], in0=modfm[:, i_sc * 8:(i_sc + 1) * 8, :], scalar1=1.0, scalar2=None,
                                                    op0=ALU.add), reads=[modfm_b], parts=[modp_b])
            tk.op("dve", lambda e: e.tensor_tensor(out=modp[:, base, :, :], in0=modp[:, base, :, :],
                                                    in1=vecs[:, go:go + 8].unsqueeze(2).to_broadcast([128, 8, 2]), op=ALU.mult),
                  reads=[modp_b, vecs_b], writes=[modp_b])
            tk.op("dve", lambda e: e.tensor_copy(out=modp[:, base + 1, :, :], in_=modfm[:, i_sh * 8:(i_sh + 1) * 8, :]),
                  reads=[modfm_b], parts=[modp_b])
            tk.op("dve", lambda e: e.tensor_scalar(out=modp[:, base + 2, :, :], in0=modfm[:, i_gate * 8:(i_gate + 1) * 8, :], scalar1=gscale,
                                                    scalar2=None, op0=ALU.mult), reads=[modfm_b], parts=[modp_b])

    if stop_after == "pos":
        return finish_early([("TR", TR), ("TC", TC), ("clam", clam)])
    build_mod()
    if stop_after == "mod":
        return finish_early([("TR", TR), ("TC", TC), ("clam", clam), ("modp", modp)])
    end_phase()
    begin_phase()

    tk.dma("sp", U_d[:, :, 0:15].rearrange("k p n -> p k n"), zt[:, 0:KC, 0:15], zt_b, reads=[zt_b], parts=[U_b])
    tk.dma("sp", UX_d[:, :, 0:2].rearrange("k p n -> p k n"), zt[:, :, 0:2], zt_b, reads=[zt_b], parts=[UX_b])
    tk.dma("sp", CUX_d[:, :, 0:2].rearrange("k p n -> p k n"), zt[:, :, 0:2], zt_b, reads=[zt_b], parts=[CUX_b])
    tk.dma("sp", CUX_d[:, :, CTX + 2:CTX + 4].rearrange("k p n -> p k n"), zt[:, :, 0:2], zt_b, reads=[zt_b], parts=[CUX_b])

    class WStream:
        def __init__(self, seq, depth=4):
            self.seq = seq
            self.depth = depth
            self.issued = 0
            self.used = 0
            self.slot_of = {}

        def _issue(self):
            blk, nel = self.seq[self.issued]
            s = self.issued % 5
            tk.dma("sp", wring_t[s][:, 0:nel], wb_ap(blk)[:, 0:nel], wring_b[s], reads=[wb_bufs[blk]], writes=[wring_b[s]])
            self.issued += 1

        def next(self, blk):
            assert self.seq[self.used][0] == blk, (self.seq[self.used], blk)
            while self.issued < len(self.seq) and self.issued <= self.used + self.depth - 1:
                self._issue()
            s = self.used % 5
            self.used += 1
            return wring_t[s], wring_b[s]

    def rms_to_h(xf, xf_b, h, h_b, sq, sq_b, N, gm, sh):
        tk.op("act", lambda e: e.activation(out=sq[:, :, 0:N], in_=xf[:, :, 0:N], func=AF.Square), reads=[xf_b], writes=[sq_b])
        pb = psring.next()
        for kc in range(KC):
            tk.op("pe", lambda e: e.matmul(psum[pb][:, 0:N], lhsT=ones_bf, rhs=sq[:, kc, 0:N], start=(kc == 0), stop=(kc == KC - 1)),
                  reads=[ones_b, sq_b], **({"writes": [ps_b[pb]]} if kc == 0 else {"parts": [ps_b[pb]]}))
        tk.op("act", lambda e: e.activation(out=small[:, 0, 0:N], in_=psum[pb][:, 0:N], func=AF.Sqrt, scale=1.0 / D, bias=consts[:, 0:1]),
              reads=[ps_b[pb], consts_b], writes=[rs_b])
        pr = psring.next()
        tk.op("dve", lambda e: e.reciprocal(out=psum[pr][:, 0:N], in_=small[:, 0, 0:N]), reads=[rs_b], writes=[ps_b[pr]])
        for kc in range(KC):
            ti = tmpring.next()
            tk.op("dve", lambda e: e.tensor_tensor(out=tmpg[ti][:, 0, 0:N], in0=xf[:, kc, 0:N], in1=psum[pr][:, 0:N], op=ALU.mult),
                  reads=[xf_b, ps_b[pr]], writes=[tmpg_b[ti]])
            tk.op("act", lambda e: e.activation(out=h[:, kc, 0:N], in_=tmpg[ti][:, 0, 0:N], func=AF.Identity,
                                                scale=gm[:, kc:kc + 1], bias=sh[:, kc:kc + 1]),
                  reads=[tmpg_b[ti], modp_b], parts=[h_b])
        return pr

    def ffn(ws, f, xf, xf_b, h, h_b, hid, hid_b, N, gate):
        for j in range(11):
            wt, wb = ws.next(BLK["UP%d" % (f + 1)] + j)
            w3 = wt[:, 0:KC * 512].rearrange("p (k o) -> p k o", o=512)
            pbs = [psring.next() for _ in range(4)]
            for mi in range(4):
                pb = pbs[mi]
                for kc in range(KC):
                    tk.op("pe", lambda e: e.matmul(psum[pb][:, 0:N], lhsT=w3[:, kc, mi * 128:(mi + 1) * 128], rhs=h[:, kc, 0:N],
                                                   start=(kc == 0), stop=(kc == KC - 1)),
                          reads=[wb, h_b], **({"writes": [ps_b[pb]]} if kc == 0 else {"parts": [ps_b[pb]]}))
            ti = tmpring.next()
            for i in range(2):
                tk.op("act", lambda e: e.activation(out=tmpg[ti][:, i, 0:N], in_=psum[pbs[i]][:, 0:N], func=AF.Silu),
                      reads=[ps_b[pbs[i]]], **({"writes": [tmpg_b[ti]]} if i == 0 else {"parts": [tmpg_b[ti]]}))
            for i in range(2):
                tk.op("dve", lambda e: e.tensor_tensor(out=hid[:, 2 * j + i, 0:N], in0=psum[pbs[2 + i]][:, 0:N], in1=tmpg[ti][:, i, 0:N], op=ALU.mult),
                      reads=[ps_b[pbs[2 + i]], tmpg_b[ti]], parts=[hid_b])
        for m in range(8):
            wt, wb = ws.next(BLK["DN%d" % (f + 1)] + m)
            w3 = wt[:, 0:22 * 128].rearrange("p (k o) -> p k o", o=128)
            pb = psring.next()
            for kc in range(22):
                tk.op("pe", lambda e: e.matmul(psum[pb][:, 0:N], lhsT=w3[:, kc, :], rhs=hid[:, kc, 0:N], start=(kc == 0), stop=(kc == 21)),
                      reads=[wb, hid_b], **({"writes": [ps_b[pb]]} if kc == 0 else {"parts": [ps_b[pb]]}))
            tk.op("dve", lambda e: e.scalar_tensor_tensor(out=xf[:, m, 0:N], in0=psum[pb][:, 0:N], scalar=gate[:, m:m + 1], in1=xf[:, m, 0:N],
                                                           op0=ALU.mult, op1=ALU.add),
                  reads=[ps_b[pb], modp_b, xf_b], writes=[xf_b])

    def up_seq(f):
        return [(BLK["UP%d" % (f + 1)] + j, KC * 512) for j in range(11)] + [(BLK["DN%d" % (f + 1)] + m, 22 * 128) for m in range(8)]

    IN_NEL = [KC * 512] * 4 + [KC * 512, KC * 512, KC * 256] * 2 + [KC * 512] * 4

    begin_phase()
    alloc_ring("A")
    xin = [sb("xin%d" % i, [128, D], F32) for i in range(2)]
    xin_b = [tk.buf("xin%d" % i) for i in range(2)]
    xinring = Ring([0, 1])
    xfm = [sb("xfm%d" % i, [128, KC, NT], F32) for i in range(2)]
    xfm_b = [tk.buf("xfm%d" % i) for i in range(2)]
    hA = sb("hA", [128, KC, NT], BF16); hA_b = tk.buf("hA")
    sqA = sb("sqA", [128, KC, NT], BF16); sqA_b = tk.buf("sqA")
    hidA = sb("hidA", [128, 22, NT], BF16); hidA_b = tk.buf("hidA")
    ust = sb("ust", [128, KC, NT], BF16); ust_b = tk.buf("ust")
    uxst = sb("uxst", [128, LC, NT], BF16); uxst_b = tk.buf("uxst")
    gst = sb("gst", [128, LC, NT], BF16); gst_b = tk.buf("gst")
    gclst = sb("gclst", [128, 16, NT], BF16); gclst_b = tk.buf("gclst")

    tilesA = [("ctx", 0, CTX)] + [("lat", i * NT, NT) for i in range(LT // NT)] + [("halo", LT, HALO)]
    if tilesA_override is not None:
        tilesA = tilesA_override
    seqA = []
    for kind, t0, N in tilesA:
        seqA += up_seq(0)
        if kind == "lat":
            seqA += [(BLK["IN"] + j, IN_NEL[j]) for j in range(14)]
        elif kind == "halo":
            seqA += [(BLK["IN"] + j, IN_NEL[j]) for j in range(7)]
        else:
            seqA += [(BLK["IN"] + j, IN_NEL[j]) for j in range(4, 7)]
    wsA = WStream(seqA)
    bin_o = VOFF["b_in"][0]

    for ti_, (kind, t0, N) in enumerate(tilesA):
        mj = 1 if kind == "ctx" else 0
        xf = xfm[ti_ % 2]; xf_b = xfm_b[ti_ % 2]
        src = ctx_d if kind == "ctx" else x_d
        nsub = max(1, N // 128)
        sw = min(N, 128)
        for s in range(nsub):
            xi = xinring.next()
            tk.dma("sp", xin[xi][0:sw, :], src[t0 + s * 128:t0 + s * 128 + sw, :], xin_b[xi], writes=[xin_b[xi]])
            for half in range(2):
                pb = psring.next()
                for q in range(4):
                    kc = half * 4 + q
                    tk.op("pe", lambda e: e.transpose(out=psum[pb][:, q * 128:q * 128 + sw], in_=xin[xi][0:sw, kc * 128:(kc + 1) * 128],
                                                      identity=ident[0:sw, 0:sw]),
                          reads=[xin_b[xi], ident_b], **({"writes": [ps_b[pb]]} if q == 0 else {"parts": [ps_b[pb]]}))
                pv = psum[pb][:, :].rearrange("p (q n) -> p q n", n=128)[:, :, 0:sw]
                ov = xf[:, half * 4:half * 4 + 4, s * 128:s * 128 + sw]
                if kind == "ctx":
                    tk.op("dve", lambda e: e.tensor_copy(out=ov, in_=pv), reads=[ps_b[pb]], parts=[xf_b])
                else:
                    tl = t0 + s * 128
                    a0 = tl // 64
                    nr = max(1, sw // 64)
                    ncol = min(sw, 64)
                    if half == 0:
                        posv = TR[:, :, a0:a0 + nr].unsqueeze(3).to_broadcast([128, 4, nr, ncol])
                    else:
                        posv = TC[:, :, 0:ncol].unsqueeze(2).to_broadcast([128, 4, nr, ncol])
                    tk.op("dve", lambda e: e.tensor_tensor(out=ov.rearrange("p q (r c) -> p q r c", c=ncol),
                                                           in0=pv.rearrange("p q (r c) -> p q r c", c=ncol), in1=posv, op=ALU.add),
                          reads=[ps_b[pb], pos_b], parts=[xf_b])
        rms_to_h(xf, xf_b, hA, hA_b, sqA, sqA_b, N, modp[:, 0, :, mj], modp[:, 1, :, mj])
        ffn(wsA, 0, xf, xf_b, hA, hA_b, hidA, hidA_b, N, modp[:, 2, :, mj])
        if kind == "lat":
            tk.dma("sp", X1_d[:, :, t0:t0 + N].rearrange("k p n -> p k n"), xf[:, :, 0:N], xf_b, reads=[xf_b], parts=[X1_b])
        rms_to_h(xf, xf_b, hA, hA_b, sqA, sqA_b, N, modp[:, 3, :, mj], modp[:, 4, :, mj])
        blocks = list(range(14)) if kind == "lat" else (list(range(7)) if kind == "halo" else [4, 5, 6])
        for j in blocks:
            wt, wb = wsA.next(BLK["IN"] + j)
            ncols = IN_NEL[j] // KC
            nch = ncols // 128
            w3 = wt[:, 0:IN_NEL[j]].rearrange("p (k o) -> p k o", o=ncols)
            pbs = [psring.next() for _ in range(nch)]
            for mi in range(nch):
                pb = pbs[mi]
                for kc in range(KC):
                    tk.op("pe", lambda e: e.matmul(psum[pb][:, 0:N], lhsT=w3[:, kc, mi * 128:(mi + 1) * 128], rhs=hA[:, kc, 0:N],
                                                   start=(kc == 0), stop=(kc == KC - 1)),
                          reads=[wb, hA_b], **({"writes": [ps_b[pb]]} if kc == 0 else {"parts": [ps_b[pb]]}))
            if j < 4:
                ti = tmpring.next()
                for i in range(2):
                    bc = bin_o + 8 + 2 * j + i
                    tk.op("act", lambda e: e.activation(out=tmpg[ti][:, i, 0:N], in_=psum[pbs[2 + i]][:, 0:N], func=AF.Sigmoid,
                                                        bias=vecs[:, bc:bc + 1], scale=1.0),
                          reads=[ps_b[pbs[2 + i]], vecs_b], **({"writes": [tmpg_b[ti]]} if i == 0 else {"parts": [tmpg_b[ti]]}))
                for i in range(2):
                    bc = bin_o + 2 * j + i
                    tk.op("dve", lambda e: e.scalar_tensor_tensor(out=ust[:, 2 * j + i, 0:N], in0=psum[pbs[i]][:, 0:N], scalar=vecs[:, bc:bc + 1],
                                                                   in1=tmpg[ti][:, i, 0:N], op0=ALU.add, op1=ALU.mult),
                          reads=[ps_b[pbs[i]], vecs_b, tmpg_b[ti]], parts=[ust_b])
            elif j < 7:
                for mi in range(nch):
                    cc = (j - 4) * 4 + mi
                    bc = bin_o + 16 + cc
                    tk.op("dve", lambda e: e.tensor_scalar(out=uxst[:, cc, 0:N], in0=psum[pbs[mi]][:, 0:N], scalar1=vecs[:, bc:bc + 1],
                                                           scalar2=None, op0=ALU.add),
                          reads=[ps_b[pbs[mi]], vecs_b], parts=[uxst_b])
            elif j < 10:
                for mi in range(nch):
                    cc = (j - 7) * 4 + mi
                    bc = bin_o + 26 + cc
                    tk.op("act", lambda e: e.activation(out=gst[:, cc, 0:N], in_=psum[pbs[mi]][:, 0:N], func=AF.Gelu_apprx_tanh,
                                                        bias=vecs[:, bc:bc + 1], scale=1.0),
                          reads=[ps_b[pbs[mi]], vecs_b], parts=[gst_b])
            else:
                for mi in range(nch):
                    c16 = (j - 10) * 4 + mi
                    bc = bin_o + 36 + c16
                    tk.op("act", lambda e: e.activation(out=gclst[:, c16, 0:N], in_=psum[pbs[mi]][:, 0:N], func=AF.Sigmoid,
                                                        bias=vecs[:, bc:bc + 1], scale=1.0),
                          reads=[ps_b[pbs[mi]], vecs_b], parts=[gclst_b])
        if kind == "ctx":
            tk.dma("sp", CUX_d[:, :, 2:2 + N].rearrange("k p n -> p k n"), uxst[:, :, 0:N], uxst_b, reads=[uxst_b], parts=[CUX_b])
        else:
            tk.dma("sp", U_d[:, :, 15 + t0:15 + t0 + N].rearrange("k p n -> p k n"), ust[:, :, 0:N], ust_b, reads=[ust_b], parts=[U_b])
            tk.dma("sp", UX_d[:, :, 2 + t0:2 + t0 + N].rearrange("k p n -> p k n"), uxst[:, :, 0:N], uxst_b, reads=[uxst_b], parts=[UX_b])
            if kind == "lat":
                tk.dma("sp", G_d[:, :, t0:t0 + N].rearrange("k p n -> p k n"), gst[:, :, 0:N], gst_b, reads=[gst_b], parts=[G_b])
                tk.dma("sp", GCL_d[:, :, t0:t0 + N].rearrange("k p n -> p k n"), gclst[:, :, 0:N], gclst_b, reads=[gclst_b], parts=[GCL_b])

    end_phase()

    dbg = {}
    if debug:
        for nm, ap_ in (("X1", X1_d), ("U", U_d), ("UX", UX_d), ("CUX", CUX_d), ("G", G_d), ("GCL", GCL_d)):
            o = nc.dram_tensor("dbg_" + nm, list(ap_.shape), ap_.dtype, kind="ExternalOutput").ap()
            dbg[nm] = o
            b_ = tk.buf("dbg_" + nm)
            tk.dma("sp", o, ap_, b_, writes=[b_])
        for nm, ap_ in (("modp", modp), ("TR", TR), ("TC", TC), ("clam", clam)):
            o = nc.dram_tensor("dbg_" + nm, list(ap_.shape), ap_.dtype, kind="ExternalOutput").ap()
            b_ = tk.buf("dbg_" + nm)
            tk.dma("sp", o, ap_, b_, writes=[b_])
    if stop_after == "A":
        z = sb("zout", [128, D], F32); z_b = tk.buf("zout")
        tk.op("dve", lambda e: e.memset(z, 0.0), writes=[z_b])
        for i in range(LT // 128):
            tk.dma("sp", out_d[i * 128:(i + 1) * 128, :], z, z_b, reads=[z_b], parts=[out_b])
        tk.barrier()
        return nc


    begin_phase()
    uxb = sb("uxb", [128, WX], BF16); uxb_b = tk.buf("uxb")
    xr32 = sb("xr32", [128, LT], F32); xr32_b = tk.buf("xr32")
    xrb = sb("xrb", [128, LT], BF16); xrb_b = tk.buf("xrb")
    Rt = sb("Rt", [128, LT], F32); Rt_b = tk.buf("Rt")
    A2t = sb("A2t", [128, LT], F32); A2t_b = tk.buf("A2t")
    IGt = sb("IGt", [128, LT], F32); IGt_b = tk.buf("IGt")
    Ht = sb("Ht", [128, LT], F32); Ht_b = tk.buf("Ht")
    gw32 = sb("gw32", [128, 4, 128], F32); gw32_b = tk.buf("gw32")
    gwb = sb("gwb", [128, LC, 4, 128], BF16); gwb_b = tk.buf("gwb")
    dgl = sb("dgl", [128, 5, 128], BF16); dgl_b = tk.buf("dgl")
    dgc = sb("dgc", [128, 31, 128], BF16); dgc_b = tk.buf("dgc")
    ub = sb("ub", [128, WU], BF16); ub_b = tk.buf("ub")
    cst = sb("cst", [128, LT], F32); cst_b = tk.buf("cst")
    cux = sb("cux", [128, CTX + 4], BF16); cux_b = tk.buf("cux")
    gath = sb("gath", [128, NCORES, LC], F32); gath_b = tk.buf("gath")
    wlc_o = VOFF["w_lc"][0]; blc_o = VOFF["b_lc"][0]; ba_o = VOFF["b_a"][0]; bx_o = VOFF["b_x"][0]
    wdw_o = VOFF["w_dw"][0]; bdw_o = VOFF["b_dw"][0]

    def lru_conv(src, src_b, Ntok, cc):
        for t0 in range(0, Ntok, NT):
            n = min(NT, Ntok - t0)
            pb = psring.next()
            for tap in range(5):
                tk.op("pe", lambda e: e.matmul(psum[pb][:, 0:n], lhsT=dgl[:, tap, :], rhs=src[:, t0 + tap:t0 + tap + n],
                                               start=(tap == 0), stop=(tap == 4)),
                      reads=[dgl_b, src_b], **({"writes": [ps_b[pb]]} if tap == 0 else {"parts": [ps_b[pb]]}))
            tk.op("act", lambda e: e.activation(out=xr32[:, t0:t0 + n], in_=psum[pb][:, 0:n], func=AF.Identity,
                                                bias=vecs[:, blc_o + cc:blc_o + cc + 1], scale=1.0),
                  reads=[ps_b[pb], vecs_b], parts=[xr32_b])
            tk.op("act", lambda e: e.activation(out=xrb[:, t0:t0 + n], in_=psum[pb][:, 0:n], func=AF.Identity,
                                                bias=vecs[:, blc_o + cc:blc_o + cc + 1], scale=1.0),
                  reads=[ps_b[pb], vecs_b], parts=[xrb_b])

    def lru_dir(Ntok, d, cc, init, reverse):
        lru_gates(Ntok, d, cc)
        lru_elem(Ntok, d, cc, init, reverse)

    def lru_gates(Ntok, d, cc):
        for t0 in range(0, Ntok, NT):
            n = min(NT, Ntok - t0)
            pr_ = psring.next(); pi_ = psring.next()
            tk.op("pe", lambda e: e.matmul(psum[pr_][:, 0:n], lhsT=gwb[:, cc, 2 * d, :], rhs=xrb[:, t0:t0 + n], start=True, stop=True),
                  reads=[gwb_b, xrb_b], writes=[ps_b[pr_]])
            tk.op("pe", lambda e: e.matmul(psum[pi_][:, 0:n], lhsT=gwb[:, cc, 2 * d + 1, :], rhs=xrb[:, t0:t0 + n], start=True, stop=True),
                  reads=[gwb_b, xrb_b], writes=[ps_b[pi_]])
            tk.op("act", lambda e: e.activation(out=Rt[:, t0:t0 + n], in_=psum[pr_][:, 0:n], func=AF.Sigmoid,
                                                bias=vecs[:, ba_o + d * 10 + cc:ba_o + d * 10 + cc + 1], scale=1.0),
                  reads=[ps_b[pr_], vecs_b], parts=[Rt_b])
            tk.op("act", lambda e: e.activation(out=IGt[:, t0:t0 + n], in_=psum[pi_][:, 0:n], func=AF.Sigmoid,
                                                bias=vecs[:, bx_o + d * 10 + cc:bx_o + d * 10 + cc + 1], scale=1.0),
                  reads=[ps_b[pi_], vecs_b], parts=[IGt_b])

    def lru_elem(Ntok, d, cc, init, reverse):
        k = d * 10 + cc
        tk.op("act", lambda e: e.activation(out=A2t[:, 0:Ntok], in_=Rt[:, 0:Ntok], func=AF.Exp, scale=clam[:, 1, k:k + 1]),
              reads=[Rt_b, clam_b], writes=[A2t_b])
        tk.op("act", lambda e: e.activation(out=Rt[:, 0:Ntok], in_=Rt[:, 0:Ntok], func=AF.Exp, scale=clam[:, 0, k:k + 1]),
              reads=[Rt_b, clam_b], writes=[Rt_b])
        tk.op("act", lambda e: e.activation(out=A2t[:, 0:Ntok], in_=A2t[:, 0:Ntok], func=AF.Sqrt, scale=-1.0, bias=consts[:, 1:2]),
              reads=[A2t_b, consts_b], writes=[A2t_b])
        tk.op("pool", lambda e: e.tensor_tensor(out=IGt[:, 0:Ntok], in0=IGt[:, 0:Ntok], in1=A2t[:, 0:Ntok], op=ALU.mult),
              reads=[IGt_b, A2t_b], writes=[IGt_b])
        tk.op("dve", lambda e: e.tensor_tensor(out=IGt[:, 0:Ntok], in0=IGt[:, 0:Ntok], in1=xr32[:, 0:Ntok], op=ALU.mult),
              reads=[IGt_b, xr32_b], writes=[IGt_b])
        if reverse:
            tk.op("dve", lambda e: e.tensor_tensor_scan(out=Ht[:, 0:Ntok][:, ::-1], data0=Rt[:, 0:Ntok][:, ::-1], data1=IGt[:, 0:Ntok][:, ::-1],
                                                        initial=init, op0=ALU.mult, op1=ALU.add),
                  reads=[Rt_b, IGt_b, carry_b, s0_b], writes=[Ht_b])
        else:
            tk.op("dve", lambda e: e.tensor_tensor_scan(out=Ht[:, 0:Ntok], data0=Rt[:, 0:Ntok], data1=IGt[:, 0:Ntok],
                                                        initial=init, op0=ALU.mult, op1=ALU.add),
                  reads=[Rt_b, IGt_b, carry_b, s0_b], writes=[Ht_b])

    def build_dgl(cc):
        for tap in range(5):
            col = wlc_o + tap * 10 + cc
            tk.op("act", lambda e: e.activation(out=dgl[:, tap, :], in_=ident, func=AF.Identity, scale=vecs[:, col:col + 1], bias=consts[:, 3:4]),
                  reads=[ident_b, vecs_b, consts_b], **({"writes": [dgl_b]} if tap == 0 else {"parts": [dgl_b]}))

    for cc in range(LC):
        tk.dma("sp", gw32, wg_d[cc], gw32_b, writes=[gw32_b])
        tk.op("dve", lambda e: e.tensor_copy(out=gwb[:, cc, :, :], in_=gw32), reads=[gw32_b], parts=[gwb_b])
        tk.dma("sp", cux, CUX_d[cc], cux_b, reads=[CUX_b], writes=[cux_b])
        tk.dma("sp", uxb, UX_d[cc], uxb_b, reads=[UX_b], writes=[uxb_b])
        do_conf = cc < KC
        if do_conf:
            kc = cc
            tk.dma("sp", ub, U_d[kc], ub_b, reads=[U_b], writes=[ub_b])
        build_dgl(cc)
        lru_conv(cux, cux_b, CTX, cc)
        lru_dir(CTX, 0, cc, 0.0, False)
        tk.op("dve", lambda e: e.tensor_copy(out=s0[:, cc:cc + 1], in_=Ht[:, CTX - 1:CTX]), reads=[Ht_b], parts=[s0_b])
        if do_conf:
            for tap in range(31):
                col = wdw_o + tap * 8 + kc
                if tap % 2 == 0:
                    tk.op("act", lambda e: e.activation(out=dgc[:, tap, :], in_=ident, func=AF.Identity, scale=vecs[:, col:col + 1], bias=consts[:, 3:4]),
                          reads=[ident_b, vecs_b, consts_b], **({"writes": [dgc_b]} if tap == 0 else {"parts": [dgc_b]}))
                else:
                    tk.op("pool", lambda e: e.tensor_scalar(out=dgc[:, tap, :], in0=ident, scalar1=vecs[:, col:col + 1], scalar2=None, op0=ALU.mult),
                          reads=[ident_b, vecs_b], parts=[dgc_b])
        lru_conv(uxb, uxb_b, LT, cc)
        lru_gates(LT, 0, cc)
        cbanks = []
        if do_conf:
            for ti in range(LT // NT):
                t0 = ti * NT
                pb = psring.next()
                cbanks.append(pb)
                for tap in range(31):
                    tk.op("pe", lambda e: e.matmul(psum[pb][:, :], lhsT=dgc[:, tap, :], rhs=ub[:, t0 + tap:t0 + tap + NT],
                                                   start=(tap == 0), stop=(tap == 30)),
                          reads=[dgc_b, ub_b], **({"writes": [ps_b[pb]]} if tap == 0 else {"parts": [ps_b[pb]]}))
        lru_elem(LT, 0, cc, s0[:, cc:cc + 1], False)
        tk.op("dve", lambda e: e.tensor_copy(out=sfin[:, cc:cc + 1], in_=Ht[:, LT - 1:LT]), reads=[Ht_b], parts=[sfin_b])
        tk.dma("sp", HF_d[cc], Ht, Ht_b, reads=[Ht_b], parts=[HF_b])
        if do_conf:
            for ti, pb in enumerate(cbanks):
                t0 = ti * NT
                if ti % 2 == 0:
                    tk.op("act", lambda e: e.activation(out=cst[:, t0:t0 + NT], in_=psum[pb][:, :], func=AF.Identity,
                                                        bias=vecs[:, bdw_o + kc:bdw_o + kc + 1], scale=1.0),
                          reads=[ps_b[pb], vecs_b], parts=[cst_b])
                else:
                    tk.op("dve", lambda e: e.tensor_scalar(out=cst[:, t0:t0 + NT], in0=psum[pb][:, :], scalar1=vecs[:, bdw_o + kc:bdw_o + kc + 1],
                                                           scalar2=None, op0=ALU.add),
                          reads=[ps_b[pb], vecs_b], parts=[cst_b])
            tk.dma("sp", C_d[kc], cst, cst_b, reads=[cst_b], parts=[C_b])

    if stop_after == "B1":
        return finish_early([])
    tk.dma("sp", cci_d, sfin, sfin_b, reads=[sfin_b], writes=[cci_b])
    tk.coll(lambda e: e.collective_compute("AllGather", ALU.bypass, replica_groups=[list(range(NCORES))],
                                           ins=[cci_d.opt()], outs=[cco_d.opt()]), reads=[cci_b], writes=[cco_b])
    tk.dma("sp", gath, cco_d.rearrange("(r p) c -> p r c", p=128), gath_b, reads=[cco_b], writes=[gath_b])
    pm_o = VOFF["pmask"][0]
    tk.op("dve", lambda e: e.tensor_scalar(out=carry, in0=gath[:, 0, :], scalar1=vecs[:, pm_o:pm_o + 1], scalar2=None, op0=ALU.mult),
          reads=[gath_b, vecs_b], writes=[carry_b])
    for r_ in range(1, NCORES):
        tk.op("dve", lambda e: e.scalar_tensor_tensor(out=carry, in0=gath[:, r_, :], scalar=vecs[:, pm_o + r_:pm_o + r_ + 1], in1=carry,
                                                       op0=ALU.mult, op1=ALU.add), reads=[gath_b, vecs_b, carry_b], writes=[carry_b])

    if stop_after == "BX":
        return finish_early([])
    for cc in range(LC):
        tk.dma("sp", uxb, UX_d[cc], uxb_b, reads=[UX_b], writes=[uxb_b])
        build_dgl(cc)
        tk.dma("sp", cst, HF_d[cc], cst_b, reads=[HF_b], writes=[cst_b])
        tk.dma("sp", ub[:, 0:LT], G_d[cc], ub_b, reads=[G_b], writes=[ub_b])
        lru_conv(uxb, uxb_b, LT, cc)
        lru_dir(LT, 1, cc, carry[:, cc:cc + 1], True)
        tk.op("pool", lambda e: e.tensor_tensor(out=Ht, in0=Ht, in1=cst, op=ALU.add), reads=[Ht_b, cst_b], writes=[Ht_b])
        tk.op("dve", lambda e: e.tensor_tensor(out=xrb, in0=Ht, in1=ub[:, 0:LT], op=ALU.mult), reads=[Ht_b, ub_b], writes=[xrb_b])
        tk.dma("sp", M_d[cc], xrb, xrb_b, reads=[xrb_b], parts=[M_b])
    end_phase()
    if stop_after == "B":
        return finish_early([("M", M_d), ("C", C_d), ("HF", HF_d)])
    if stop_after == "B0":
        return finish_early([])

    begin_phase()
    alloc_ring("C")
    c32 = sb("c32", [128, KC, NT], F32); c32_b = tk.buf("c32")
    xfC = sb("xfC", [128, KC, NT], F32); xfC_b = tk.buf("xfC")
    B1 = sb("B1", [128, KC, NT], BF16); B1_b = tk.buf("B1")
    B2 = sb("B2", [128, KC, NT], BF16); B2_b = tk.buf("B2")
    hidC = sb("hidC", [128, 22, NT], BF16); hidC_b = tk.buf("hidC")
    gcl = sb("gcl", [128, 16, NT], BF16); gcl_b = tk.buf("gcl")
    mrg = sb("mrg", [128, LC, NT], BF16); mrg_b = tk.buf("mrg")
    gcy = sb("gcy", [128, KC, NT], F32); gcy_b = tk.buf("gcy")
    ost = [sb("ost%d" % i, [128, D], F32) for i in range(2)]
    ost_b = [tk.buf("ost%d" % i) for i in range(2)]
    gln_o = VOFF["g_ln"][0]; bln_o = VOFF["b_ln"][0]; gf_o = VOFF["g_final"][0]
    seqC = []
    for i in range(LT // NT):
        seqC += [(BLK["CO"] + j, KC * 512) for j in range(2)] + [(BLK["LO"] + j, LC * 256) for j in range(4)]
        seqC += [(BLK["WO"] + j, KC * 512) for j in range(2)] + up_seq(1)
    wsC = WStream(seqC)
    for i in range(LT // NT):
        t0 = i * NT
        N = NT
        tk.dma("sp", c32, C_d[:, :, t0:t0 + N].rearrange("k p n -> p k n"), c32_b, reads=[C_b], writes=[c32_b])
        tk.dma("sp", gcl, GCL_d[:, :, t0:t0 + N].rearrange("k p n -> p k n"), gcl_b, reads=[GCL_b], writes=[gcl_b])
        tk.dma("sp", mrg, M_d[:, :, t0:t0 + N].rearrange("k p n -> p k n"), mrg_b, reads=[M_b], writes=[mrg_b])
        tk.dma("sp", xfC, X1_d[:, :, t0:t0 + N].rearrange("k p n -> p k n"), xfC_b, reads=[X1_b], writes=[xfC_b])
        tk.op("dve", lambda e: e.tensor_copy(out=B1, in_=c32), reads=[c32_b], writes=[B1_b])
        tk.op("act", lambda e: e.activation(out=B2, in_=c32, func=AF.Square), reads=[c32_b], writes=[B2_b])
        p1 = psring.next(); p2 = psring.next()
        for kc in range(KC):
            tk.op("pe", lambda e: e.matmul(psum[p1][:, :], lhsT=ones_bf, rhs=B1[:, kc, :], start=(kc == 0), stop=(kc == KC - 1)),
                  reads=[ones_b, B1_b], **({"writes": [ps_b[p1]]} if kc == 0 else {"parts": [ps_b[p1]]}))
        for kc in range(KC):
            tk.op("pe", lambda e: e.matmul(psum[p2][:, :], lhsT=ones_bf, rhs=B2[:, kc, :], start=(kc == 0), stop=(kc == KC - 1)),
                  reads=[ones_b, B2_b], **({"writes": [ps_b[p2]]} if kc == 0 else {"parts": [ps_b[p2]]}))
        tk.op("dve", lambda e: e.tensor_scalar(out=small[:, 0, :], in0=psum[p1][:, :], scalar1=1.0 / D, scalar2=None, op0=ALU.mult),
              reads=[ps_b[p1]], writes=[rs_b])
        tk.op("dve", lambda e: e.tensor_tensor(out=small[:, 1, :], in0=small[:, 0, :], in1=small[:, 0, :], op=ALU.mult), reads=[rs_b], writes=[rs_b])
        tk.op("dve", lambda e: e.scalar_tensor_tensor(out=small[:, 2, :], in0=psum[p2][:, :], scalar=1.0 / D, in1=small[:, 1, :],
                                                       op0=ALU.mult, op1=ALU.subtract), reads=[ps_b[p2], rs_b], writes=[rs_b])
        tk.op("act", lambda e: e.activation(out=small[:, 2, :], in_=small[:, 2, :], func=AF.Sqrt, scale=1.0, bias=consts[:, 0:1]),
              reads=[rs_b, consts_b], writes=[rs_b])
        pr = psring.next()
        tk.op("dve", lambda e: e.reciprocal(out=psum[pr][:, :], in_=small[:, 2, :]), reads=[rs_b], writes=[ps_b[pr]])
        for kc in range(KC):
            ti = tmpring.next()
            tk.op("dve", lambda e: e.tensor_tensor(out=tmpg[ti][:, 0, :], in0=c32[:, kc, :], in1=small[:, 0, :], op=ALU.subtract),
                  reads=[c32_b, rs_b], writes=[tmpg_b[ti]])
            tk.op("dve", lambda e: e.tensor_tensor(out=tmpg[ti][:, 1, :], in0=tmpg[ti][:, 0, :], in1=psum[pr][:, :], op=ALU.mult),
                  reads=[tmpg_b[ti], ps_b[pr]], writes=[tmpg_b[ti]])
            tk.op("act", lambda e: e.activation(out=B1[:, kc, :], in_=tmpg[ti][:, 1, :], func=AF.Silu,
                                                scale=vecs[:, gln_o + kc:gln_o + kc + 1], bias=vecs[:, bln_o + kc:bln_o + kc + 1]),
                  reads=[tmpg_b[ti], vecs_b], **({"writes": [B1_b]} if kc == 0 else {"parts": [B1_b]}))
        for j in range(2):
            wt, wb = wsC.next(BLK["CO"] + j)
            w3 = wt[:, 0:KC * 512].rearrange("p (k o) -> p k o", o=512)
            pbs = [psring.next() for _ in range(4)]
            for mi in range(4):
                pb = pbs[mi]
                for kc in range(KC):
                    tk.op("pe", lambda e: e.matmul(psum[pb][:, :], lhsT=w3[:, kc, mi * 128:(mi + 1) * 128], rhs=B1[:, kc, :],
                                                   start=(kc == 0), stop=(kc == KC - 1)),
                          reads=[wb, B1_b], **({"writes": [ps_b[pb]]} if kc == 0 else {"parts": [ps_b[pb]]}))
            for mi in range(4):
                m = 4 * j + mi
                tk.op("dve", lambda e: e.tensor_tensor(out=gcy[:, m, :], in0=psum[pbs[mi]][:, :], in1=gcl[:, m, :], op=ALU.mult),
                      reads=[ps_b[pbs[mi]], gcl_b], **({"writes": [gcy_b]} if m == 0 else {"parts": [gcy_b]}))
        for j in range(4):
            wt, wb = wsC.next(BLK["LO"] + j)
            w3 = wt[:, 0:LC * 256].rearrange("p (k o) -> p k o", o=256)
            pbs = [psring.next() for _ in range(2)]
            for mi in range(2):
                pb = pbs[mi]
                for kc in range(LC):
                    tk.op("pe", lambda e: e.matmul(psum[pb][:, :], lhsT=w3[:, kc, mi * 128:(mi + 1) * 128], rhs=mrg[:, kc, :],
                                                   start=(kc == 0), stop=(kc == LC - 1)),
                          reads=[wb, mrg_b], **({"writes": [ps_b[pb]]} if kc == 0 else {"parts": [ps_b[pb]]}))
            ti = tmpring.next()
            for mi in range(2):
                m = 2 * j + mi
                tk.op("dve", lambda e: e.tensor_tensor(out=tmpg[ti][:, mi, :], in0=psum[pbs[mi]][:, :], in1=gcl[:, 8 + m, :], op=ALU.mult),
                      reads=[ps_b[pbs[mi]], gcl_b], **({"writes": [tmpg_b[ti]]} if mi == 0 else {"parts": [tmpg_b[ti]]}))
            tk.op("pool", lambda e: e.tensor_tensor(out=B2[:, 2 * j:2 * j + 2, :], in0=tmpg[ti], in1=gcy[:, 2 * j:2 * j + 2, :], op=ALU.add),
                  reads=[tmpg_b[ti], gcy_b], **({"writes": [B2_b]} if j == 0 else {"parts": [B2_b]}))
        for j in range(2):
            wt, wb = wsC.next(BLK["WO"] + j)
            w3 = wt[:, 0:KC * 512].rearrange("p (k o) -> p k o", o=512)
            pbs = [psring.next() for _ in range(4)]
            for mi in range(4):
                pb = pbs[mi]
                for kc in range(KC):
                    tk.op("pe", lambda e: e.matmul(psum[pb][:, :], lhsT=w3[:, kc, mi * 128:(mi + 1) * 128], rhs=B2[:, kc, :],
                                                   start=(kc == 0), stop=(kc == KC - 1)),
                          reads=[wb, B2_b], **({"writes": [ps_b[pb]]} if kc == 0 else {"parts": [ps_b[pb]]}))
            for mi in range(4):
                m = 4 * j + mi
                tk.op("dve", lambda e: e.scalar_tensor_tensor(out=xfC[:, m, :], in0=psum[pbs[mi]][:, :], scalar=modp[:, 5, m:m + 1, 0],
                                                               in1=xfC[:, m, :], op0=ALU.mult, op1=ALU.add),
                      reads=[ps_b[pbs[mi]], modp_b, xfC_b], writes=[xfC_b])
        rms_to_h(xfC, xfC_b, B1, B1_b, B2, B2_b, N, modp[:, 6, :, 0], modp[:, 7, :, 0])
        ffn(wsC, 1, xfC, xfC_b, B1, B1_b, hidC, hidC_b, N, modp[:, 8, :, 0])
        tk.op("act", lambda e: e.activation(out=B2, in_=xfC, func=AF.Square), reads=[xfC_b], writes=[B2_b])
        pb = psring.next()
        for kc in range(KC):
            tk.op("pe", lambda e: e.matmul(psum[pb][:, :], lhsT=ones_bf, rhs=B2[:, kc, :], start=(kc == 0), stop=(kc == KC - 1)),
                  reads=[ones_b, B2_b], **({"writes": [ps_b[pb]]} if kc == 0 else {"parts": [ps_b[pb]]}))
        tk.op("act", lambda e: e.activation(out=small[:, 0, :], in_=psum[pb][:, :], func=AF.Sqrt, scale=1.0 / D, bias=consts[:, 0:1]),
              reads=[ps_b[pb], consts_b], writes=[rs_b])
        pr = psring.next()
        tk.op("dve", lambda e: e.reciprocal(out=psum[pr][:, :], in_=small[:, 0, :]), reads=[rs_b], writes=[ps_b[pr]])
        for kc in range(KC):
            tk.op("dve", lambda e: e.scalar_tensor_tensor(out=c32[:, kc, :], in0=xfC[:, kc, :], scalar=vecs[:, gf_o + kc:gf_o + kc + 1],
                                                           in1=psum[pr][:, :], op0=ALU.mult, op1=ALU.mult),
                  reads=[xfC_b, vecs_b, ps_b[pr]], **({"writes": [c32_b]} if kc == 0 else {"parts": [c32_b]}))
        for s_ in range(N // 128):
            oi = s_ % 2
            for half in range(2):
                pb = psring.next()
                for q in range(4):
                    kc = half * 4 + q
                    tk.op("pe", lambda e: e.transpose(out=psum[pb][:, q * 128:(q + 1) * 128], in_=c32[:, kc, s_ * 128:(s_ + 1) * 128], identity=ident),
                          reads=[c32_b, ident_b], **({"writes": [ps_b[pb]]} if q == 0 else {"parts": [ps_b[pb]]}))
                if half == 0:
                    tk.op("act", lambda e: e.activation(out=ost[oi][:, 0:512], in_=psum[pb][:, :], func=AF.Copy), reads=[ps_b[pb]], writes=[ost_b[oi]])
                else:
                    tk.op("dve", lambda e: e.tensor_copy(out=ost[oi][:, 512:1024], in_=psum[pb][:, :]), reads=[ps_b[pb]], parts=[ost_b[oi]])
            tk.dma("sp", out_d[t0 + s_ * 128:t0 + (s_ + 1) * 128, :], ost[oi], ost_b[oi], reads=[ost_b[oi]], parts=[out_b])
    end_phase()
    if debug:
        pass
    return nc


def _fm(v, k):
    return np.ascontiguousarray(np.asarray(v, np.float32).reshape(k, 128).T)


def _blk3(W, colgroups, kcn):
    sub = np.concatenate([W[:, a:b] for a, b in colgroups], axis=1)
    n = sub.shape[1]
    arr = sub.reshape(kcn, 128, n).transpose(1, 0, 2).reshape(128, kcn * n)
    out = np.zeros((128, WBE), np.float32)
    out[:, :kcn * n] = arr
    return out


def weight_blocks(inp):
    blocks = [None] * NBLK
    for f in range(2):
        wu = inp["w_ffn%d_up" % (f + 1)][0]; wd = inp["w_ffn%d_down" % (f + 1)][0]
        for j in range(11):
            blocks[BLK["UP%d" % (f + 1)] + j] = _blk3(wu, [(j * 256, (j + 1) * 256), (FF + j * 256, FF + (j + 1) * 256)], KC)
        for m in range(8):
            blocks[BLK["DN%d" % (f + 1)] + m] = _blk3(wd, [(m * 128, (m + 1) * 128)], 22)
    wi = inp["w_in"][0]
    for j in range(4):
        blocks[BLK["IN"] + j] = _blk3(wi, [(j * 256, (j + 1) * 256), (D + j * 256, D + (j + 1) * 256)], KC)
    for j in range(3):
        cw = 512 if j < 2 else 256
        blocks[BLK["IN"] + 4 + j] = _blk3(wi, [(2048 + j * 512, 2048 + j * 512 + cw)], KC)
        blocks[BLK["IN"] + 7 + j] = _blk3(wi, [(3328 + j * 512, 3328 + j * 512 + cw)], KC)
    for j in range(4):
        blocks[BLK["IN"] + 10 + j] = _blk3(wi, [(4608 + j * 512, 4608 + (j + 1) * 512)], KC)
    for j in range(2):
        blocks[BLK["CO"] + j] = _blk3(inp["w_conf_out"][0], [(j * 512, (j + 1) * 512)], KC)
        blocks[BLK["WO"] + j] = _blk3(inp["w_out"][0], [(j * 512, (j + 1) * 512)], KC)
    for j in range(4):
        blocks[BLK["LO"] + j] = _blk3(inp["w_lru_out"][0], [(j * 256, (j + 1) * 256)], LC)
    return blocks


def make_in_maps(inp, R=NCORES):
    x = inp["x"]; ctx = inp["ctx"]
    maps = []
    blocks = weight_blocks(inp)
    NS = (NBLK + R - 1) // R
    for c in range(NCORES):
        b, half = c // 2, c % 2
        if half == 0:
            xl = x[b, 0:LT + HALO]
            cl = ctx[b]
        else:
            xl = x[b, SEQ - LT - HALO:SEQ][::-1]
            cl = ctx[b][::-1]
        d = [half, 1 - half]
        vec = np.zeros((128, NV), np.float32)

        def put(name, arr):
            o, k = VOFF[name]
            assert arr.shape == (128, k), (name, arr.shape)
            vec[:, o:o + k] = arr

        put("g_n1", _fm(inp["g_n1"][0], 8)); put("g_n2", _fm(inp["g_n2"][0], 8)); put("g_n3", _fm(inp["g_n3"][0], 8))
        put("g_final", _fm(inp["g_final"], 8)); put("b_mod", _fm(inp["b_mod"][0], 72)); put("b_in", _fm(inp["b_in"][0], 52))
        wdw = inp["w_dw"][0]
        if half == 1:
            wdw = wdw[::-1]
        put("w_dw", np.concatenate([_fm(wdw[t], 8) for t in range(31)], axis=1))
        put("b_dw", _fm(inp["b_dw"][0], 8)); put("g_ln", _fm(inp["g_ln"][0], 8)); put("b_ln", _fm(inp["b_ln"][0], 8))
        wl = inp["w_lru_conv"][0]
        z = np.zeros((1, LRU), np.float32)
        w5 = np.concatenate([wl, z], 0) if half == 0 else np.concatenate([z, wl[::-1]], 0)
        put("w_lc", np.concatenate([_fm(w5[t], 10) for t in range(5)], axis=1))
        put("b_lc", _fm(inp["b_lru_conv"][0], 10))
        put("b_a", np.concatenate([_fm(inp["b_rec_gate"][0, dd], 10) for dd in d], axis=1))
        put("b_x", np.concatenate([_fm(inp["b_in_gate"][0, dd], 10) for dd in d], axis=1))
        put("lam", np.concatenate([_fm(inp["lru_lambda"][0, dd], 10) for dd in d], axis=1))
        c2 = np.zeros((128, 16), np.float32)
        cb = _fm(inp["c"][b], 8); cc_ = _fm(inp["c_ctx"], 8)
        c2[:, 0::2] = cb; c2[:, 1::2] = cc_
        put("c2", c2)
        fl = np.zeros((128, 2), np.float32); fl[:, 0] = half; fl[:, 1] = 1.0 - 2.0 * half
        put("flags", fl)
        pm = np.zeros((128, 8), np.float32); pm[:, c ^ 1] = 1.0
        put("pmask", pm)
        wsh = np.zeros((NS, 128, WBE), np.float32)
        for s_ in range(NS):
            b_ = s_ * R + (c % R)
            if b_ < NBLK:
                wsh[s_] = blocks[b_]
        wa = inp["w_rec_gate"][0]; wx = inp["w_in_gate"][0]
        wg = np.stack([wa[d[0]], wx[d[0]], wa[d[1]], wx[d[1]]], axis=2)
        maps.append({
            "x_loc": np.ascontiguousarray(xl, np.float32), "ctx_loc": np.ascontiguousarray(cl, np.float32), "vecs": vec,
            "w_mod": inp["w_mod"][0], "wshare": wsh,
            "w_gates": np.ascontiguousarray(wg, np.float32),
        })
    return maps


def kernel(**inputs):
    inp = {k: np.asarray(v) for k, v in inputs.items()}
    nc = build()
    maps = make_in_maps(inp)
    res = run_bass_kernel_spmd(nc, maps, core_ids=list(range(NCORES)))
    out = np.empty((4, SEQ, D), np.float32)
    for c in range(NCORES):
        b, half = c // 2, c % 2
        y = np.asarray(res.results[c]["y_loc"])
        if half == 0:
            out[b, 0:LT] = y
        else:
            out[b, LT:SEQ] = y[::-1]
    return out
```
